# Optimizing a Trainium2 kernel written in Bass

```python
import math
import jax, jax.numpy as jnp
from jax import lax
import numpy as np

D_MODEL = 1024
BATCH = 16
SEQ = 4096
DEPTH = 4

N_MIXERS = 4
D_FF = 4 * D_MODEL
EPS = 1e-6

SSM_GROUP = 16
SSM_GROUPS = D_MODEL // SSM_GROUP
SSM_STATE = 64
DT_MIN = 1e-3
DT_MAX = 1e-1

CONV_WIDTH = 31

GMLP_CHUNK = 128
GMLP_HEADS = 4
GMLP_WIDTH = D_MODEL

ATT_CONFIGS = ((128, 1), (512, 4), (2048, 16))
ATT_GROUPS = len(ATT_CONFIGS)
ATT_HEADS = 8
HEAD_DIM = 64
ATT_GROUP_WIDTH = ATT_HEADS * HEAD_DIM

kernel_name = "interleaved_s5_conv_gmlp_dilated_attn_trunk"


def _rmsnorm(x, g):
    xf = x.astype(jnp.float32)
    y = xf * lax.rsqrt(jnp.mean(xf * xf, axis=-1, keepdims=True) + EPS)
    return (y * g.astype(jnp.float32)).astype(x.dtype)


def _layernorm(x, g, b):
    xf = x.astype(jnp.float32)
    mu = jnp.mean(xf, axis=-1, keepdims=True)
    var = jnp.mean(jnp.square(xf - mu), axis=-1, keepdims=True)
    y = (xf - mu) * lax.rsqrt(var + EPS)
    return (y * g.astype(jnp.float32) + b.astype(jnp.float32)).astype(x.dtype)


def _s5_mixer(h, a_re, a_im, b_re, b_im, c_re, c_im, d_skip, log_dt, w_glu):
    bsz, s, _ = h.shape
    f32 = jnp.float32
    u = h.astype(f32).reshape(bsz, s, SSM_GROUPS, SSM_GROUP)
    a = lax.complex(a_re.astype(f32), a_im.astype(f32))
    dt = jnp.exp(log_dt.astype(f32))[:, None]
    a_bar = jnp.exp(a * dt)
    b_mat = lax.complex(b_re.astype(f32), b_im.astype(f32))
    b_bar = ((a_bar - 1.0) / a)[..., None] * b_mat
    bu = jnp.einsum('bsgp,gnp->sbgn', u.astype(jnp.complex64), b_bar)
    a_elems = jnp.broadcast_to(a_bar[None, None], (s, 1, SSM_GROUPS, SSM_STATE))

    def combine(left, right):
        a_l, b_l = left
        a_r, b_r = right
        return a_r * a_l, a_r * b_l + b_r

    _, states = lax.associative_scan(combine, (a_elems, bu), axis=0)
    c_mat = lax.complex(c_re.astype(f32), c_im.astype(f32))
    y = jnp.real(jnp.einsum('sbgn,gpn->bsgp', states, c_mat))
    y = y + d_skip.astype(f32).reshape(SSM_GROUPS, SSM_GROUP) * u
    y = jax.nn.gelu(y.reshape(bsz, s, D_MODEL)).astype(h.dtype)
    z = y @ w_glu
    return z[..., :D_MODEL] * jax.nn.sigmoid(z[..., D_MODEL:])


def _conv_mixer(h, w_pw1, b_pw1, w_dw, b_dw, ln_g, ln_b, w_pw2, b_pw2):
    z = h @ w_pw1 + b_pw1
    z = z[..., :D_MODEL] * jax.nn.sigmoid(z[..., D_MODEL:])
    y = lax.conv_general_dilated(
        z, w_dw[:, None, :].astype(z.dtype), window_strides=(1,),
        padding=((CONV_WIDTH - 1, 0),),
        dimension_numbers=('NWC', 'WIO', 'NWC'),
        feature_group_count=D_MODEL) + b_dw
    y = jax.nn.silu(_layernorm(y, ln_g, ln_b))
    return y @ w_pw2 + b_pw2


def _gmlp_mixer(h, w_in, ln_g, ln_b, w_s, b_s, w_out):
    bsz, s, _ = h.shape
    z = jax.nn.gelu(h @ w_in)
    u, v = z[..., :GMLP_WIDTH], z[..., GMLP_WIDTH:]
    v = _layernorm(v, ln_g, ln_b)
    n_chunks = s // GMLP_CHUNK
    v = v.reshape(bsz, n_chunks, GMLP_CHUNK, GMLP_HEADS, GMLP_WIDTH // GMLP_HEADS)
    causal = jnp.tril(jnp.ones((GMLP_CHUNK, GMLP_CHUNK), dtype=bool))
    ws = jnp.where(causal[None], w_s, 0.0)
    v = jnp.einsum('hts,bcshe->bcthe', ws, v) + b_s.T[None, None, :, :, None]
    v = v.reshape(bsz, s, GMLP_WIDTH)
    return (u * v) @ w_out


def _dilated_window_attention(q, k, v, window, dil):
    bsz, s, nh, hd = q.shape
    steps = window // dil
    blk = steps
    span = dil * blk
    s_pad = -(-s // span) * span
    pad = ((0, 0), (0, s_pad - s), (0, 0), (0, 0))
    nb = s_pad // span

    def split(t):
        t = jnp.pad(t, pad)
        return t.reshape(bsz, nb, blk, dil, nh, hd).transpose(0, 3, 1, 2, 4, 5)

    def with_prev(t):
        prev = jnp.pad(t, ((0, 0), (0, 0), (1, 0), (0, 0), (0, 0), (0, 0)))[:, :, :-1]
        return jnp.concatenate([prev, t], axis=3)

    qb = split(q)
    kk = with_prev(split(k))
    vv = with_prev(split(v))
    scores = jnp.einsum('brnihd,brnjhd->brnhij', qb, kk,
                        preferred_element_type=jnp.float32) * (HEAD_DIM ** -0.5)
    i_idx = jnp.arange(blk)[:, None]
    j_idx = jnp.arange(2 * blk)[None, :]
    dist = i_idx + blk - j_idx
    band = (dist >= 0) & (dist <= steps)
    has_prev = (jnp.arange(nb) > 0)[:, None, None]
    valid = band[None] & (has_prev | (j_idx >= blk)[None])
    scores = jnp.where(valid[None, None, :, None], scores, -jnp.inf)
    lse = jax.nn.logsumexp(scores, axis=-1)
    probs = jnp.exp(scores - lse[..., None])
    out = jnp.einsum('brnhij,brnjhd->brnihd', probs, vv.astype(jnp.float32))
    out = out.transpose(0, 2, 3, 1, 4, 5).reshape(bsz, s_pad, nh, hd)[:, :s]
    lse = lse.transpose(0, 2, 4, 1, 3).reshape(bsz, s_pad, nh)[:, :s]
    return out, lse


def _attention_mixer(h, w_qkv, w_o):
    bsz, s, _ = h.shape
    qkv = (h @ w_qkv).reshape(bsz, s, 3, ATT_GROUPS, ATT_HEADS, HEAD_DIM)
    outs, lses = [], []
    for g, (window, dil) in enumerate(ATT_CONFIGS):
        o, l = _dilated_window_attention(qkv[:, :, 0, g], qkv[:, :, 1, g], qkv[:, :, 2, g], window, dil)
        outs.append(o)
        lses.append(l)
    outs = jnp.stack(outs, axis=0)
    weights = jax.nn.softmax(jnp.stack(lses, axis=0), axis=0)
    merged = jnp.sum(weights[..., None] * outs, axis=0)
    return merged.reshape(bsz, s, ATT_GROUP_WIDTH).astype(h.dtype) @ w_o


def _mlp(h, w_in, w_out):
    return jnp.square(jax.nn.relu(h @ w_in)) @ w_out


def _n_layers_of(m):
    return len(range(m, DEPTH, N_MIXERS))


def setup_inputs(seed: int = 0) -> dict:
    key = jax.random.key(seed)
    ks = jax.random.split(key, 32)
    nrm = jax.random.normal
    f32 = jnp.float32
    D, G, N, P = D_MODEL, SSM_GROUPS, SSM_STATE, SSM_GROUP
    nA, nB, nC, nD = (_n_layers_of(m) for m in range(N_MIXERS))
    E, T = GMLP_WIDTH, GMLP_CHUNK
    qkv_width = 3 * ATT_GROUPS * ATT_GROUP_WIDTH
    n_idx = jnp.arange(N, dtype=f32)
    return {
        "x": nrm(ks[0], (BATCH, SEQ, D), f32),
        "norm_mix": 1.0 + 0.02 * nrm(ks[1], (DEPTH, D), f32),
        "norm_mlp": 1.0 + 0.02 * nrm(ks[2], (DEPTH, D), f32),
        "norm_final": 1.0 + 0.02 * nrm(ks[3], (D,), f32),
        "ssm_a_re": -0.5 + 0.01 * nrm(ks[4], (nA, G, N), f32),
        "ssm_a_im": math.pi * n_idx + 0.01 * nrm(ks[5], (nA, G, N), f32),
        "ssm_b_re": nrm(ks[6], (nA, G, N, P), f32) * (2 * P) ** -0.5,
        "ssm_b_im": nrm(ks[7], (nA, G, N, P), f32) * (2 * P) ** -0.5,
        "ssm_c_re": nrm(ks[8], (nA, G, P, N), f32) * (2 * N) ** -0.5,
        "ssm_c_im": nrm(ks[9], (nA, G, P, N), f32) * (2 * N) ** -0.5,
        "ssm_d": nrm(ks[10], (nA, D), f32),
        "ssm_log_dt": jax.random.uniform(ks[11], (nA, G), f32, math.log(DT_MIN), math.log(DT_MAX)),
        "ssm_w_glu": nrm(ks[12], (nA, D, 2 * D), f32) * D ** -0.5,
        "conv_w_pw1": nrm(ks[13], (nB, D, 2 * D), f32) * D ** -0.5,
        "conv_b_pw1": 0.01 * nrm(ks[14], (nB, 2 * D), f32),
        "conv_w_dw": nrm(ks[15], (nB, CONV_WIDTH, D), f32) * CONV_WIDTH ** -0.5,
        "conv_b_dw": 0.01 * nrm(ks[16], (nB, D), f32),
        "conv_ln_g": 1.0 + 0.02 * nrm(ks[17], (nB, D), f32),
        "conv_ln_b": 0.01 * nrm(ks[18], (nB, D), f32),
        "conv_w_pw2": nrm(ks[19], (nB, D, D), f32) * D ** -0.5,
        "conv_b_pw2": 0.01 * nrm(ks[20], (nB, D), f32),
        "gmlp_w_in": nrm(ks[21], (nC, D, 2 * E), f32) * D ** -0.5,
        "gmlp_ln_g": 1.0 + 0.02 * nrm(ks[22], (nC, E), f32),
        "gmlp_ln_b": 0.01 * nrm(ks[23], (nC, E), f32),
        "gmlp_w_s": nrm(ks[24], (nC, GMLP_HEADS, T, T), f32) * T ** -0.5,
        "gmlp_b_s": 1.0 + 0.02 * nrm(ks[25], (nC, GMLP_HEADS, T), f32),
        "gmlp_w_out": nrm(ks[26], (nC, E, D), f32) * E ** -0.5,
        "attn_w_qkv": nrm(ks[27], (nD, D, qkv_width), f32) * D ** -0.5,
        "attn_w_o": nrm(ks[28], (nD, ATT_GROUP_WIDTH, D), f32) * ATT_GROUP_WIDTH ** -0.5,
        "mlp_w_in": nrm(ks[29], (DEPTH, D, D_FF), f32) * D ** -0.5,
        "mlp_w_out": nrm(ks[30], (DEPTH, D_FF, D), f32) * D_FF ** -0.5,
    }


def reference(x, norm_mix, norm_mlp, norm_final,
              ssm_a_re, ssm_a_im, ssm_b_re, ssm_b_im, ssm_c_re, ssm_c_im, ssm_d, ssm_log_dt, ssm_w_glu,
              conv_w_pw1, conv_b_pw1, conv_w_dw, conv_b_dw, conv_ln_g, conv_ln_b, conv_w_pw2, conv_b_pw2,
              gmlp_w_in, gmlp_ln_g, gmlp_ln_b, gmlp_w_s, gmlp_b_s, gmlp_w_out,
              attn_w_qkv, attn_w_o, mlp_w_in, mlp_w_out):
    for i in range(DEPTH):
        m, j = i % N_MIXERS, i // N_MIXERS
        h = _rmsnorm(x, norm_mix[i])
        if m == 0:
            y = _s5_mixer(h, ssm_a_re[j], ssm_a_im[j], ssm_b_re[j], ssm_b_im[j], ssm_c_re[j],
                          ssm_c_im[j], ssm_d[j], ssm_log_dt[j], ssm_w_glu[j])
        elif m == 1:
            y = _conv_mixer(h, conv_w_pw1[j], conv_b_pw1[j], conv_w_dw[j], conv_b_dw[j],
                            conv_ln_g[j], conv_ln_b[j], conv_w_pw2[j], conv_b_pw2[j])
        elif m == 2:
            y = _gmlp_mixer(h, gmlp_w_in[j], gmlp_ln_g[j], gmlp_ln_b[j], gmlp_w_s[j],
                            gmlp_b_s[j], gmlp_w_out[j])
        else:
            y = _attention_mixer(h, attn_w_qkv[j], attn_w_o[j])
        x = x + y.astype(x.dtype)
        x = x + _mlp(_rmsnorm(x, norm_mlp[i]), mlp_w_in[i], mlp_w_out[i]).astype(x.dtype)
    return _rmsnorm(x, norm_final)
```

```python
import contextlib
import math
import numpy as np
import concourse.bass as bass
import concourse.mybir as mybir
from concourse.bass_utils import run_bass_kernel_spmd

F32 = mybir.dt.float32
BF16 = mybir.dt.bfloat16
ALU = mybir.AluOpType
AF = mybir.ActivationFunctionType

D = 1024
KT = D // 128
DFF = 4096
FT = DFF // 128
EPS = 1e-6
NCORES = 8

ENGS = ["pe", "act", "dve", "pool", "sp"]
SAME_ENG_SYNC = True


class Buf:
    def __init__(self, name):
        self.name = name
        self.w = None
        self.r = {}
        self.dsem = None
        self.dcnt = 0
        self.ssem = None
        self.scnt = 0


class K:
    def __init__(self, nc, stack):
        self.nc = nc
        self.stack = stack
        self.ops = {e: [] for e in ENGS}
        self.sem = {}
        self.cnt = {}
        self.waited = {e: {} for e in ENGS}
        self.nsem = 0
        for e in ENGS:
            self.new_sem(e)
        self.same_eng_sync = SAME_ENG_SYNC
        self.pstack = None
        self.dma_bufs = []
        self.free_dma_sems = []
        self.deferred = None

    def _alloc_sem(self, name):
        self.nsem += 1
        return self.stack.enter_context(self.nc.semaphore(f"{name}_{self.nsem}"))

    def new_sem(self, e):
        self.sem[e] = self._alloc_sem("s_" + e)
        self.cnt[e] = 0

    def new_phase(self):
        for e in ENGS:
            if self.cnt[e] > 20000:
                self.new_sem(e)

    def sbuf(self, name, shape, dtype):
        st = self.pstack if self.pstack is not None else self.stack
        self.ntens = getattr(self, "ntens", 0) + 1
        return st.enter_context(self.nc.sbuf_tensor(f"{name}_{self.ntens}", shape, dtype))

    def barrier(self):
        toks = [(self.sem[e], self.cnt[e]) for e in ENGS if self.cnt[e] > 0]
        for b in self.dma_bufs:
            if b.dsem is not None:
                toks.append((b.dsem, b.dcnt))
            if b.ssem is not None:
                toks.append((b.ssem, b.scnt))
        for e in ENGS:
            self.wait_all(e, toks)

    def _get_dma_sem(self):
        if self.free_dma_sems:
            return self.free_dma_sems.pop()
        return self._alloc_sem("d"), 0

    def _recycle_dma_sems(self):
        for b in self.dma_bufs:
            if b.dsem is not None:
                if not getattr(b, "sw", False):
                    self.free_dma_sems.append((b.dsem, b.dcnt))
                b.dsem = None
                b.sw = False
            if b.ssem is not None:
                self.free_dma_sems.append((b.ssem, b.scnt))
                b.ssem = None
        self.dma_bufs = []

    def begin_phase(self):
        self.barrier()
        self.new_phase()
        self.pstack = contextlib.ExitStack()

    def end_phase(self):
        self.barrier()
        self.run()
        self._recycle_dma_sems()
        self.pstack.close()
        self.pstack = None

    def psum(self, name, shape, dtype):
        return self.stack.enter_context(self.nc.psum_tensor(name, shape, dtype))

    def _deps(self, reads, writes, extra, nowaw=False):
        deps = {}

        def add(tok):
            if tok is None:
                return
            k = id(tok[0])
            if k not in deps or deps[k][1] < tok[1]:
                deps[k] = tok

        for b in reads:
            add(b.w)
        for b in writes:
            if not (nowaw and b.w is not None and b.w[0] is b.dsem):
                add(b.w)
            for t in b.r.values():
                add(t)
        for t in extra:
            add(t)
        return deps

    def _waits(self, eng, deps):
        waits = []
        for k, (sem, val) in deps.items():
            if eng == "pe" and sem is self.sem["pe"]:
                continue
            if (not self.same_eng_sync) and sem is self.sem.get(eng):
                continue
            if self.waited[eng].get(k, 0) < val:
                self.waited[eng][k] = val
                waits.append((sem, val))
        return waits

    def _mark(self, tok, reads, writes):
        for b in reads:
            k = id(tok[0])
            b.r[k] = tok
        for b in writes:
            b.w = tok
            b.r = {}

    def emit(self, eng, fn, reads=(), writes=(), extra=(), nosame=False):
        if self.deferred is not None:
            self.deferred.append(lambda: self._emit(eng, fn, reads, writes, extra, nosame))
            return None
        return self._emit(eng, fn, reads, writes, extra, nosame)

    def _emit(self, eng, fn, reads=(), writes=(), extra=(), nosame=False):
        deps = self._deps(reads, writes, extra)
        if nosame:
            deps = {kk: v for kk, v in deps.items() if v[0] is not self.sem[eng]}
        waits = self._waits(eng, deps)
        self.cnt[eng] += 1
        tok = (self.sem[eng], self.cnt[eng])
        self.ops[eng].append(("op", waits, fn, self.sem[eng]))
        self._mark(tok, reads, writes)
        return tok

    def dma(self, q, out, in_, reads=(), writes=(), kind="load", extra=(), nowaw=False):
        if self.deferred is not None:
            self.deferred.append(lambda: self._dma(q, out, in_, reads, writes, kind, extra, nowaw))
            return None
        return self._dma(q, out, in_, reads, writes, kind, extra, nowaw)

    def _dma(self, q, out, in_, reads=(), writes=(), kind="load", extra=(), nowaw=False):
        deps = self._deps(reads, writes, extra, nowaw)
        waits = self._waits(q, deps)
        if kind == "load":
            b = writes[0]
            if b.dsem is None:
                if q == "pool":
                    b.dsem, b.dcnt = self._alloc_sem("dsw"), 0
                    b.sw = True
                else:
                    b.dsem, b.dcnt = self._get_dma_sem()
                self.dma_bufs.append(b)
            b.dcnt += 16
            tok = (b.dsem, b.dcnt)
        else:
            b = reads[0]
            if b.ssem is None:
                b.ssem, b.scnt = self._get_dma_sem()
                self.dma_bufs.append(b)
            b.scnt += 16
            tok = (b.ssem, b.scnt)
        self.ops[q].append(("dma", waits, (out, in_), tok[0]))
        self._mark(tok, reads, writes)
        return tok

    def wait_all(self, eng, toks):
        deps = {}
        for t in toks:
            if t is None:
                continue
            k = id(t[0])
            if k not in deps or deps[k][1] < t[1]:
                deps[k] = t
        waits = self._waits(eng, deps)
        self.ops[eng].append(("wait", waits, None, None))

    def run(self):
        nc = self.nc
        with nc.Block() as block:
            def replay(e, name):
                for kind, waits, fn, sem in self.ops[name]:
                    for (s, v) in waits:
                        e.wait_ge(s, v)
                    if kind == "op":
                        fn(e).then_inc(sem, 1)
                    elif kind == "dma":
                        e.dma_start(out=fn[0], in_=fn[1]).then_inc(sem, 16)

            @block.tensor
            def _(e):
                replay(e, "pe")

            @block.scalar
            def _(e):
                replay(e, "act")

            @block.vector
            def _(e):
                replay(e, "dve")

            @block.gpsimd
            def _(e):
                replay(e, "pool")

            @block.sync
            def _(e):
                replay(e, "sp")
        self.ops = {e: [] for e in ENGS}


class PS:
    def __init__(self, k):
        self.t = [k.psum(f"psb{i}", [128, 512], F32) for i in range(8)]
        self.b = [Buf(f"psb{i}") for i in range(8)]
        self.i = 0

    def next(self):
        i = self.i
        self.i = (self.i + 1) % 8
        return self.t[i], self.b[i]


def load_w_bf16(k, w_dram, dst, dst_buf, rows, cols, row0=0, col0=0, chunk_rt=4, q="pool"):
    rt = rows // 128
    for r in range(0, rt, chunk_rt):
        n = min(chunk_rt, rt - r)
        src = w_dram[row0 + r * 128: row0 + (r + n) * 128, col0:col0 + cols].rearrange("(t p) c -> p t c", p=128)
        k.dma(q, dst[:, r:r + n, :], src, writes=[dst_buf], nowaw=True)


def load_w_cols(k, w_dram, dst, name, rows, cols, cw=512, q="pool"):
    bufs = []
    for c0 in range(0, cols, cw):
        b = Buf(f"{name}_c{c0}")
        src = w_dram[0:rows, c0:c0 + cw].rearrange("(t p) c -> p t c", p=128)
        k.dma(q, dst[:, :, c0:c0 + cw], src, writes=[b])
        bufs.append(b)
    return bufs


class Common:
    def __init__(self, k, NT):
        self.k = k
        self.NT = NT
        self.ps = PS(k)
        self.ones_bf = k.sbuf("ones_bf", [128, 128], BF16)
        self.ones_b = Buf("ones_bf")
        k.emit("pool", lambda e: e.memset(self.ones_bf[:], 1.0), writes=[self.ones_b])
        self.rstd = k.sbuf("rstd", [128, NT], F32)
        self.rstd_b = Buf("rstd")


def rmsnorm_T(k, C, x, x_b, g, g_b, kt_g, h, h_b, n, sq, sq_bs, out_fn=None, in_fn=None, sq_parts=None):
    pt, pb = C.ps.next()
    if sq_parts is None:
        k.emit("act", lambda e: e.activation(out=sq[:, :, :n], in_=x[:, :, :n], func=AF.Square),
               reads=[x_b], writes=sq_bs)

        def mm(e):
            ins = None
            for kt in range(KT):
                ins = e.matmul(pt[:, :n], C.ones_bf[:], sq[:, kt, :n], start=(kt == 0), stop=(kt == KT - 1))
            return ins
        k.emit("pe", mm, reads=list(sq_bs) + [C.ones_b], writes=[pb])
    else:
        for kt in range(KT):
            sp_, sp_b = sq_parts[kt % len(sq_parts)]
            k.emit("act", lambda e, kt=kt, sp_=sp_: e.activation(out=sp_[:, :n], in_=x[:, kt, :n], func=AF.Square),
                   reads=[x_b], writes=[sp_b])
            k.emit("pe", lambda e, kt=kt, sp_=sp_: e.matmul(pt[:, :n], C.ones_bf[:], sp_[:, :n], start=(kt == 0),
                                                          stop=(kt == KT - 1)),
                   reads=[sp_b, C.ones_b], writes=[pb])
    k.emit("act", lambda e: e.activation(out=C.rstd[:, :n], in_=pt[:, :n], func=AF.Sqrt, scale=1.0 / D, bias=EPS),
           reads=[pb], writes=[C.rstd_b])
    k.emit("dve", lambda e: e.reciprocal(out=C.rstd[:, :n], in_=C.rstd[:, :n]), reads=[C.rstd_b], writes=[C.rstd_b])
    for kt in range(KT):
        o_ap = h[:, kt, :n] if out_fn is None else out_fn(kt)
        i0 = x[:, kt, :n] if in_fn is None else in_fn(x[:, kt, :n])
        i1 = C.rstd[:, :n] if in_fn is None else in_fn(C.rstd[:, :n])
        k.emit("dve", lambda e, kt=kt, o_ap=o_ap, i0=i0, i1=i1: e.scalar_tensor_tensor(
            out=o_ap, in0=i0, scalar=g[:, kt_g + kt:kt_g + kt + 1], in1=i1,
            op0=ALU.mult, op1=ALU.mult), reads=[x_b, g_b, C.rstd_b], writes=[h_b], nosame=(kt > 0))


class MLPPhase:
    def __init__(self, k, C, NT):
        self.k, self.C, self.NT = k, C, NT
        self.win = k.sbuf("mlp_win", [128, KT, DFF], BF16)
        self.win_b = Buf("mlp_win")
        self.wout = k.sbuf("mlp_wout", [128, FT, D], BF16)
        self.wout_b = Buf("mlp_wout")
        self.h = k.sbuf("mlp_h", [128, KT, NT], BF16)
        self.h_b = Buf("mlp_h")
        self.a = k.sbuf("mlp_a", [128, FT, NT], BF16)
        self.a_b = [Buf(f"mlp_a{i}") for i in range(FT)]
        self.r = [k.sbuf(f"mlp_r{i}", [128, NT], BF16) for i in range(2)]
        self.r_b = [Buf(f"mlp_r{i}") for i in range(2)]

    def load_weights(self, w_in, w_out, layer):
        k = self.k
        NCH = 4
        cw = DFF // NCH
        self.win_cb = [Buf(f"mlp_win_c{i}") for i in range(NCH)]
        for i in range(NCH):
            src = w_in[layer][:, i * cw:(i + 1) * cw].rearrange("(t p) c -> p t c", p=128)
            k.dma("pool", self.win[:, :, i * cw:(i + 1) * cw], src, writes=[self.win_cb[i]])
        self.ft_per_chunk = cw // 128
        load_w_bf16(k, w_out[layer], self.wout, self.wout_b, DFF, D, chunk_rt=4)

    def front(self, x, x_b, g, g_b, gcol, n):
        parts = [(self.r[i], self.r_b[i]) for i in range(2)]
        rmsnorm_T(self.k, self.C, x, x_b, g, g_b, gcol, self.h, self.h_b, n, None, None, sq_parts=parts)

    def mlp1(self, n):
        k, C = self.k, self.C
        for ft in range(FT):
            pt, pb = C.ps.next()

            def mm(e, ft=ft, pt=pt):
                ins = None
                for kt in range(KT):
                    ins = e.matmul(pt[:, :n], self.win[:, kt, ft * 128:(ft + 1) * 128], self.h[:, kt, :n],
                                   start=(kt == 0), stop=(kt == KT - 1))
                return ins
            k.emit("pe", mm, reads=[self.win_cb[ft // self.ft_per_chunk], self.h_b], writes=[pb])
            r, rb = self.r[ft % 2], self.r_b[ft % 2]
            k.emit("act", lambda e, pt=pt, r=r: e.activation(out=r[:, :n], in_=pt[:, :n], func=AF.Relu),
                   reads=[pb], writes=[rb])
            eng = "dve" if ft % 2 == 0 else "pool"
            k.emit(eng, lambda e, r=r, ft=ft: e.tensor_tensor(out=self.a[:, ft, :n], in0=r[:, :n], in1=r[:, :n],
                                                              op=ALU.mult),
                   reads=[rb], writes=[self.a_b[ft]])

    def mlp2(self, x, x_b, n):
        k, C = self.k, self.C
        for dt in range(KT):
            pt, pb = C.ps.next()

            def mm2(e, dt=dt, pt=pt):
                ins = None
                for ft in range(FT):
                    ins = e.matmul(pt[:, :n], self.wout[:, ft, dt * 128:(dt + 1) * 128], self.a[:, ft, :n],
                                   start=(ft == 0), stop=(ft == FT - 1))
                return ins
            k.emit("pe", mm2, reads=[self.wout_b] + self.a_b, writes=[pb])
            k.emit("dve", lambda e, dt=dt, pt=pt: e.tensor_tensor(out=x[:, dt, :n], in0=pt[:, :n], in1=x[:, dt, :n],
                                                                  op=ALU.add),
                   reads=[pb, x_b], writes=[x_b], nosame=(dt > 0))


VEC_SPECS = [("norm_mix", 32), ("norm_mlp", 32), ("norm_final", 8), ("ssm_d", 8), ("conv_b_pw1", 16),
             ("conv_w_dw", 31 * 8), ("conv_b_dw", 8), ("conv_ln_g", 8), ("conv_ln_b", 8), ("conv_b_pw2", 8)]
VCOL = {}
_c = 0
for _n, _w in VEC_SPECS:
    VCOL[_n] = _c
    _c += _w
NV = _c


def pack_vecs(inp):
    v = np.zeros((128, NV), np.float32)
    for name, w in VEC_SPECS:
        a = np.asarray(inp[name], np.float32).reshape(-1)
        v[:, VCOL[name]:VCOL[name] + w] = a.reshape(w, 128).T
    return v


CONST_SPECS = [("ident", 128), ("maskA", 128), ("maskB", 128), ("tril", 128), ("blkmask", 128), ("negA", 128), ("negB", 128)]
CCOL = {}
_c = 0
for _n, _w in CONST_SPECS:
    CCOL[_n] = _c
    _c += _w
NCONST = _c


def make_consts():
    c = np.zeros((128, NCONST), np.float32)
    j = np.arange(128)[:, None]
    i = np.arange(128)[None, :]
    c[:, CCOL["ident"]:CCOL["ident"] + 128] = (j == i)
    c[:, CCOL["maskA"]:CCOL["maskA"] + 128] = (j >= i)
    c[:, CCOL["maskB"]:CCOL["maskB"] + 128] = (j <= i)
    c[:, CCOL["tril"]:CCOL["tril"] + 128] = (j <= i)
    c[:, CCOL["blkmask"]:CCOL["blkmask"] + 128] = ((i % 8) >= (j % 8))
    c[:, CCOL["negA"]:CCOL["negA"] + 128] = np.where(j >= i, 0.0, -30000.0)
    c[:, CCOL["negB"]:CCOL["negB"] + 128] = np.where(j <= i, 0.0, -30000.0)
    return c


class DU:
    def __init__(self, name, ntok, unit=256):
        self.unit = unit
        self.b = [Buf(f"{name}_{i}") for i in range((ntok + unit - 1) // unit)]

    def rng(self, t0, t1):
        return self.b[t0 // self.unit:(t1 + self.unit - 1) // self.unit]


def _ldx(k, P, xt, xt_b, t, NT):
    k.dma("sp", xt[t % 2][:], P["X"][:, t * NT:(t + 1) * NT].rearrange("(kt p) n -> p kt n", p=128),
          reads=P["Xu"].rng(t * NT, (t + 1) * NT), writes=[xt_b[t % 2]])


def get_cst(k, P):
    cst = k.sbuf("consts_sb", [128, NCONST], F32)
    cst_b = Buf("consts")
    k.dma("sp", cst[:], P["consts"], writes=[cst_b])
    P["cst"], P["cst_b"] = cst, cst_b
    return cst, cst_b


def conv_phase(k, C, P, NT, layer):
    CW = 31
    TOK, S = P["TOK"], P["S"]
    k.begin_phase()
    vec, vec_b = P["vec"], P["vec_b"]
    cst, cst_b = get_cst(k, P)
    C.ident = k.sbuf("ident_bf", [128, 128], BF16); C.ident_b = Buf("ident_bf")
    k.emit("dve", lambda e: e.tensor_copy(out=C.ident[:], in_=cst[:, CCOL["ident"]:CCOL["ident"] + 128]),
           reads=[cst_b], writes=[C.ident_b])
    pw1 = k.sbuf("cv_pw1", [128, KT, 2 * D], BF16); pw1_b = Buf("cv_pw1")
    pw2 = k.sbuf("cv_pw2", [128, KT, D], BF16); pw2_b = Buf("cv_pw2")
    pw1_cb = load_w_cols(k, P["conv_w_pw1"], pw1, "cv_pw1", D, 2 * D)
    load_w_bf16(k, P["conv_w_pw2"], pw2, pw2_b, D, D, chunk_rt=4)
    diag = k.sbuf("cv_diag", [128, KT, CW, 128], BF16); diag_b = Buf("cv_diag")
    for ct in range(KT):
        for j in range(CW):
            col = VCOL["conv_w_dw"] + j * 8 + ct
            k.emit("dve", lambda e, ct=ct, j=j, col=col: e.tensor_scalar(
                out=diag[:, ct, j, :], in0=C.ident[:], scalar1=vec[:, col:col + 1], scalar2=None, op0=ALU.mult),
                reads=[C.ident_b, vec_b], writes=[diag_b])
    xt = [k.sbuf(f"cv_x{i}", [128, KT, NT], F32) for i in range(2)]
    xt_b = [Buf(f"cv_x{i}") for i in range(2)]
    h = k.sbuf("cv_h", [128, KT, NT], BF16); h_b = Buf("cv_h")
    zg = [k.sbuf(f"cv_zg{i}", [128, KT, CW - 1 + NT], BF16) for i in range(2)]
    zg_b = [[Buf(f"cv_zg{i}_{c}") for c in range(KT)] for i in range(2)]
    y = k.sbuf("cv_y", [128, KT, NT], F32); y_b = [Buf(f"cv_y{c}") for c in range(KT)]
    ybf = k.sbuf("cv_ybf", [128, KT, NT], BF16); ybf_b = Buf("cv_ybf")
    ysq = k.sbuf("cv_ysq", [128, KT, NT], BF16); ysq_b = Buf("cv_ysq")
    sg = [k.sbuf(f"cv_sg{i}", [128, NT], F32) for i in range(2)]
    sg_b = [Buf(f"cv_sg{i}") for i in range(2)]
    mean = k.sbuf("cv_mean", [128, NT], F32); mean_b = Buf("cv_mean")
    var = k.sbuf("cv_var", [128, NT], F32); var_b = Buf("cv_var")
    ntiles = TOK // NT
    tps = S // NT
    gcol = VCOL["norm_mix"] + layer * 8
    so = k.sbuf("cv_so", [128, KT, NT], BF16); so_b = Buf("cv_so")
    sqp = [(k.sbuf(f"cv_sqp{i}", [128, NT], BF16), Buf(f"cv_sqp{i}")) for i in range(2)]

    def F(t):
        rmsnorm_T(k, C, xt[t % 2], xt_b[t % 2], vec, vec_b, gcol, h, h_b, NT, None, None, sq_parts=sqp)

    def G(t):
        z, zb = zg[t % 2], zg_b[t % 2]
        zp, zpb = zg[(t + 1) % 2], zg_b[(t + 1) % 2]
        if t % tps == 0:
            k.emit("pool", lambda e: e.memset(z[:, :, 0:CW - 1], 0.0), writes=zb)
        else:
            k.emit("pool", lambda e: e.tensor_copy(out=z[:, :, 0:CW - 1], in_=zp[:, :, NT:NT + CW - 1]), reads=zpb, writes=zb)
        for ct in range(KT):
            pa, pab = C.ps.next()
            pb_, pbb = C.ps.next()

            def mm(e, ct=ct, pa=pa, pb_=pb_):
                ins = None
                for half, pt in ((0, pa), (1, pb_)):
                    c0 = half * D + ct * 128
                    for kt in range(KT):
                        ins = e.matmul(pt[:, :NT], pw1[:, kt, c0:c0 + 128], h[:, kt, :NT], start=(kt == 0),
                                       stop=(kt == KT - 1))
                return ins
            k.emit("pe", mm, reads=[pw1_cb[ct // 4], pw1_cb[2 + ct // 4], h_b], writes=[pab, pbb])
            s_, s_b = sg[ct % 2], sg_b[ct % 2]
            c2 = VCOL["conv_b_pw1"] + 8 + ct
            c1 = VCOL["conv_b_pw1"] + ct
            k.emit("act", lambda e, pb_=pb_, s_=s_, c2=c2: e.activation(out=s_[:, :NT], in_=pb_[:, :NT], func=AF.Sigmoid,
                                                                       bias=vec[:, c2:c2 + 1]),
                   reads=[pbb, vec_b], writes=[s_b])
            k.emit("dve", lambda e, pa=pa, s_=s_, c1=c1, ct=ct: e.scalar_tensor_tensor(
                out=z[:, ct, CW - 1:CW - 1 + NT], in0=pa[:, :NT], scalar=vec[:, c1:c1 + 1], in1=s_[:, :NT],
                op0=ALU.add, op1=ALU.mult), reads=[pab, s_b, vec_b], writes=[zb[ct]])

    def Cv(t):
        z, zb = zg[t % 2], zg_b[t % 2]
        for ct in range(KT):
            pt, pb = C.ps.next()

            def mmc(e, ct=ct, pt=pt):
                ins = None
                for j in range(CW):
                    ins = e.matmul(pt[:, :NT], diag[:, ct, j, :], z[:, ct, j:j + NT], start=(j == 0), stop=(j == CW - 1))
                return ins
            k.emit("pe", mmc, reads=[diag_b, zb[ct]], writes=[pb])
            cb = VCOL["conv_b_dw"] + ct
            k.emit("act", lambda e, pt=pt, ct=ct, cb=cb: e.activation(out=y[:, ct, :NT], in_=pt[:, :NT], func=AF.Identity,
                                                                     bias=vec[:, cb:cb + 1]),
                   reads=[pb, vec_b], writes=[y_b[ct]])
            k.emit("act", lambda e, pt=pt, ct=ct, cb=cb: e.activation(out=ysq[:, ct, :NT], in_=pt[:, :NT], func=AF.Square,
                                                                     bias=vec[:, cb:cb + 1]),
                   reads=[pb, vec_b], writes=[ysq_b], nosame=True)
            k.emit("dve", lambda e, ct=ct: e.tensor_copy(out=ybf[:, ct, :NT], in_=y[:, ct, :NT]),
                   reads=[y_b[ct]], writes=[ybf_b], nosame=True)

    def L(t):
        pm, pmb = C.ps.next()
        pq, pqb = C.ps.next()

        def mms(e):
            ins = None
            for src, pt in ((ybf, pm), (ysq, pq)):
                for kt in range(KT):
                    ins = e.matmul(pt[:, :NT], C.ones_bf[:], src[:, kt, :NT], start=(kt == 0), stop=(kt == KT - 1))
            return ins
        k.emit("pe", mms, reads=[ybf_b, ysq_b, C.ones_b], writes=[pmb, pqb])
        k.emit("act", lambda e: e.activation(out=mean[:, :NT], in_=pm[:, :NT], func=AF.Identity, scale=1.0 / D),
               reads=[pmb], writes=[mean_b])
        k.emit("dve", lambda e: e.tensor_tensor(out=var[:, :NT], in0=mean[:, :NT], in1=mean[:, :NT], op=ALU.mult),
               reads=[mean_b], writes=[var_b])
        k.emit("dve", lambda e: e.scalar_tensor_tensor(out=var[:, :NT], in0=pq[:, :NT], scalar=1.0 / D,
                                                       in1=var[:, :NT], op0=ALU.mult, op1=ALU.subtract),
               reads=[pqb, var_b], writes=[var_b])
        k.emit("act", lambda e: e.activation(out=var[:, :NT], in_=var[:, :NT], func=AF.Sqrt, bias=EPS),
               reads=[var_b], writes=[var_b])
        k.emit("dve", lambda e: e.reciprocal(out=var[:, :NT], in_=var[:, :NT]), reads=[var_b], writes=[var_b])
        for ct in range(KT):
            k.emit("dve", lambda e, ct=ct: e.tensor_tensor(out=y[:, ct, :NT], in0=y[:, ct, :NT], in1=mean[:, :NT],
                                                           op=ALU.subtract), reads=[y_b[ct], mean_b], writes=[y_b[ct]])
            k.emit("pool", lambda e, ct=ct: e.tensor_tensor(out=y[:, ct, :NT], in0=y[:, ct, :NT], in1=var[:, :NT],
                                                            op=ALU.mult), reads=[y_b[ct], var_b], writes=[y_b[ct]])
            cg = VCOL["conv_ln_g"] + ct
            cbb = VCOL["conv_ln_b"] + ct
            k.emit("act", lambda e, ct=ct, cg=cg, cbb=cbb: e.activation(
                out=so[:, ct, :NT], in_=y[:, ct, :NT], func=AF.Silu, scale=vec[:, cg:cg + 1], bias=vec[:, cbb:cbb + 1]),
                reads=[y_b[ct], vec_b], writes=[so_b], nosame=True)

    def W(t):
        x, xb = xt[t % 2], xt_b[t % 2]
        for dt in range(KT):
            pt, pb = C.ps.next()

            def mm2(e, dt=dt, pt=pt):
                ins = None
                for kt in range(KT):
                    ins = e.matmul(pt[:, :NT], pw2[:, kt, dt * 128:(dt + 1) * 128], so[:, kt, :NT], start=(kt == 0),
                                   stop=(kt == KT - 1))
                return ins
            k.emit("pe", mm2, reads=[pw2_b, so_b], writes=[pb])
            c3 = VCOL["conv_b_pw2"] + dt
            k.emit("dve", lambda e, dt=dt, pt=pt, c3=c3: e.scalar_tensor_tensor(
                out=x[:, dt, :NT], in0=pt[:, :NT], scalar=vec[:, c3:c3 + 1], in1=x[:, dt, :NT], op0=ALU.add, op1=ALU.add),
                reads=[pb, xb, vec_b], writes=[xb], nosame=(dt > 0))
        k.dma("sp", P["Xo"][:, t * NT:(t + 1) * NT].rearrange("(kt p) n -> p kt n", p=128), x[:], reads=[xb],
              writes=P["Xou"].rng(t * NT, (t + 1) * NT), kind="store")

    _ldx(k, P, xt, xt_b, 0, NT)
    F(0)
    G(0)
    for t in range(ntiles):
        if t + 1 < ntiles:
            _ldx(k, P, xt, xt_b, t + 1, NT)
            F(t + 1)
        Cv(t)
        L(t)
        if t + 1 < ntiles:
            G(t + 1)
        W(t)
    k.end_phase()


def gmlp_phase(k, C, P, NT, layer):
    TOK = P["TOK"]
    E = D
    k.begin_phase()
    vec, vec_b = P["vec"], P["vec_b"]
    cst, cst_b = get_cst(k, P)
    win = k.sbuf("gm_win", [128, KT, 2 * E], BF16); win_b = Buf("gm_win")
    wout = k.sbuf("gm_wout", [128, KT, D], BF16); wout_b = Buf("gm_wout")
    win_cb = load_w_cols(k, P["gmlp_w_in"], win, "gm_win", D, 2 * E)
    load_w_bf16(k, P["gmlp_w_out"], wout, wout_b, E, D, chunk_rt=4)
    wsf = k.sbuf("gm_wsf", [128, 4, 128], F32); wsf_b = Buf("gm_wsf")
    k.dma("sp", wsf[:], P["gmlp_wsT"], writes=[wsf_b])
    wsT = k.sbuf("gm_wsT", [128, 4, 128], BF16); wsT_b = Buf("gm_wsT")
    tril = cst[:, CCOL["tril"]:CCOL["tril"] + 128]
    for hh in range(4):
        k.emit("dve", lambda e, hh=hh: e.tensor_tensor(out=wsT[:, hh, :], in0=wsf[:, hh, :], in1=tril, op=ALU.mult),
               reads=[wsf_b, cst_b], writes=[wsT_b])
    bsf = k.sbuf("gm_bsf", [1, 512], F32); bsf_b = Buf("gm_bsf")
    k.dma("sp", bsf[:], P["gmlp_b_s"], writes=[bsf_b])
    bsr = k.sbuf("gm_bsr", [1, 512], BF16); bsr_b = Buf("gm_bsr")
    k.emit("dve", lambda e: e.tensor_copy(out=bsr[:], in_=bsf[:]), reads=[bsf_b], writes=[bsr_b])
    lng = k.sbuf("gm_lng", [128, E], F32); lng_b = Buf("gm_lng")
    lnb = k.sbuf("gm_lnb", [128, E], F32); lnb_b = Buf("gm_lnb")
    k.dma("sp", lng[:], P["gmlp_ln_g"].partition_broadcast(128), writes=[lng_b])
    k.dma("sp", lnb[:], P["gmlp_ln_b"].partition_broadcast(128), writes=[lnb_b])
    xt = [k.sbuf(f"gm_x{i}", [128, KT, NT], F32) for i in range(2)]
    xt_b = [Buf(f"gm_x{i}") for i in range(2)]
    h = k.sbuf("gm_h", [128, KT, NT], BF16); h_b = Buf("gm_h")
    sqp = [(k.sbuf(f"gm_sq{i}", [128, NT], BF16), Buf(f"gm_sq{i}")) for i in range(2)]
    gate = k.sbuf("gm_gate", [128, KT, NT], BF16); gate_b = Buf("gm_gate")
    us = [k.sbuf(f"gm_u{i}", [128, KT, NT], BF16) for i in range(2)]
    us_b = [Buf(f"gm_u{i}") for i in range(2)]
    vt = [k.sbuf(f"gm_vt{i}", [128, E], F32) for i in range(2)]
    vt_b = [Buf(f"gm_vt{i}") for i in range(2)]
    vln = [k.sbuf(f"gm_vln{i}", [128, E], BF16) for i in range(2)]
    vln_b = [Buf(f"gm_vln{i}") for i in range(2)]
    st = [k.sbuf(f"gm_st{i}", [128, 2, 6], F32) for i in range(2)]
    st_b = [Buf(f"gm_st{i}") for i in range(2)]
    mv = [k.sbuf(f"gm_mv{i}", [128, 2], F32) for i in range(2)]
    mv_b = [Buf(f"gm_mv{i}") for i in range(2)]
    gcol = VCOL["norm_mix"] + layer * 8
    nch = NT // 128
    n_t = TOK // NT

    def F(t):
        rmsnorm_T(k, C, xt[t % 2], xt_b[t % 2], vec, vec_b, gcol, h, h_b, NT, None, None, sq_parts=sqp)

    def U(t):
        u, u_b = us[t % 2], us_b[t % 2]
        for ft in range(KT):
            pt, pb = C.ps.next()

            def mm(e, ft=ft, pt=pt):
                ins = None
                for kt in range(KT):
                    ins = e.matmul(pt[:, :NT], win[:, kt, ft * 128:(ft + 1) * 128], h[:, kt, :NT], start=(kt == 0),
                                   stop=(kt == KT - 1))
                return ins
            k.emit("pe", mm, reads=[win_cb[ft // 4], h_b], writes=[pb])
            k.emit("act", lambda e, ft=ft, pt=pt, u=u: e.activation(out=u[:, ft, :NT], in_=pt[:, :NT], func=AF.Gelu_apprx_tanh),
                   reads=[pb], writes=[u_b], nosame=(ft > 0))

    def V(t, c):
        ci = t * nch + c
        v_, v_b = vt[ci % 2], vt_b[ci % 2]
        vl, vl_b = vln[ci % 2], vln_b[ci % 2]
        s_, s_b = st[ci % 2], st_b[ci % 2]
        m_, m_b = mv[ci % 2], mv_b[ci % 2]
        for half in range(2):
            pt, pb = C.ps.next()

            def mmv(e, half=half, pt=pt):
                ins = None
                for kt in range(KT):
                    ins = e.matmul(pt[:, :512], h[:, kt, c * 128:(c + 1) * 128],
                                   win[:, kt, E + half * 512:E + (half + 1) * 512], start=(kt == 0), stop=(kt == KT - 1))
                return ins
            k.emit("pe", mmv, reads=[win_cb[2 + half], h_b], writes=[pb])
            k.emit("act", lambda e, half=half, pt=pt: e.activation(
                out=v_[:, half * 512:(half + 1) * 512], in_=pt[:, :512], func=AF.Gelu_apprx_tanh),
                reads=[pb], writes=[v_b], nosame=(half > 0))
            k.emit("dve", lambda e, half=half: e.bn_stats(out=s_[:, half, :], in_=v_[:, half * 512:(half + 1) * 512]),
                   reads=[v_b], writes=[s_b], nosame=(half > 0))
        k.emit("dve", lambda e: e.bn_aggr(out=m_[:], in_=s_[:].rearrange("p a b -> p (a b)")), reads=[s_b], writes=[m_b])
        k.emit("act", lambda e: e.activation(out=m_[:, 1:2], in_=m_[:, 1:2], func=AF.Sqrt, bias=EPS), reads=[m_b], writes=[m_b])
        k.emit("dve", lambda e: e.reciprocal(out=m_[:, 1:2], in_=m_[:, 1:2]), reads=[m_b], writes=[m_b])
        k.emit("dve", lambda e: e.tensor_scalar(out=v_[:], in0=v_[:], scalar1=m_[:, 0:1], scalar2=m_[:, 1:2],
                                                op0=ALU.subtract, op1=ALU.mult), reads=[v_b, m_b], writes=[v_b])
        k.emit("pool", lambda e: e.tensor_tensor(out=v_[:], in0=v_[:], in1=lng[:], op=ALU.mult), reads=[v_b, lng_b], writes=[v_b])
        k.emit("pool", lambda e: e.tensor_tensor(out=vl[:], in0=v_[:], in1=lnb[:], op=ALU.add), reads=[v_b, lnb_b], writes=[vl_b])

    def S(t, c):
        ci = t * nch + c
        vl, vl_b = vln[ci % 2], vln_b[ci % 2]
        u, u_b = us[t % 2], us_b[t % 2]
        for eh in range(2):
            pt, pb = C.ps.next()

            def mms(e, eh=eh, pt=pt):
                ins = None
                for q in range(4):
                    et = eh * 4 + q
                    hd = et // 2
                    e.matmul(pt[:, q * 128:(q + 1) * 128], vl[:, et * 128:(et + 1) * 128], wsT[:, hd, :],
                             start=True, stop=False)
                    ins = e.matmul(pt[:, q * 128:(q + 1) * 128], C.ones_bf[0:1, :], bsr[0:1, hd * 128:(hd + 1) * 128],
                                   start=False, stop=True)
                return ins
            k.emit("pe", mms, reads=[vl_b, wsT_b, bsr_b, C.ones_b], writes=[pb])
            k.emit("dve", lambda e, eh=eh, pt=pt: e.tensor_tensor(
                out=gate[:, eh * 4:(eh + 1) * 4, c * 128:(c + 1) * 128],
                in0=pt[:, :512].rearrange("p (q n) -> p q n", q=4),
                in1=u[:, eh * 4:(eh + 1) * 4, c * 128:(c + 1) * 128], op=ALU.mult),
                reads=[pb, u_b], writes=[gate_b], nosame=True)

    def W(t):
        x, xb = xt[t % 2], xt_b[t % 2]
        for dt in range(KT):
            pt, pb = C.ps.next()

            def mm2(e, dt=dt, pt=pt):
                ins = None
                for kt in range(KT):
                    ins = e.matmul(pt[:, :NT], wout[:, kt, dt * 128:(dt + 1) * 128], gate[:, kt, :NT], start=(kt == 0),
                                   stop=(kt == KT - 1))
                return ins
            k.emit("pe", mm2, reads=[wout_b, gate_b], writes=[pb])
            k.emit("dve", lambda e, dt=dt, pt=pt: e.tensor_tensor(out=x[:, dt, :NT], in0=pt[:, :NT], in1=x[:, dt, :NT],
                                                                  op=ALU.add), reads=[pb, xb], writes=[xb], nosame=(dt > 0))
        k.dma("sp", P["Xo"][:, t * NT:(t + 1) * NT].rearrange("(kt p) n -> p kt n", p=128), x[:], reads=[xb],
              writes=P["Xou"].rng(t * NT, (t + 1) * NT), kind="store")

    _ldx(k, P, xt, xt_b, 0, NT)
    F(0); U(0); V(0, 0)
    for t in range(n_t):
        if t + 1 < n_t:
            _ldx(k, P, xt, xt_b, t + 1, NT)
        for c in range(1, nch):
            V(t, c)
            if c == nch - 1 and t + 1 < n_t:
                F(t + 1)
            S(t, c - 1)
        if t + 1 < n_t:
            U(t + 1)
        S(t, nch - 1)
        if t + 1 < n_t:
            V(t + 1, 0)
        W(t)
    k.end_phase()


ATT_DIL = (1, 4, 16)
GW = 512
QW = 3 * GW


def attn_phase_a(k, C, P, NT, layer):
    TOK = P["TOK"]
    k.begin_phase()
    vec, vec_b = P["vec"], P["vec_b"]
    wq = k.sbuf("at_wqkv", [128, KT, 3 * QW], BF16); wq_b = Buf("at_wqkv")
    wq_cb = load_w_cols(k, P["attn_w_qkv"], wq, "at_wqkv", D, 3 * QW)
    xt = [k.sbuf(f"at_x{i}", [128, KT, NT], F32) for i in range(2)]
    xt_b = [Buf(f"at_x{i}") for i in range(2)]
    h = k.sbuf("at_h", [128, KT, NT], BF16); h_b = Buf("at_h")
    sq = k.sbuf("at_sq", [128, KT, NT], BF16); sq_b = Buf("at_sq")
    qk = [k.sbuf(f"at_qk{i}", [128, 24, NT], BF16) for i in range(2)]
    qk_b = [Buf(f"at_qk{i}") for i in range(2)]
    vtok = [k.sbuf(f"at_vtok{i}", [128, NT // 128, QW], BF16) for i in range(2)]
    vtok_b = [Buf(f"at_vtok{i}") for i in range(2)]
    gcol = VCOL["norm_mix"] + layer * 8
    ev = 0
    for t in range(TOK // NT):
        x, xb = xt[t % 2], xt_b[t % 2]
        q_, q_b = qk[t % 2], qk_b[t % 2]
        v_, v_b = vtok[t % 2], vtok_b[t % 2]
        if t == 0:
            _ldx(k, P, xt, xt_b, 0, NT)
        if (t + 1) * NT < P["TOK"]:
            _ldx(k, P, xt, xt_b, t + 1, NT)
        rmsnorm_T(k, C, x, xb, vec, vec_b, gcol, h, h_b, NT, sq, [sq_b])
        for ft in range(24):
            pt, pb = C.ps.next()

            def mm(e, ft=ft, pt=pt):
                ins = None
                for kt in range(KT):
                    ins = e.matmul(pt[:, :NT], wq[:, kt, ft * 128:(ft + 1) * 128], h[:, kt, :NT], start=(kt == 0),
                                   stop=(kt == KT - 1))
                return ins
            k.emit("pe", mm, reads=[wq_cb[ft // 4], h_b], writes=[pb])
            ev += 1
            if ev % 2:
                k.emit("act", lambda e, ft=ft, pt=pt, q_=q_: e.activation(out=q_[:, ft, :NT], in_=pt[:, :NT], func=AF.Copy),
                       reads=[pb], writes=[q_b])
            else:
                k.emit("dve", lambda e, ft=ft, pt=pt, q_=q_: e.tensor_copy(out=q_[:, ft, :NT], in_=pt[:, :NT]),
                       reads=[pb], writes=[q_b])
        k.dma("sp", P["QK"][:, t * NT:(t + 1) * NT].rearrange("(ft p) n -> p ft n", p=128), q_[:], reads=[q_b],
              writes=P["QKu"].rng(t * NT, (t + 1) * NT), kind="store")
        for c in range(NT // 128):
            for g in range(3):
                pt, pb = C.ps.next()

                def mmv(e, g=g, pt=pt, c=c):
                    ins = None
                    for kt in range(KT):
                        ins = e.matmul(pt[:, :GW], h[:, kt, c * 128:(c + 1) * 128],
                                       wq[:, kt, 2 * QW + g * GW:2 * QW + (g + 1) * GW], start=(kt == 0), stop=(kt == KT - 1))
                    return ins
                k.emit("pe", mmv, reads=[wq_cb[6 + g], h_b], writes=[pb])
                ev += 1
                if ev % 2:
                    k.emit("act", lambda e, g=g, pt=pt, c=c, v_=v_: e.activation(out=v_[:, c, g * GW:(g + 1) * GW],
                                                                               in_=pt[:, :GW], func=AF.Copy),
                           reads=[pb], writes=[v_b])
                else:
                    k.emit("dve", lambda e, g=g, pt=pt, c=c, v_=v_: e.tensor_copy(out=v_[:, c, g * GW:(g + 1) * GW],
                                                                                in_=pt[:, :GW]),
                           reads=[pb], writes=[v_b])
        k.dma("sp", P["V"][t * NT:(t + 1) * NT, :].rearrange("(c p) f -> p c f", p=128), v_[:], reads=[v_b],
              writes=P["Vu"].rng(t * NT, (t + 1) * NT), kind="store")
    k.end_phase()


def attn_phase_b(k, C, P):
    TOK, S, NSEQ = P["TOK"], P["S"], P["NSEQ"]
    NKT = S // 128
    k.begin_phase()
    cst, cst_b = get_cst(k, P)
    mask = k.sbuf("ab_negm", [128, 2, 128], BF16); mask_b = Buf("ab_negm")
    for blk, nm in ((0, "maskA"), (1, "maskB")):
        k.emit("dve", lambda e, blk=blk, nm=nm: e.tensor_copy(out=mask[:, blk, :], in_=cst[:, CCOL[nm]:CCOL[nm] + 128]),
               reads=[cst_b], writes=[mask_b])
    identb = k.sbuf("ab_ident", [128, 128], BF16); identb_b = Buf("ab_ident")
    k.emit("dve", lambda e: e.tensor_copy(out=identb[:], in_=cst[:, CCOL["ident"]:CCOL["ident"] + 128]),
           reads=[cst_b], writes=[identb_b])
    sel = k.sbuf("ab_sel", [65, 64], F32); sel_b = Buf("ab_sel")
    k.emit("pool", lambda e: e.memset(sel[:], 0.0), writes=[sel_b])
    k.emit("pool", lambda e: e.memset(sel[64:65, :], 1.0), writes=[sel_b])
    qT = [k.sbuf(f"ab_q{i}", [128, S], BF16) for i in range(2)]
    kT = [k.sbuf(f"ab_k{i}", [128, S], BF16) for i in range(2)]
    vd = [k.sbuf(f"ab_v{i}", [128, NKT, 2, 65], BF16) for i in range(2)]
    qT_b = [Buf(f"ab_q{i}") for i in range(2)]
    kT_b = [Buf(f"ab_k{i}") for i in range(2)]
    vd_b = [Buf(f"ab_v{i}") for i in range(2)]
    for i in range(2):
        k.emit("pool", lambda e, i=i: e.memset(vd[i][:, :, :, 64:65], 1.0), writes=[vd_b[i]])
    oacc = k.sbuf("ab_oacc", [65, 2, S], F32); oacc_b = Buf("ab_oacc")
    NPE = 4
    pexp = [([k.sbuf(f"ab_pexp{i}_{hh}", [128, 2, 128], BF16) for hh in range(2)],
             [Buf(f"ab_pexp{i}_{hh}") for hh in range(2)]) for i in range(NPE)]
    mT = k.sbuf("ab_mT", [64, 2, S], BF16); mT_b = Buf("ab_mT")
    rec = [k.sbuf(f"ab_rec{i}", [64, 512], F32) for i in range(2)]
    rec_b = [Buf(f"ab_rec{i}") for i in range(2)]
    it = 0
    pi = 0
    ri = 0
    iters = []

    def make_block(sq_, hp, g):
        nonlocal it
        s0 = sq_ * S
        d = ATT_DIL[g]
        q_, k_, v_ = qT[it % 2], kT[it % 2], vd[it % 2]
        q_b, k_b, v_b = qT_b[it % 2], kT_b[it % 2], vd_b[it % 2]
        it += 1
        rq = g * GW + hp * 128
        nmt = S // (128 * d)

        def loads():
            k.dma("sp", q_[:], P["QK"][rq:rq + 128, s0:s0 + S], reads=P["QKu"].rng(s0, s0 + S), writes=[q_b])
            k.dma("sp", k_[:], P["QK"][QW + rq:QW + rq + 128, s0:s0 + S], reads=P["QKu"].rng(s0, s0 + S), writes=[k_b])
            for r in range(d):
                for hh in range(2):
                    src = P["V"][s0 + r:s0 + S:d, rq + hh * 64:rq + (hh + 1) * 64].rearrange("(mt p) dd -> p mt dd", p=128)
                    k.dma("sp", v_[:, r * nmt:(r + 1) * nmt, hh, 0:64], src, reads=P["Vu"].rng(s0, s0 + S),
                          writes=[v_b], nowaw=True)
        first = True
        for r in range(d):
            for mt in range(nmt):
                tb = r * nmt + mt
                qs = slice(r + d * 128 * mt, r + d * 128 * mt + d * 127 + 1, d)
                ks_prev = slice(r + d * 128 * (mt - 1), r + d * 128 * (mt - 1) + d * 127 + 1, d)
                blks = [1] if mt == 0 else [0, 1]
                st = {}

                def front(qs=qs, ks_prev=ks_prev, blks=blks, st=st, ld=(loads if first else None)):
                    nonlocal pi
                    if ld is not None:
                        ld()
                    scs = [C.ps.next() for _ in range(2)]
                    scv = [sc[:, :256].rearrange("p (b n) -> p b n", b=2) for sc, _ in scs]

                    def mmqk(e):
                        ins = None
                        for hh in range(2):
                            for blk in blks:
                                ksl = ks_prev if blk == 0 else qs
                                ins = e.matmul(scv[hh][:, blk, :], k_[hh * 64:(hh + 1) * 64, ksl],
                                               q_[hh * 64:(hh + 1) * 64, qs], start=True, stop=True)
                        return ins
                    k.emit("pe", mmqk, reads=[q_b, k_b], writes=[scs[0][1], scs[1][1]])
                    pe_, pe_b = pexp[pi % NPE]
                    pi += 1
                    b0 = blks[0]
                    for hh in range(2):
                        k.emit("act", lambda e, hh=hh: e.activation(
                            out=pe_[hh][:, b0:2, :], in_=scv[hh][:, b0:2, :], func=AF.Exp, scale=0.125),
                            reads=[scs[hh][1]], writes=[pe_b[hh]])
                        k.emit("pool" if hh == 0 else "dve", lambda e, hh=hh: e.tensor_tensor(
                            out=pe_[hh][:, b0:2, :], in0=pe_[hh][:, b0:2, :], in1=mask[:, b0:2, :], op=ALU.mult),
                            reads=[pe_b[hh], mask_b], writes=[pe_b[hh]])
                    st["pe"] = (pe_, pe_b)

                def back(qs=qs, blks=blks, st=st, tb=tb):
                    pe_, pe_b = st["pe"]
                    po, po_b = C.ps.next()
                    pov = po[0:65, 0:256].rearrange("p (a n) -> p a n", a=2)

                    def mmpv(e):
                        ins = None
                        for hh in range(2):
                            for bi, blk in enumerate(blks):
                                vt_ = tb - 1 if blk == 0 else tb
                                ins = e.matmul(pov[:, hh, :], v_[:, vt_, hh, :], pe_[hh][:, blk, :],
                                               start=(bi == 0), stop=(bi == len(blks) - 1))
                        return ins
                    k.emit("pe", mmpv, reads=[v_b] + pe_b, writes=[po_b])
                    if g == 0:
                        k.emit("dve", lambda e: e.tensor_copy(out=oacc[:, :, qs], in_=pov), reads=[po_b], writes=[oacc_b])
                    else:
                        k.emit("dve", lambda e: e.tensor_tensor(out=oacc[:, :, qs], in0=pov, in1=oacc[:, :, qs], op=ALU.add),
                               reads=[po_b, oacc_b], writes=[oacc_b])
                iters.append([front, back, None])
                first = False

    def make_post(sq_, hp):
        s0 = sq_ * S

        def post():
            nonlocal ri
            for hh in range(2):
                for c0 in range(0, S, 512):
                    db, db_b = C.ps.next()
                    k.emit("pe", lambda e, db=db, hh=hh, c0=c0: e.matmul(db[0:64, :512], sel[:, :], oacc[:, hh, c0:c0 + 512],
                                                                        start=True, stop=True),
                           reads=[sel_b, oacc_b], writes=[db_b])
                    r_, r_b = rec[ri % 2], rec_b[ri % 2]
                    ri += 1
                    k.emit("dve", lambda e, db=db, r_=r_: e.reciprocal(out=r_[:, :], in_=db[0:64, :512]),
                           reads=[db_b], writes=[r_b])
                    k.emit("pool", lambda e, r_=r_, hh=hh, c0=c0: e.tensor_tensor(
                        out=mT[:, hh, c0:c0 + 512], in0=oacc[0:64, hh, c0:c0 + 512], in1=r_[:, :], op=ALU.mult),
                        reads=[r_b, oacc_b], writes=[mT_b])
            for hh in range(2):
                rm = (hp * 2 + hh) * 64
                k.dma("sp", P["M"][rm:rm + 64, s0:s0 + S], mT[:, hh, :], reads=[mT_b], writes=P["Mu"].rng(s0, s0 + S),
                      kind="store")
        return post

    for sq_ in range(NSEQ):
        for hp in range(4):
            for g in range(3):
                make_block(sq_, hp, g)
            iters[-1][2] = make_post(sq_, hp)
    n_it = len(iters)
    LA = 2
    for i in range(min(LA, n_it)):
        iters[i][0]()
    for i in range(n_it):
        if i + LA < n_it:
            iters[i + LA][0]()
        iters[i][1]()
        if iters[i][2] is not None:
            iters[i][2]()
    k.end_phase()


def attn_phase_c(k, C, P, NT):
    TOK = P["TOK"]
    k.begin_phase()
    wo = k.sbuf("ac_wo", [128, 4, D], BF16); wo_b = Buf("ac_wo")
    load_w_bf16(k, P["attn_w_o"], wo, wo_b, GW, D, chunk_rt=4)
    xt = [k.sbuf(f"ac_x{i}", [128, KT, NT], F32) for i in range(2)]
    xt_b = [Buf(f"ac_x{i}") for i in range(2)]
    mt_ = [k.sbuf(f"ac_m{i}", [128, 4, NT], BF16) for i in range(2)]
    mt_b = [Buf(f"ac_m{i}") for i in range(2)]
    for t in range(TOK // NT):
        x, xb = xt[t % 2], xt_b[t % 2]
        m, mb = mt_[t % 2], mt_b[t % 2]
        if t == 0:
            _ldx(k, P, xt, xt_b, 0, NT)
        if (t + 1) * NT < P["TOK"]:
            _ldx(k, P, xt, xt_b, t + 1, NT)
        def _ldm(tt):
            k.dma("sp", mt_[tt % 2][:], P["M"][:, tt * NT:(tt + 1) * NT].rearrange("(kt p) n -> p kt n", p=128),
                  reads=P["Mu"].rng(tt * NT, (tt + 1) * NT), writes=[mt_b[tt % 2]])
        if t == 0:
            _ldm(0)
        if (t + 1) * NT < P["TOK"]:
            _ldm(t + 1)
        for dt in range(KT):
            pt, pb = C.ps.next()

            def mm(e, dt=dt, pt=pt, m=m):
                ins = None
                for kt in range(4):
                    ins = e.matmul(pt[:, :NT], wo[:, kt, dt * 128:(dt + 1) * 128], m[:, kt, :NT], start=(kt == 0),
                                   stop=(kt == 3))
                return ins
            k.emit("pe", mm, reads=[wo_b, mb], writes=[pb])
            k.emit("dve", lambda e, dt=dt, pt=pt, x=x: e.tensor_tensor(out=x[:, dt, :NT], in0=pt[:, :NT], in1=x[:, dt, :NT],
                                                                      op=ALU.add), reads=[pb, xb], writes=[xb])
        k.dma("sp", P["Xo"][:, t * NT:(t + 1) * NT].rearrange("(kt p) n -> p kt n", p=128), x[:], reads=[xb],
              writes=P["Xou"].rng(t * NT, (t + 1) * NT), kind="store")
    k.end_phase()


SL = 8
SCB = 32
SNT = SL * SCB
TWO_PI = 2.0 * math.pi


def s5_prep(k, C, P, own_phase=True):
    if own_phase:
        k.begin_phase()
    else:
        k.deferred = []
    cst, cst_b = get_cst(k, P)
    GH = 16
    f32t = lambda name, shape: (k.sbuf(name, shape, F32), Buf(name))
    aTr, aTr_b = f32t("sp_aTr", [64, 64]); aTi, aTi_b = f32t("sp_aTi", [64, 64])
    ldt, ldt_b = f32t("sp_ldt", [64, 64])
    k.dma("sp", aTr[:], P["ssm_aT_re"], writes=[aTr_b])
    k.dma("sp", aTi[:], P["ssm_aT_im"], writes=[aTi_b])
    k.dma("sp", ldt[:], P["ssm_log_dt"].partition_broadcast(64), writes=[ldt_b])
    lr, lr_b = f32t("sp_lr", [64, 64]); li, li_b = f32t("sp_li", [64, 64])
    k.emit("act", lambda e: e.activation(out=ldt[:], in_=ldt[:], func=AF.Exp), reads=[ldt_b], writes=[ldt_b])
    k.emit("dve", lambda e: e.tensor_tensor(out=lr[:], in0=aTr[:], in1=ldt[:], op=ALU.mult), reads=[aTr_b, ldt_b], writes=[lr_b])
    k.emit("dve", lambda e: e.tensor_tensor(out=li[:], in0=aTi[:], in1=ldt[:], op=ALU.mult), reads=[aTi_b, ldt_b], writes=[li_b])
    PWr, PWr_b = f32t("sp_PWr", [64, 16, 64]); PWi, PWi_b = f32t("sp_PWi", [64, 16, 64])
    KV, KV_b = f32t("sp_KV", [64, 16, 64])
    mag, mag_b = f32t("sp_mag", [64, 16, 64])
    rr, rr_b = f32t("sp_rr", [64, 16, 64]); nf, nf_b = f32t("sp_nf", [64, 16, 64]); mk, mk_b = f32t("sp_mk", [64, 16, 64])
    ni = k.sbuf("sp_ni", [64, 16, 64], mybir.dt.int32); ni_b = Buf("sp_ni")
    powers = [-(s_ + 1) for s_ in range(8)] + [t_ + 1 for t_ in range(8)]
    for idx, kk in enumerate(powers):
        k.emit("pool", lambda e, idx=idx, kk=kk: e.memset(KV[:, idx, :], float(kk)), writes=[KV_b], nosame=True)
    bc16 = lambda ap: ap.unsqueeze(1).to_broadcast([64, 16, 64])
    k.emit("dve", lambda e: e.tensor_tensor(out=mag[:], in0=KV[:], in1=bc16(lr[:]), op=ALU.mult), reads=[KV_b, lr_b], writes=[mag_b])
    k.emit("act", lambda e: e.activation(out=mag[:], in_=mag[:], func=AF.Exp), reads=[mag_b], writes=[mag_b])
    for dst, dst_b, off in ((PWi, PWi_b, 0.0), (PWr, PWr_b, 0.25)):
        k.emit("dve", lambda e: e.tensor_tensor(out=rr[:], in0=KV[:], in1=bc16(li[:]), op=ALU.mult), reads=[KV_b, li_b], writes=[rr_b])
        k.emit("dve", lambda e, off=off: e.tensor_scalar(out=rr[:], in0=rr[:], scalar1=1.0 / TWO_PI, scalar2=64.5 + off,
                                                        op0=ALU.mult, op1=ALU.add), reads=[rr_b], writes=[rr_b])
        k.emit("dve", lambda e: e.tensor_copy(out=ni[:], in_=rr[:]), reads=[rr_b], writes=[ni_b])
        k.emit("dve", lambda e: e.tensor_copy(out=nf[:], in_=ni[:]), reads=[ni_b], writes=[nf_b])
        k.emit("dve", lambda e: e.tensor_tensor(out=rr[:], in0=rr[:], in1=nf[:], op=ALU.subtract), reads=[rr_b, nf_b], writes=[rr_b])
        k.emit("dve", lambda e: e.tensor_single_scalar(out=mk[:], in_=rr[:], scalar=0.0, op=ALU.is_lt), reads=[rr_b], writes=[mk_b])
        k.emit("dve", lambda e: e.tensor_tensor(out=rr[:], in0=rr[:], in1=mk[:], op=ALU.add), reads=[rr_b, mk_b], writes=[rr_b])
        k.emit("dve", lambda e: e.tensor_single_scalar(out=mk[:], in_=rr[:], scalar=1.0, op=ALU.is_ge), reads=[rr_b], writes=[mk_b])
        k.emit("dve", lambda e: e.tensor_tensor(out=rr[:], in0=rr[:], in1=mk[:], op=ALU.subtract), reads=[rr_b, mk_b], writes=[rr_b])
        k.emit("act", lambda e: e.activation(out=nf[:], in_=rr[:], func=AF.Sin, scale=TWO_PI, bias=-math.pi),
               reads=[rr_b], writes=[nf_b])
        k.emit("dve", lambda e, dst=dst: e.tensor_tensor(out=dst[:], in0=nf[:], in1=mag[:], op=ALU.mult),
               reads=[nf_b, mag_b], writes=[dst_b])
    gr, gr_b = f32t("sp_gr", [64, 64]); gi, gi_b = f32t("sp_gi", [64, 64])
    den, den_b = f32t("sp_den", [64, 64]); t1, t1_b = f32t("sp_t1", [64, 64]); t2, t2_b = f32t("sp_t2", [64, 64])
    am1, am1_b = f32t("sp_am1", [64, 64])
    K1 = 8
    dv = lambda fn, reads, writes: k.emit("dve", fn, reads=reads, writes=writes)
    dv(lambda e: e.tensor_scalar_add(out=am1[:], in0=PWr[:, K1, :], scalar1=-1.0), [PWr_b], [am1_b])
    dv(lambda e: e.tensor_tensor(out=den[:], in0=aTr[:], in1=aTr[:], op=ALU.mult), [aTr_b], [den_b])
    dv(lambda e: e.tensor_tensor(out=t1[:], in0=aTi[:], in1=aTi[:], op=ALU.mult), [aTi_b], [t1_b])
    dv(lambda e: e.tensor_tensor(out=den[:], in0=den[:], in1=t1[:], op=ALU.add), [den_b, t1_b], [den_b])
    dv(lambda e: e.reciprocal(out=den[:], in_=den[:]), [den_b], [den_b])
    dv(lambda e: e.tensor_tensor(out=t1[:], in0=am1[:], in1=aTr[:], op=ALU.mult), [am1_b, aTr_b], [t1_b])
    dv(lambda e: e.tensor_tensor(out=t2[:], in0=PWi[:, K1, :], in1=aTi[:], op=ALU.mult), [PWi_b, aTi_b], [t2_b])
    dv(lambda e: e.tensor_tensor(out=t1[:], in0=t1[:], in1=t2[:], op=ALU.add), [t1_b, t2_b], [t1_b])
    dv(lambda e: e.tensor_tensor(out=gr[:], in0=t1[:], in1=den[:], op=ALU.mult), [t1_b, den_b], [gr_b])
    dv(lambda e: e.tensor_tensor(out=t1[:], in0=PWi[:, K1, :], in1=aTr[:], op=ALU.mult), [PWi_b, aTr_b], [t1_b])
    dv(lambda e: e.tensor_tensor(out=t2[:], in0=am1[:], in1=aTi[:], op=ALU.mult), [am1_b, aTi_b], [t2_b])
    dv(lambda e: e.tensor_tensor(out=t1[:], in0=t1[:], in1=t2[:], op=ALU.subtract), [t1_b, t2_b], [t1_b])
    dv(lambda e: e.tensor_tensor(out=gi[:], in0=t1[:], in1=den[:], op=ALU.mult), [t1_b, den_b], [gi_b])
    ab8, ab8_b = f32t("sp_ab8", [64, 2, 64])
    dv(lambda e: e.tensor_copy(out=ab8[:, 0, :], in_=PWr[:, 15, :]), [PWr_b], [ab8_b])
    dv(lambda e: e.tensor_copy(out=ab8[:, 1, :], in_=PWi[:, 15, :]), [PWi_b], [ab8_b])
    k.dma("sp", P["Wd_ab"], ab8[:], reads=[ab8_b], writes=[P["Wd_b"]], kind="store")
    if "s5dump" in P["dbg"]:
        k.dma("sp", P["dump"][0:64, 0:1024], PWr[:].rearrange("p a b -> p (a b)"), reads=[PWr_b], writes=[P["Wd_b"]], kind="store")
        k.dma("sp", P["dump"][0:64, 1024:2048], PWi[:].rearrange("p a b -> p (a b)"), reads=[PWi_b], writes=[P["Wd_b"]], kind="store")
        k.dma("sp", P["dump"][0:64, 2048:2112], gr[:], reads=[gr_b], writes=[P["Wd_b"]], kind="store")
        k.dma("sp", P["dump"][0:64, 2112:2176], gi[:], reads=[gi_b], writes=[P["Wd_b"]], kind="store")
    Bre, Bre_b = f32t("sp_Bre", [64, GH, 16]); Bim, Bim_b = f32t("sp_Bim", [64, GH, 16])
    Cre, Cre_b = f32t("sp_Cre", [64, GH, 16]); Cim, Cim_b = f32t("sp_Cim", [64, GH, 16])
    Gre, Gre_b = f32t("sp_Gre", [64, GH, 16]); Gim, Gim_b = f32t("sp_Gim", [64, GH, 16])
    tA, tA_b = f32t("sp_tA", [64, GH, 16]); tB, tB_b = f32t("sp_tB", [64, GH, 16])
    Ere, Ere_b = f32t("sp_Ere", [64, GH, 16, 8]); Eim, Eim_b = f32t("sp_Eim", [64, GH, 16, 8])
    Fre, Fre_b = f32t("sp_Fre", [64, GH, 16, 8]); Fmi, Fmi_b = f32t("sp_Fmi", [64, GH, 16, 8])
    Qre, Qre_b = f32t("sp_Qre", [64, GH, 128]); Qim, Qim_b = f32t("sp_Qim", [64, GH, 128])
    tQ, tQ_b = f32t("sp_tQ", [64, GH, 128])
    T_sb = k.sbuf("sp_Tsb", [128, GH, 128], BF16); T_sbb = Buf("sp_Tsb")
    Bt_sb = k.sbuf("sp_Btsb", [128, GH, 2, 64], BF16); Bt_sbb = Buf("sp_Btsb")
    F_sb = k.sbuf("sp_Fsb", [64, GH, 2, 128], BF16); F_sbb = Buf("sp_Fsb")
    blkmask = cst[:, CCOL["blkmask"]:CCOL["blkmask"] + 128]
    ident64 = cst[0:64, CCOL["ident"]:CCOL["ident"] + 64]

    def bc(ap2d, n):
        return ap2d.unsqueeze(2).to_broadcast([64, GH, n])

    def cmul(out_re, out_re_b, out_im, out_im_b, x_re, x_im, x_bs, w_re, w_im, w_bs, n, neg_im=False):
        dv(lambda e: e.tensor_tensor(out=tA[:, :, :n] if n <= 16 else tQ[:], in0=x_re, in1=bc(w_re, n), op=ALU.mult),
           x_bs + w_bs, [tA_b if n <= 16 else tQ_b])
        sc1 = tA[:, :, :n] if n <= 16 else tQ[:]
        sc1_b = tA_b if n <= 16 else tQ_b
        dv(lambda e: e.tensor_tensor(out=out_re, in0=x_im, in1=bc(w_im, n), op=ALU.mult), x_bs + w_bs, [out_re_b])
        dv(lambda e: e.tensor_tensor(out=out_re, in0=sc1, in1=out_re, op=ALU.subtract), [sc1_b, out_re_b], [out_re_b])
        dv(lambda e: e.tensor_tensor(out=sc1, in0=x_re, in1=bc(w_im, n), op=ALU.mult), x_bs + w_bs, [sc1_b])
        dv(lambda e: e.tensor_tensor(out=out_im, in0=x_im, in1=bc(w_re, n), op=ALU.mult), x_bs + w_bs, [out_im_b])
        if neg_im:
            dv(lambda e: e.scalar_tensor_tensor(out=out_im, in0=sc1, scalar=-1.0, in1=out_im, op0=ALU.mult, op1=ALU.subtract),
               [sc1_b, out_im_b], [out_im_b])
        else:
            dv(lambda e: e.tensor_tensor(out=out_im, in0=sc1, in1=out_im, op=ALU.add), [sc1_b, out_im_b], [out_im_b])

    t4, t4_b = f32t("sp_t4", [64, GH, 16, 8])

    def cmul4(o_re, o_re_b, o_im, o_im_b, x_re, x_im, x_bs, k0, g0, neg_im):
        xr = x_re[:].unsqueeze(3).to_broadcast([64, GH, 16, 8])
        xi = x_im[:].unsqueeze(3).to_broadcast([64, GH, 16, 8])
        wr = PWr[:, k0:k0 + 8, g0:g0 + GH].rearrange("p s g -> p g s").unsqueeze(2).to_broadcast([64, GH, 16, 8])
        wi = PWi[:, k0:k0 + 8, g0:g0 + GH].rearrange("p s g -> p g s").unsqueeze(2).to_broadcast([64, GH, 16, 8])
        wb = [PWr_b, PWi_b]
        dv(lambda e: e.tensor_tensor(out=t4[:], in0=xr, in1=wr, op=ALU.mult), x_bs + wb, [t4_b])
        dv(lambda e: e.tensor_tensor(out=o_re[:], in0=xi, in1=wi, op=ALU.mult), x_bs + wb, [o_re_b])
        dv(lambda e: e.tensor_tensor(out=o_re[:], in0=t4[:], in1=o_re[:], op=ALU.subtract), [t4_b, o_re_b], [o_re_b])
        dv(lambda e: e.tensor_tensor(out=t4[:], in0=xr, in1=wi, op=ALU.mult), x_bs + wb, [t4_b])
        dv(lambda e: e.tensor_tensor(out=o_im[:], in0=xi, in1=wr, op=ALU.mult), x_bs + wb, [o_im_b])
        if neg_im:
            dv(lambda e: e.scalar_tensor_tensor(out=o_im[:], in0=t4[:], scalar=-1.0, in1=o_im[:], op0=ALU.mult, op1=ALU.subtract),
               [t4_b, o_im_b], [o_im_b])
        else:
            dv(lambda e: e.tensor_tensor(out=o_im[:], in0=t4[:], in1=o_im[:], op=ALU.add), [t4_b, o_im_b], [o_im_b])

    for gh in range(64 // GH):
        g0 = gh * GH
        k.dma("sp", Bre[:], P["ssm_BT_re"][:, g0:g0 + GH, :], writes=[Bre_b])
        k.dma("sp", Bim[:], P["ssm_BT_im"][:, g0:g0 + GH, :], writes=[Bim_b])
        k.dma("sp", Cre[:], P["ssm_CT_re"][:, g0:g0 + GH, :], writes=[Cre_b])
        k.dma("sp", Cim[:], P["ssm_CT_im"][:, g0:g0 + GH, :], writes=[Cim_b])
        cmul(Gre[:], Gre_b, Gim[:], Gim_b, Bre[:], Bim[:], [Bre_b, Bim_b], gr[:, g0:g0 + GH], gi[:, g0:g0 + GH], [gr_b, gi_b], 16)
        cmul4(Ere, Ere_b, Eim, Eim_b, Gre, Gim, [Gre_b, Gim_b], 0, g0, False)
        cmul4(Fre, Fre_b, Fmi, Fmi_b, Cre, Cim, [Cre_b, Cim_b], 8, g0, True)
        Ere2 = Ere[:].rearrange("p g a b -> p g (a b)")
        Eim2 = Eim[:].rearrange("p g a b -> p g (a b)")
        Fre2 = Fre[:].rearrange("p g a b -> p g (a b)")
        Fmi2 = Fmi[:].rearrange("p g a b -> p g (a b)")
        cmul(Qre[:], Qre_b, Qim[:], Qim_b, Ere2, Eim2, [Ere_b, Eim_b], PWr[:, 15, g0:g0 + GH], PWi[:, 15, g0:g0 + GH],
             [PWr_b, PWi_b], 128)
        for g4 in range(0, GH, 4):
            pt, pb = C.ps.next()

            def mmT(e, g4=g4, pt=pt):
                ins = None
                for q in range(4):
                    e.matmul(pt[:, q * 128:(q + 1) * 128], Ere2[:, g4 + q, :], Fre2[:, g4 + q, :], start=True, stop=False)
                    ins = e.matmul(pt[:, q * 128:(q + 1) * 128], Eim2[:, g4 + q, :], Fmi2[:, g4 + q, :], start=False, stop=True)
                return ins
            k.emit("pe", mmT, reads=[Ere_b, Eim_b, Fre_b, Fmi_b], writes=[pb])
            dv(lambda e, g4=g4, pt=pt: e.tensor_tensor(out=T_sb[:, g4:g4 + 4, :], in0=pt[:, :512].rearrange("p (q n) -> p q n", q=4),
                                                        in1=blkmask.unsqueeze(1).to_broadcast([128, 4, 128]), op=ALU.mult),
               [pb, cst_b], [T_sbb])
        for ri, Q in ((0, Qre), (1, Qim)):
            for g8 in range(0, GH, 8):
                pt, pb = C.ps.next()

                def mmB(e, g8=g8, pt=pt, Q=Q):
                    ins = None
                    for q in range(8):
                        ins = e.matmul(pt[:, q * 64:(q + 1) * 64], Q[:, g8 + q, :], ident64, start=True, stop=True)
                    return ins
                k.emit("pe", mmB, reads=[Qre_b, Qim_b, cst_b], writes=[pb])
                k.emit("act", lambda e, g8=g8, pt=pt, ri=ri: e.activation(
                    out=Bt_sb[:, g8:g8 + 8, ri, :], in_=pt[:, :512].rearrange("p (q n) -> p q n", q=8), func=AF.Copy),
                    reads=[pb], writes=[Bt_sbb])
        k.emit("act", lambda e: e.activation(out=F_sb[:, :, 0, :], in_=Fre2, func=AF.Copy), reads=[Fre_b], writes=[F_sbb])
        k.emit("act", lambda e: e.activation(out=F_sb[:, :, 1, :], in_=Fmi2, func=AF.Copy), reads=[Fmi_b], writes=[F_sbb])
        k.dma("sp", P["Wd_T"][:, g0:g0 + GH, :], T_sb[:], reads=[T_sbb], writes=[P["Wd_b"]], kind="store")
        k.dma("sp", P["Wd_Bt"][:, g0:g0 + GH, :, :], Bt_sb[:], reads=[Bt_sbb], writes=[P["Wd_b"]], kind="store")
        k.dma("sp", P["Wd_F"][:, g0:g0 + GH, :, :], F_sb[:], reads=[F_sbb], writes=[P["Wd_b"]], kind="store")
    if own_phase:
        k.end_phase()
        return None
    thunks = k.deferred
    k.deferred = None
    return thunks


def s5_phase_a(k, C, P, layer, with_prep=False):
    TOK = P["TOK"]
    NT = SNT
    k.begin_phase()
    thunks = s5_prep(k, C, P, own_phase=False) if with_prep else []
    n_tiles = TOK // NT
    per_tile = (len(thunks) + n_tiles - 1) // n_tiles if thunks else 0
    vec, vec_b = P["vec"], P["vec_b"]
    xt = [k.sbuf(f"sa_x{i}", [128, KT, NT], F32) for i in range(2)]
    xt_b = [Buf(f"sa_x{i}") for i in range(2)]
    hp = [k.sbuf(f"sa_hp{i}", [128, KT, SL, SCB], BF16) for i in range(2)]
    hp_b = [Buf(f"sa_hp{i}") for i in range(2)]
    sq = k.sbuf("sa_sq", [128, KT, NT], BF16); sq_b = Buf("sa_sq")
    gcol = VCOL["norm_mix"] + layer * 8
    for t in range(TOK // NT):
        x, xb = xt[t % 2], xt_b[t % 2]
        h, hb = hp[t % 2], hp_b[t % 2]
        if t == 0:
            _ldx(k, P, xt, xt_b, 0, NT)
        if (t + 1) * NT < P["TOK"]:
            _ldx(k, P, xt, xt_b, t + 1, NT)
        hview = h[:].rearrange("p k s c -> p k (s c)")
        rmsnorm_T(k, C, x, xb, vec, vec_b, gcol, None, hb, NT, sq, [sq_b],
                  out_fn=lambda kt, h=h: h[:, kt, :, :].rearrange("p s c -> p c s"),
                  in_fn=lambda ap: ap.rearrange("p (c s) -> p c s", s=SL))
        dst = P["Ud"][t].rearrange("(gh gl) p s c -> (gl p) gh (s c)", gl=8)
        k.dma("sp", dst, h[:].rearrange("p k s c -> p k (s c)"), reads=[hb], writes=[P["Udu"][t]], kind="store")
        for th in thunks[t * per_tile:(t + 1) * per_tile]:
            th()
    for th in thunks[n_tiles * per_tile:]:
        th()
    k.end_phase()


def s5_phase_b(k, C, P):
    TOK, S, NSEQ = P["TOK"], P["S"], P["NSEQ"]
    NB = S // SNT
    NG = 64
    k.begin_phase()
    Tw = k.sbuf("sb_T", [128, NG, 128], BF16); Tw_b = Buf("sb_T")
    Btw = k.sbuf("sb_Bt", [128, NG, 2, 64], BF16); Btw_b = Buf("sb_Bt")
    Fw = k.sbuf("sb_F", [64, NG, 2, 128], BF16); Fw_b = Buf("sb_F")
    ab = k.sbuf("sb_ab", [64, 2, NG], F32); ab_b = Buf("sb_ab")
    k.dma("sp", Tw[:], P["Wd_T"], reads=[P["Wd_b"]], writes=[Tw_b])
    k.dma("sp", Btw[:], P["Wd_Bt"], reads=[P["Wd_b"]], writes=[Btw_b])
    k.dma("sp", Fw[:], P["Wd_F"], reads=[P["Wd_b"]], writes=[Fw_b])
    k.dma("sp", ab[:], P["Wd_ab"], reads=[P["Wd_b"]], writes=[ab_b])
    W = NSEQ * NG
    AR2 = k.sbuf("sb_AR2", [64, 2, NSEQ, NG], F32); AR2_b = Buf("sb_AR2")
    ASI = k.sbuf("sb_ASI", [64, 2, NSEQ, NG], F32); ASI_b = Buf("sb_ASI")
    for sq_ in range(NSEQ):
        for ri in range(2):
            k.emit("dve", lambda e, sq_=sq_, ri=ri: e.tensor_copy(out=AR2[:, ri, sq_, :], in_=ab[:, 0, :]), reads=[ab_b], writes=[AR2_b])
        k.emit("dve", lambda e, sq_=sq_: e.tensor_scalar(out=ASI[:, 0, sq_, :], in0=ab[:, 1, :], scalar1=-1.0, scalar2=None,
                                                        op0=ALU.mult), reads=[ab_b], writes=[ASI_b])
        k.emit("dve", lambda e, sq_=sq_: e.tensor_copy(out=ASI[:, 1, sq_, :], in_=ab[:, 1, :]), reads=[ab_b], writes=[ASI_b])
    U = [k.sbuf(f"sb_U{i}", [128, NSEQ, NG, SCB], BF16) for i in range(2)]
    U_b = [Buf(f"sb_U{i}") for i in range(2)]
    Vs = [k.sbuf(f"sb_V{i}", [64, 2, NSEQ, NG, SCB + 1], F32) for i in range(2)]
    Vs_b = [Buf(f"sb_V{i}") for i in range(2)]
    Hb = k.sbuf("sb_Hb", [64, 2, NSEQ, NG, SCB], BF16); Hb_b = Buf("sb_Hb")
    Ys = [k.sbuf(f"sb_Ys{i}", [128, NSEQ, NG, SCB], BF16) for i in range(2)]
    Ys_b = [Buf(f"sb_Ys{i}") for i in range(2)]
    HG = NG // 2
    t1 = [k.sbuf(f"sb_t1{i}", [64, 2, NSEQ, HG], F32) for i in range(2)]
    t2 = [k.sbuf(f"sb_t2{i}", [64, 2, NSEQ, HG], F32) for i in range(2)]
    t1_b = [Buf(f"sb_t1{i}") for i in range(2)]
    t2_b = [Buf(f"sb_t2{i}") for i in range(2)]
    k.emit("pool", lambda e: e.memset(Vs[0][:, :, :, :, 0:1], 0.0), writes=[Vs_b[0]])

    def load_u(cb):
        u, ub = U[cb % 2], U_b[cb % 2]
        for sq_ in range(NSEQ):
            tile = sq_ * NB + cb
            k.dma("sp", u[:, sq_, :, :], P["Ud"][tile].rearrange("g p s c -> (p s) g c"),
                  reads=[P["Udu"][tile]], writes=[ub], nowaw=True)

    def v_stage(cb):
        u, ub = U[cb % 2], U_b[cb % 2]
        V, V_b = Vs[cb % 2], Vs_b[cb % 2]
        for sq_ in range(NSEQ):
            for ri in range(2):
                for g8 in range(0, NG, 16):
                    pt, pb = C.ps.next()

                    def mmV(e, sq_=sq_, ri=ri, g8=g8, pt=pt):
                        ins = None
                        for q in range(16):
                            ins = e.matmul(pt[0:64, q * SCB:(q + 1) * SCB], Btw[:, g8 + q, ri, :], u[:, sq_, g8 + q, :],
                                           start=True, stop=True)
                        return ins
                    k.emit("pe", mmV, reads=[Btw_b, ub], writes=[pb])
                    src = pt[0:64, :16 * SCB].rearrange("p (q c) -> p q c", q=16)
                    dst = V[:, ri, sq_, g8:g8 + 16, 1:SCB + 1]
                    k.emit("act", lambda e, src=src, dst=dst: e.activation(out=dst, in_=src, func=AF.Copy), reads=[pb],
                           writes=[V_b])

    load_u(0)
    v_stage(0)
    for cb in range(NB):
        u, ub = U[cb % 2], U_b[cb % 2]
        ys, ysb = Ys[cb % 2], Ys_b[cb % 2]
        V, V_b = Vs[cb % 2], Vs_b[cb % 2]
        if cb + 1 < NB:
            load_u(cb + 1)
            v_stage(cb + 1)
        for c in range(SCB):
            for step in range(4):
                for hf_ in range(2):
                    gs = slice(hf_ * HG, (hf_ + 1) * HG)
                    a, a_b, b_, b_b = t1[hf_], t1_b[hf_], t2[hf_], t2_b[hf_]
                    if step == 0:
                        k.emit("dve", lambda e, a=a, gs=gs, c=c, V=V: e.tensor_tensor(out=a[:], in0=AR2[:, :, :, gs], in1=V[:, :, :, gs, c],
                                                                                  op=ALU.mult), reads=[AR2_b, V_b], writes=[a_b])
                    elif step == 1:
                        k.emit("dve", lambda e, b_=b_, gs=gs, c=c, V=V: e.tensor_tensor(out=b_[:], in0=ASI[:, :, :, gs],
                                                                                    in1=V[:, ::-1, :, gs, c], op=ALU.mult),
                               reads=[ASI_b, V_b], writes=[b_b])
                    elif step == 2:
                        k.emit("dve", lambda e, a=a, b_=b_: e.tensor_tensor(out=a[:], in0=a[:], in1=b_[:], op=ALU.add),
                               reads=[a_b, b_b], writes=[a_b])
                    else:
                        k.emit("dve", lambda e, a=a, gs=gs, c=c, V=V: e.tensor_tensor(out=V[:, :, :, gs, c + 1], in0=V[:, :, :, gs, c + 1],
                                                                                  in1=a[:], op=ALU.add), reads=[a_b, V_b], writes=[V_b])
        if cb + 1 < NB:
            Vn, Vn_b = Vs[(cb + 1) % 2], Vs_b[(cb + 1) % 2]
            k.emit("dve", lambda e, V=V, Vn=Vn: e.tensor_copy(out=Vn[:, :, :, :, 0:1], in_=V[:, :, :, :, SCB:SCB + 1]),
                   reads=[V_b, Vn_b], writes=[Vn_b])
        for ri in range(2):
            k.emit("act", lambda e, ri=ri, V=V: e.activation(out=Hb[:, ri], in_=V[:, ri, :, :, 0:SCB], func=AF.Copy), reads=[V_b],
                   writes=[Hb_b])
        for sq_ in range(NSEQ):
            for g8 in range(0, NG, 16):
                pt, pb = C.ps.next()

                def mmY(e, sq_=sq_, g8=g8, pt=pt, u=u):
                    ins = None
                    for q in range(16):
                        g = g8 + q
                        o = pt[:, q * SCB:(q + 1) * SCB]
                        e.matmul(o, Tw[:, g, :], u[:, sq_, g, :], start=True, stop=False)
                        e.matmul(o, Fw[:, g, 0, :], Hb[:, 0, sq_, g, :], start=False, stop=False)
                        ins = e.matmul(o, Fw[:, g, 1, :], Hb[:, 1, sq_, g, :], start=False, stop=True)
                    return ins
                k.emit("pe", mmY, reads=[Tw_b, Fw_b, ub, Hb_b], writes=[pb])
                src = pt[:, :16 * SCB].rearrange("p (q c) -> p q c", q=16)
                dst = ys[:, sq_, g8:g8 + 16, :]
                k.emit("act", lambda e, src=src, dst=dst: e.activation(out=dst, in_=src, func=AF.Copy), reads=[pb], writes=[ysb])
        for sq_ in range(NSEQ):
            tile = sq_ * NB + cb
            k.dma("sp", P["Yd"][tile].rearrange("g q t c -> (q t) g c"), ys[:, sq_, :, :],
                  reads=[ysb], writes=[P["Ydu"][tile]], kind="store")
    k.end_phase()


def s5_phase_c(k, C, P, layer):
    TOK = P["TOK"]
    NT = SNT
    k.begin_phase()
    vec, vec_b = P["vec"], P["vec_b"]
    wg = k.sbuf("sc_wglu", [128, KT, 2 * D], BF16); wg_b = Buf("sc_wglu")
    wg_cb = load_w_cols(k, P["ssm_w_glu"], wg, "sc_wglu", D, 2 * D)
    xt = [k.sbuf(f"sc_x{i}", [128, KT, NT], F32) for i in range(2)]
    xt_b = [Buf(f"sc_x{i}") for i in range(2)]
    yp = [k.sbuf(f"sc_yp{i}", [128, KT, SL, SCB], BF16) for i in range(2)]
    yp_b = [Buf(f"sc_yp{i}") for i in range(2)]
    hf = k.sbuf("sc_hf", [128, KT, NT], F32); hf_b = Buf("sc_hf")
    sqp = [(k.sbuf(f"sc_sq{i}", [128, NT], BF16), Buf(f"sc_sq{i}")) for i in range(2)]
    gls = [k.sbuf(f"sc_gl{i}", [128, KT, NT], BF16) for i in range(2)]
    gls_b = [Buf(f"sc_gl{i}") for i in range(2)]
    sg = [k.sbuf(f"sc_sg{i}", [128, NT], F32) for i in range(2)]
    sg_b = [Buf(f"sc_sg{i}") for i in range(2)]
    gcol = VCOL["norm_mix"] + layer * 8
    n_t = TOK // NT

    def ldy(tt):
        src = P["Yd"][tt].rearrange("(gh gl) q t c -> (gl q) gh (t c)", gl=8)
        k.dma("sp", yp[tt % 2][:].rearrange("p k s c -> p k (s c)"), src, reads=[P["Ydu"][tt]], writes=[yp_b[tt % 2]])

    def F(t):
        x, xb = xt[t % 2], xt_b[t % 2]
        y_, y_b = yp[t % 2], yp_b[t % 2]
        gl, gl_b = gls[t % 2], gls_b[t % 2]
        rmsnorm_T(k, C, x, xb, vec, vec_b, gcol, hf, hf_b, NT, None, None, sq_parts=sqp)
        for kt in range(KT):
            dc = VCOL["ssm_d"] + kt
            k.emit("dve", lambda e, kt=kt, dc=dc: e.scalar_tensor_tensor(
                out=hf[:, kt, :].rearrange("p (c s) -> p c s", s=SL), in0=hf[:, kt, :].rearrange("p (c s) -> p c s", s=SL),
                scalar=vec[:, dc:dc + 1], in1=y_[:, kt, :, :].rearrange("p s c -> p c s"), op0=ALU.mult, op1=ALU.add),
                reads=[hf_b, y_b, vec_b], writes=[hf_b], nosame=(kt > 0))

    def F2(t):
        gl, gl_b = gls[t % 2], gls_b[t % 2]
        k.emit("act", lambda e: e.activation(out=gl[:], in_=hf[:], func=AF.Gelu_apprx_tanh), reads=[hf_b], writes=[gl_b])

    def M(t, cts, store):
        x, xb = xt[t % 2], xt_b[t % 2]
        gl, gl_b = gls[t % 2], gls_b[t % 2]
        for ct in cts:
            pa, pab = C.ps.next()
            pb_, pbb = C.ps.next()

            def mm(e, ct=ct, pa=pa, pb_=pb_):
                ins = None
                for half, pt in ((0, pa), (1, pb_)):
                    c0 = half * D + ct * 128
                    for kt in range(KT):
                        ins = e.matmul(pt[:, :NT], wg[:, kt, c0:c0 + 128], gl[:, kt, :NT], start=(kt == 0), stop=(kt == KT - 1))
                return ins
            k.emit("pe", mm, reads=[wg_cb[ct // 4], wg_cb[2 + ct // 4], gl_b], writes=[pab, pbb])
            s_, s_b = sg[ct % 2], sg_b[ct % 2]
            k.emit("act", lambda e, pb_=pb_, s_=s_: e.activation(out=s_[:, :NT], in_=pb_[:, :NT], func=AF.Sigmoid),
                   reads=[pbb], writes=[s_b])
            k.emit("dve", lambda e, pa=pa, s_=s_: e.tensor_tensor(out=s_[:, :NT], in0=pa[:, :NT], in1=s_[:, :NT], op=ALU.mult),
                   reads=[pab, s_b], writes=[s_b])
            k.emit("pool", lambda e, ct=ct, s_=s_: e.tensor_tensor(out=x[:, ct, :NT], in0=x[:, ct, :NT], in1=s_[:, :NT], op=ALU.add),
                   reads=[s_b, xb], writes=[xb])
        if store:
            k.dma("sp", P["Xo"][:, t * NT:(t + 1) * NT].rearrange("(kt p) n -> p kt n", p=128), x[:], reads=[xb],
                  writes=P["Xou"].rng(t * NT, (t + 1) * NT), kind="store")

    _ldx(k, P, xt, xt_b, 0, NT)
    ldy(0)
    F(0)
    F2(0)
    for t in range(n_t):
        if t + 1 < n_t:
            _ldx(k, P, xt, xt_b, t + 1, NT)
            ldy(t + 1)
        M(t, range(0, 4), False)
        if t + 1 < n_t:
            F(t + 1)
        M(t, range(4, KT), True)
        if t + 1 < n_t:
            F2(t + 1)
    k.end_phase()


def mlp_phase(k, C, P, NT, layer, final=False):
    TOK = P["TOK"]
    k.begin_phase()
    vec, vec_b = P["vec"], P["vec_b"]
    mlp = MLPPhase(k, C, NT)
    mlp.load_weights(P["mlp_w_in"], P["mlp_w_out"], layer)
    xt = [k.sbuf(f"ml_x{i}", [128, KT, NT], F32) for i in range(2)]
    xt_b = [Buf(f"ml_x{i}") for i in range(2)]
    n_t = TOK // NT
    gcol = VCOL["norm_mlp"] + layer * 8

    def load(t):
        k.dma("sp", xt[t % 2][:], P["X"][:, t * NT:(t + 1) * NT].rearrange("(kt p) n -> p kt n", p=128),
              reads=P["Xu"].rng(t * NT, (t + 1) * NT), writes=[xt_b[t % 2]])

    load(0)
    mlp.front(xt[0], xt_b[0], vec, vec_b, gcol, NT)
    for t in range(n_t):
        x, xb = xt[t % 2], xt_b[t % 2]
        mlp.mlp1(NT)
        if t + 1 < n_t:
            load(t + 1)
            mlp.front(xt[(t + 1) % 2], xt_b[(t + 1) % 2], vec, vec_b, gcol, NT)
        mlp.mlp2(x, xb, NT)
        if final:
            rmsnorm_T(k, C, x, xb, vec, vec_b, VCOL["norm_final"], x, xb, NT, mlp.a[:, 0:KT, :], mlp.a_b[0:KT])
        k.dma("sp", P["Xo"][:, t * NT:(t + 1) * NT].rearrange("(kt p) n -> p kt n", p=128), x[:], reads=[xb],
              writes=P["Xou"].rng(t * NT, (t + 1) * NT), kind="store")
    k.end_phase()


WEIGHT_INPUTS = [("mlp_w_in", [4, D, DFF]), ("mlp_w_out", [4, DFF, D]), ("ssm_w_glu", [D, 2 * D]),
                 ("conv_w_pw1", [D, 2 * D]), ("conv_w_pw2", [D, D]), ("gmlp_w_in", [D, 2 * D]),
                 ("gmlp_w_out", [D, D]), ("attn_w_qkv", [D, 4608]), ("attn_w_o", [512, D])]


SMALL_INPUTS = [("gmlp_wsT", [128, 4, 128]), ("gmlp_b_s", [1, 512]), ("gmlp_ln_g", [D]), ("gmlp_ln_b", [D]),
                ("ssm_aT_re", [64, 64]), ("ssm_aT_im", [64, 64]), ("ssm_log_dt", [64]),
                ("ssm_BT_re", [64, 64, 16]), ("ssm_BT_im", [64, 64, 16]), ("ssm_CT_re", [64, 64, 16]), ("ssm_CT_im", [64, 64, 16])]


def pack_small(inp):
    o = {}
    ws = np.asarray(inp["gmlp_w_s"], np.float32).reshape(4, 128, 128)
    o["gmlp_wsT"] = np.ascontiguousarray(ws.transpose(2, 0, 1))
    o["gmlp_b_s"] = np.ascontiguousarray(np.asarray(inp["gmlp_b_s"], np.float32).reshape(1, 512))
    o["gmlp_ln_g"] = np.ascontiguousarray(np.asarray(inp["gmlp_ln_g"], np.float32).reshape(D))
    o["gmlp_ln_b"] = np.ascontiguousarray(np.asarray(inp["gmlp_ln_b"], np.float32).reshape(D))
    f = lambda n, shape: np.asarray(inp[n], np.float32).reshape(shape)
    o["ssm_aT_re"] = np.ascontiguousarray(f("ssm_a_re", (64, 64)).T)
    o["ssm_aT_im"] = np.ascontiguousarray(f("ssm_a_im", (64, 64)).T)
    o["ssm_log_dt"] = np.ascontiguousarray(f("ssm_log_dt", (64,)))
    o["ssm_BT_re"] = np.ascontiguousarray(f("ssm_b_re", (64, 64, 16)).transpose(1, 0, 2))
    o["ssm_BT_im"] = np.ascontiguousarray(f("ssm_b_im", (64, 64, 16)).transpose(1, 0, 2))
    o["ssm_CT_re"] = np.ascontiguousarray(f("ssm_c_re", (64, 16, 64)).transpose(2, 0, 1))
    o["ssm_CT_im"] = np.ascontiguousarray(f("ssm_c_im", (64, 16, 64)).transpose(2, 0, 1))
    return o


def build_program(cfg):
    NSEQ, S = cfg["NSEQ"], cfg["S"]
    TOK = NSEQ * S
    phases = cfg["phases"]
    nc = bass.Bass("TRN2", target_bir_lowering=False)
    P = {"TOK": TOK, "S": S, "NSEQ": NSEQ, "dbg": cfg.get("dbg", ())}

    def din(name, shape):
        P[name] = nc.dram_tensor(name, list(shape), F32, kind="ExternalInput").ap()
        return P[name]

    xT = din("xT", [D, TOK])
    vecs = din("vecs", [128, NV])
    consts = din("consts", [128, NCONST])
    for name, shape in WEIGHT_INPUTS:
        din(name, shape)
    for name, shape in SMALL_INPUTS:
        din(name, shape)
    out = nc.dram_tensor("out", [D, TOK], F32, kind="ExternalOutput").ap()
    xs = nc.dram_tensor("xs", [D, TOK], F32, kind="Internal").ap()
    Xu_in, Xu_s, Xu_out = DU("xin", TOK), DU("xs", TOK), DU("xout", TOK)
    P["QK"] = nc.dram_tensor("qk_s", [2 * QW, TOK], BF16, kind="Internal").ap()
    P["V"] = nc.dram_tensor("v_s", [TOK, QW], BF16, kind="Internal").ap()
    P["M"] = nc.dram_tensor("m_s", [GW, TOK], BF16, kind="Internal").ap()
    P["QKu"], P["Vu"], P["Mu"] = DU("qk", TOK), DU("v", TOK), DU("m", TOK)
    ntl = TOK // SNT
    P["Ud"] = nc.dram_tensor("ud_s", [ntl, 64, 16, SL, SCB], BF16, kind="Internal").ap()
    P["Yd"] = nc.dram_tensor("yd_s", [ntl, 64, 16, SL, SCB], BF16, kind="Internal").ap()
    P["Udu"] = [Buf(f"ud{i}") for i in range(ntl)]
    P["Ydu"] = [Buf(f"yd{i}") for i in range(ntl)]
    P["Wd_T"] = nc.dram_tensor("wd_T", [128, 64, 128], BF16, kind="Internal").ap()
    P["Wd_Bt"] = nc.dram_tensor("wd_Bt", [128, 64, 2, 64], BF16, kind="Internal").ap()
    P["Wd_F"] = nc.dram_tensor("wd_F", [64, 64, 2, 128], BF16, kind="Internal").ap()
    P["Wd_ab"] = nc.dram_tensor("wd_ab", [64, 2, 64], F32, kind="Internal").ap()
    P["Wd_b"] = Buf("wd")
    if "s5dump" in P["dbg"]:
        P["dump"] = nc.dram_tensor("dump", [128, 4096], F32, kind="ExternalOutput").ap()

    with contextlib.ExitStack() as stack:
        k = K(nc, stack)
        C = Common(k, 512)
        vec = k.sbuf("vecs_sb", [128, NV], F32); vec_b = Buf("vecs")
        k.dma("sp", vec[:], vecs, writes=[vec_b])
        P["vec"], P["vec_b"] = vec, vec_b
        for i, ph in enumerate(phases):
            first, last = (i == 0), (i == len(phases) - 1)
            P["X"], P["Xu"] = (xT, Xu_in) if first else (xs, Xu_s)
            P["Xo"], P["Xou"] = (out, Xu_out) if last else (xs, Xu_s)
            name, layer = ph
            if name == "mlp":
                mlp_phase(k, C, P, 512, layer, final=(last and cfg.get("final", False)))
            elif name == "conv":
                conv_phase(k, C, P, 256, layer)
            elif name == "gmlp":
                gmlp_phase(k, C, P, 512, layer)
            elif name == "s5":
                if cfg.get("s5_stop") == "prep":
                    s5_prep(k, C, P)
                else:
                    s5_phase_a(k, C, P, layer, with_prep=True)
                    s5_phase_b(k, C, P)
                    s5_phase_c(k, C, P, layer)
            elif name == "attn":
                attn_phase_a(k, C, P, 512, layer)
                if cfg.get("attn_stop") != "a":
                    attn_phase_b(k, C, P)
                    if cfg.get("attn_stop") != "b":
                        attn_phase_c(k, C, P, 512)
            else:
                raise ValueError(name)
        k.wait_all("sp", [b.w for b in Xu_out.b])
        k.run()
        if cfg.get("verbose"):
            print("semaphores used:", k.nsem)
    return nc


FULL_PHASES = [("s5", 0), ("mlp", 0), ("conv", 1), ("mlp", 1), ("gmlp", 2), ("mlp", 2), ("attn", 3), ("mlp", 3)]
_NC_CACHE = {}


def kernel(**inputs):
    x = np.asarray(inputs["x"], np.float32)
    B, S, Dm = x.shape
    nseq = B // NCORES
    key = (nseq, S)
    if key not in _NC_CACHE:
        _NC_CACHE[key] = build_program(dict(NSEQ=nseq, S=S, phases=FULL_PHASES, final=True))
    nc = _NC_CACHE[key]
    shared = pack_small(inputs)
    shared["vecs"] = pack_vecs(inputs)
    shared["consts"] = make_consts()
    for name, shape in WEIGHT_INPUTS:
        shared[name] = np.ascontiguousarray(np.asarray(inputs[name], np.float32).reshape(shape))
    in_maps = []
    for c in range(NCORES):
        m = dict(shared)
        xc = x[c * nseq:(c + 1) * nseq].reshape(nseq * S, Dm)
        m["xT"] = np.ascontiguousarray(xc.T)
        in_maps.append(m)
    res = run_bass_kernel_spmd(nc, in_maps, core_ids=list(range(NCORES)))
    out = np.empty((B, S, Dm), np.float32)
    for c in range(NCORES):
        out[c * nseq:(c + 1) * nseq] = np.ascontiguousarray(res.results[c]["out"].T).reshape(nseq, S, Dm)
    return out
```

```python
import contextlib
import math
import numpy as np
import concourse.bass as bass
import concourse.mybir as mybir
from concourse.bass_utils import run_bass_kernel_spmd

F32 = mybir.dt.float32
BF16 = mybir.dt.bfloat16
ALU = mybir.AluOpType
AF = mybir.ActivationFunctionType

D = 1024
KT = D // 128
DFF = 4096
FT = DFF // 128
EPS = 1e-6
NCORES = 8

ENGS = ["pe", "act", "dve", "pool", "sp"]
SAME_ENG_SYNC = True


class Buf:
    def __init__(self, name):
        self.name = name
        self.w = None
        self.r = {}
        self.dsem = None
        self.dcnt = 0
        self.ssem = None
        self.scnt = 0


class K:
    def __init__(self, nc, stack):
        self.nc = nc
        self.stack = stack
        self.ops = {e: [] for e in ENGS}
        self.sem = {}
        self.cnt = {}
        self.waited = {e: {} for e in ENGS}
        self.nsem = 0
        for e in ENGS:
            self.new_sem(e)
        self.same_eng_sync = SAME_ENG_SYNC
        self.pstack = None
        self.dma_bufs = []
        self.free_dma_sems = []
        self.deferred = None

    def _alloc_sem(self, name):
        self.nsem += 1
        return self.stack.enter_context(self.nc.semaphore(f"{name}_{self.nsem}"))

    def new_sem(self, e):
        self.sem[e] = self._alloc_sem("s_" + e)
        self.cnt[e] = 0

    def new_phase(self):
        for e in ENGS:
            if self.cnt[e] > 20000:
                self.new_sem(e)

    def sbuf(self, name, shape, dtype):
        st = self.pstack if self.pstack is not None else self.stack
        self.ntens = getattr(self, "ntens", 0) + 1
        return st.enter_context(self.nc.sbuf_tensor(f"{name}_{self.ntens}", shape, dtype))

    def barrier(self):
        toks = [(self.sem[e], self.cnt[e]) for e in ENGS if self.cnt[e] > 0]
        for b in self.dma_bufs:
            if b.dsem is not None:
                toks.append((b.dsem, b.dcnt))
            if b.ssem is not None:
                toks.append((b.ssem, b.scnt))
        for e in ENGS:
            self.wait_all(e, toks)

    def _get_dma_sem(self):
        if self.free_dma_sems:
            return self.free_dma_sems.pop()
        return self._alloc_sem("d"), 0

    def _recycle_dma_sems(self):
        for b in self.dma_bufs:
            if b.dsem is not None:
                if not getattr(b, "sw", False):
                    self.free_dma_sems.append((b.dsem, b.dcnt))
                b.dsem = None
                b.sw = False
            if b.ssem is not None:
                self.free_dma_sems.append((b.ssem, b.scnt))
                b.ssem = None
        self.dma_bufs = []

    def begin_phase(self):
        self.barrier()
        self.new_phase()
        self.pstack = contextlib.ExitStack()

    def end_phase(self):
        self.barrier()
        self.run()
        self._recycle_dma_sems()
        self.pstack.close()
        self.pstack = None

    def psum(self, name, shape, dtype):
        return self.stack.enter_context(self.nc.psum_tensor(name, shape, dtype))

    def _deps(self, reads, writes, extra, nowaw=False):
        deps = {}

        def add(tok):
            if tok is None:
                return
            k = id(tok[0])
            if k not in deps or deps[k][1] < tok[1]:
                deps[k] = tok

        for b in reads:
            add(b.w)
        for b in writes:
            if not (nowaw and b.w is not None and b.w[0] is b.dsem):
                add(b.w)
            for t in b.r.values():
                add(t)
        for t in extra:
            add(t)
        return deps

    def _waits(self, eng, deps):
        waits = []
        for k, (sem, val) in deps.items():
            if eng == "pe" and sem is self.sem["pe"]:
                continue
            if (not self.same_eng_sync) and sem is self.sem.get(eng):
                continue
            if self.waited[eng].get(k, 0) < val:
                self.waited[eng][k] = val
                waits.append((sem, val))
        return waits

    def _mark(self, tok, reads, writes):
        for b in reads:
            k = id(tok[0])
            b.r[k] = tok
        for b in writes:
            b.w = tok
            b.r = {}

    def emit(self, eng, fn, reads=(), writes=(), extra=(), nosame=False):
        if self.deferred is not None:
            self.deferred.append(lambda: self._emit(eng, fn, reads, writes, extra, nosame))
            return None
        return self._emit(eng, fn, reads, writes, extra, nosame)

    def _emit(self, eng, fn, reads=(), writes=(), extra=(), nosame=False):
        deps = self._deps(reads, writes, extra)
        if nosame:
            deps = {kk: v for kk, v in deps.items() if v[0] is not self.sem[eng]}
        waits = self._waits(eng, deps)
        self.cnt[eng] += 1
        tok = (self.sem[eng], self.cnt[eng])
        self.ops[eng].append(("op", waits, fn, self.sem[eng]))
        self._mark(tok, reads, writes)
        return tok

    def dma(self, q, out, in_, reads=(), writes=(), kind="load", extra=(), nowaw=False):
        if self.deferred is not None:
            self.deferred.append(lambda: self._dma(q, out, in_, reads, writes, kind, extra, nowaw))
            return None
        return self._dma(q, out, in_, reads, writes, kind, extra, nowaw)

    def _dma(self, q, out, in_, reads=(), writes=(), kind="load", extra=(), nowaw=False):
        deps = self._deps(reads, writes, extra, nowaw)
        waits = self._waits(q, deps)
        if kind == "load":
            b = writes[0]
            if b.dsem is None:
                if q == "pool":
                    b.dsem, b.dcnt = self._alloc_sem("dsw"), 0
                    b.sw = True
                else:
                    b.dsem, b.dcnt = self._get_dma_sem()
                self.dma_bufs.append(b)
            b.dcnt += 16
            tok = (b.dsem, b.dcnt)
        else:
            b = reads[0]
            if b.ssem is None:
                b.ssem, b.scnt = self._get_dma_sem()
                self.dma_bufs.append(b)
            b.scnt += 16
            tok = (b.ssem, b.scnt)
        self.ops[q].append(("dma", waits, (out, in_), tok[0]))
        self._mark(tok, reads, writes)
        return tok

    def wait_all(self, eng, toks):
        deps = {}
        for t in toks:
            if t is None:
                continue
            k = id(t[0])
            if k not in deps or deps[k][1] < t[1]:
                deps[k] = t
        waits = self._waits(eng, deps)
        self.ops[eng].append(("wait", waits, None, None))

    def run(self):
        nc = self.nc
        with nc.Block() as block:
            def replay(e, name):
                for kind, waits, fn, sem in self.ops[name]:
                    for (s, v) in waits:
                        e.wait_ge(s, v)
                    if kind == "op":
                        fn(e).then_inc(sem, 1)
                    elif kind == "dma":
                        e.dma_start(out=fn[0], in_=fn[1]).then_inc(sem, 16)

            @block.tensor
            def _(e):
                replay(e, "pe")

            @block.scalar
            def _(e):
                replay(e, "act")

            @block.vector
            def _(e):
                replay(e, "dve")

            @block.gpsimd
            def _(e):
                replay(e, "pool")

            @block.sync
            def _(e):
                replay(e, "sp")
        self.ops = {e: [] for e in ENGS}


class PS:
    def __init__(self, k):
        self.t = [k.psum(f"psb{i}", [128, 512], F32) for i in range(8)]
        self.b = [Buf(f"psb{i}") for i in range(8)]
        self.i = 0

    def next(self):
        i = self.i
        self.i = (self.i + 1) % 8
        return self.t[i], self.b[i]


def load_w_bf16(k, w_dram, dst, dst_buf, rows, cols, row0=0, col0=0, chunk_rt=4, q="pool"):
    rt = rows // 128
    for r in range(0, rt, chunk_rt):
        n = min(chunk_rt, rt - r)
        src = w_dram[row0 + r * 128: row0 + (r + n) * 128, col0:col0 + cols].rearrange("(t p) c -> p t c", p=128)
        k.dma(q, dst[:, r:r + n, :], src, writes=[dst_buf], nowaw=True)


def load_w_cols(k, w_dram, dst, name, rows, cols, cw=512, q="pool"):
    bufs = []
    for c0 in range(0, cols, cw):
        b = Buf(f"{name}_c{c0}")
        src = w_dram[0:rows, c0:c0 + cw].rearrange("(t p) c -> p t c", p=128)
        k.dma(q, dst[:, :, c0:c0 + cw], src, writes=[b])
        bufs.append(b)
    return bufs


class Common:
    def __init__(self, k, NT):
        self.k = k
        self.NT = NT
        self.ps = PS(k)
        self.ones_bf = k.sbuf("ones_bf", [128, 128], BF16)
        self.ones_b = Buf("ones_bf")
        k.emit("pool", lambda e: e.memset(self.ones_bf[:], 1.0), writes=[self.ones_b])
        self.rstd = k.sbuf("rstd", [128, NT], F32)
        self.rstd_b = Buf("rstd")


def rmsnorm_T(k, C, x, x_b, g, g_b, kt_g, h, h_b, n, sq, sq_bs, out_fn=None, in_fn=None, sq_parts=None):
    pt, pb = C.ps.next()
    if sq_parts is None:
        k.emit("act", lambda e: e.activation(out=sq[:, :, :n], in_=x[:, :, :n], func=AF.Square),
               reads=[x_b], writes=sq_bs)

        def mm(e):
            ins = None
            for kt in range(KT):
                ins = e.matmul(pt[:, :n], C.ones_bf[:], sq[:, kt, :n], start=(kt == 0), stop=(kt == KT - 1))
            return ins
        k.emit("pe", mm, reads=list(sq_bs) + [C.ones_b], writes=[pb])
    else:
        for kt in range(KT):
            sp_, sp_b = sq_parts[kt % len(sq_parts)]
            k.emit("act", lambda e, kt=kt, sp_=sp_: e.activation(out=sp_[:, :n], in_=x[:, kt, :n], func=AF.Square),
                   reads=[x_b], writes=[sp_b])
            k.emit("pe", lambda e, kt=kt, sp_=sp_: e.matmul(pt[:, :n], C.ones_bf[:], sp_[:, :n], start=(kt == 0),
                                                          stop=(kt == KT - 1)),
                   reads=[sp_b, C.ones_b], writes=[pb])
    k.emit("act", lambda e: e.activation(out=C.rstd[:, :n], in_=pt[:, :n], func=AF.Sqrt, scale=1.0 / D, bias=EPS),
           reads=[pb], writes=[C.rstd_b])
    k.emit("dve", lambda e: e.reciprocal(out=C.rstd[:, :n], in_=C.rstd[:, :n]), reads=[C.rstd_b], writes=[C.rstd_b])
    for kt in range(KT):
        o_ap = h[:, kt, :n] if out_fn is None else out_fn(kt)
        i0 = x[:, kt, :n] if in_fn is None else in_fn(x[:, kt, :n])
        i1 = C.rstd[:, :n] if in_fn is None else in_fn(C.rstd[:, :n])
        k.emit("dve", lambda e, kt=kt, o_ap=o_ap, i0=i0, i1=i1: e.scalar_tensor_tensor(
            out=o_ap, in0=i0, scalar=g[:, kt_g + kt:kt_g + kt + 1], in1=i1,
            op0=ALU.mult, op1=ALU.mult), reads=[x_b, g_b, C.rstd_b], writes=[h_b], nosame=(kt > 0))


class MLPPhase:
    def __init__(self, k, C, NT):
        self.k, self.C, self.NT = k, C, NT
        self.win = k.sbuf("mlp_win", [128, KT, DFF], BF16)
        self.win_b = Buf("mlp_win")
        self.wout = k.sbuf("mlp_wout", [128, FT, D], BF16)
        self.wout_b = Buf("mlp_wout")
        self.h = k.sbuf("mlp_h", [128, KT, NT], BF16)
        self.h_b = Buf("mlp_h")
        self.a = k.sbuf("mlp_a", [128, FT, NT], BF16)
        self.a_b = [Buf(f"mlp_a{i}") for i in range(FT)]
        self.r = [k.sbuf(f"mlp_r{i}", [128, NT], BF16) for i in range(2)]
        self.r_b = [Buf(f"mlp_r{i}") for i in range(2)]

    def load_weights(self, w_in, w_out, layer):
        k = self.k
        NCH = 4
        cw = DFF // NCH
        self.win_cb = [Buf(f"mlp_win_c{i}") for i in range(NCH)]
        for i in range(NCH):
            src = w_in[layer][:, i * cw:(i + 1) * cw].rearrange("(t p) c -> p t c", p=128)
            k.dma("pool", self.win[:, :, i * cw:(i + 1) * cw], src, writes=[self.win_cb[i]])
        self.ft_per_chunk = cw // 128
        load_w_bf16(k, w_out[layer], self.wout, self.wout_b, DFF, D, chunk_rt=4)

    def front(self, x, x_b, g, g_b, gcol, n):
        parts = [(self.r[i], self.r_b[i]) for i in range(2)]
        rmsnorm_T(self.k, self.C, x, x_b, g, g_b, gcol, self.h, self.h_b, n, None, None, sq_parts=parts)

    def mlp1(self, n):
        k, C = self.k, self.C
        for ft in range(FT):
            pt, pb = C.ps.next()

            def mm(e, ft=ft, pt=pt):
                ins = None
                for kt in range(KT):
                    ins = e.matmul(pt[:, :n], self.win[:, kt, ft * 128:(ft + 1) * 128], self.h[:, kt, :n],
                                   start=(kt == 0), stop=(kt == KT - 1))
                return ins
            k.emit("pe", mm, reads=[self.win_cb[ft // self.ft_per_chunk], self.h_b], writes=[pb])
            r, rb = self.r[ft % 2], self.r_b[ft % 2]
            k.emit("act", lambda e, pt=pt, r=r: e.activation(out=r[:, :n], in_=pt[:, :n], func=AF.Relu),
                   reads=[pb], writes=[rb])
            eng = "dve" if ft % 2 == 0 else "pool"
            k.emit(eng, lambda e, r=r, ft=ft: e.tensor_tensor(out=self.a[:, ft, :n], in0=r[:, :n], in1=r[:, :n],
                                                              op=ALU.mult),
                   reads=[rb], writes=[self.a_b[ft]])

    def mlp2(self, x, x_b, n):
        k, C = self.k, self.C
        for dt in range(KT):
            pt, pb = C.ps.next()

            def mm2(e, dt=dt, pt=pt):
                ins = None
                for ft in range(FT):
                    ins = e.matmul(pt[:, :n], self.wout[:, ft, dt * 128:(dt + 1) * 128], self.a[:, ft, :n],
                                   start=(ft == 0), stop=(ft == FT - 1))
                return ins
            k.emit("pe", mm2, reads=[self.wout_b] + self.a_b, writes=[pb])
            k.emit("dve", lambda e, dt=dt, pt=pt: e.tensor_tensor(out=x[:, dt, :n], in0=pt[:, :n], in1=x[:, dt, :n],
                                                                  op=ALU.add),
                   reads=[pb, x_b], writes=[x_b], nosame=(dt > 0))


VEC_SPECS = [("norm_mix", 32), ("norm_mlp", 32), ("norm_final", 8), ("ssm_d", 8), ("conv_b_pw1", 16),
             ("conv_w_dw", 31 * 8), ("conv_b_dw", 8), ("conv_ln_g", 8), ("conv_ln_b", 8), ("conv_b_pw2", 8)]
VCOL = {}
_c = 0
for _n, _w in VEC_SPECS:
    VCOL[_n] = _c
    _c += _w
NV = _c


def pack_vecs(inp):
    v = np.zeros((128, NV), np.float32)
    for name, w in VEC_SPECS:
        a = np.asarray(inp[name], np.float32).reshape(-1)
        v[:, VCOL[name]:VCOL[name] + w] = a.reshape(w, 128).T
    return v


CONST_SPECS = [("ident", 128), ("maskA", 128), ("maskB", 128), ("tril", 128), ("blkmask", 128), ("negA", 128), ("negB", 128)]
CCOL = {}
_c = 0
for _n, _w in CONST_SPECS:
    CCOL[_n] = _c
    _c += _w
NCONST = _c


def make_consts():
    c = np.zeros((128, NCONST), np.float32)
    j = np.arange(128)[:, None]
    i = np.arange(128)[None, :]
    c[:, CCOL["ident"]:CCOL["ident"] + 128] = (j == i)
    c[:, CCOL["maskA"]:CCOL["maskA"] + 128] = (j >= i)
    c[:, CCOL["maskB"]:CCOL["maskB"] + 128] = (j <= i)
    c[:, CCOL["tril"]:CCOL["tril"] + 128] = (j <= i)
    c[:, CCOL["blkmask"]:CCOL["blkmask"] + 128] = ((i % 8) >= (j % 8))
    c[:, CCOL["negA"]:CCOL["negA"] + 128] = np.where(j >= i, 0.0, -30000.0)
    c[:, CCOL["negB"]:CCOL["negB"] + 128] = np.where(j <= i, 0.0, -30000.0)
    return c


class DU:
    def __init__(self, name, ntok, unit=256):
        self.unit = unit
        self.b = [Buf(f"{name}_{i}") for i in range((ntok + unit - 1) // unit)]

    def rng(self, t0, t1):
        return self.b[t0 // self.unit:(t1 + self.unit - 1) // self.unit]


def _ldx(k, P, xt, xt_b, t, NT):
    k.dma("sp", xt[t % 2][:], P["X"][:, t * NT:(t + 1) * NT].rearrange("(kt p) n -> p kt n", p=128),
          reads=P["Xu"].rng(t * NT, (t + 1) * NT), writes=[xt_b[t % 2]])


def get_cst(k, P):
    cst = k.sbuf("consts_sb", [128, NCONST], F32)
    cst_b = Buf("consts")
    k.dma("sp", cst[:], P["consts"], writes=[cst_b])
    P["cst"], P["cst_b"] = cst, cst_b
    return cst, cst_b


def conv_phase(k, C, P, NT, layer):
    CW = 31
    TOK, S = P["TOK"], P["S"]
    k.begin_phase()
    vec, vec_b = P["vec"], P["vec_b"]
    cst, cst_b = get_cst(k, P)
    C.ident = k.sbuf("ident_bf", [128, 128], BF16); C.ident_b = Buf("ident_bf")
    k.emit("dve", lambda e: e.tensor_copy(out=C.ident[:], in_=cst[:, CCOL["ident"]:CCOL["ident"] + 128]),
           reads=[cst_b], writes=[C.ident_b])
    pw1 = k.sbuf("cv_pw1", [128, KT, 2 * D], BF16); pw1_b = Buf("cv_pw1")
    pw2 = k.sbuf("cv_pw2", [128, KT, D], BF16); pw2_b = Buf("cv_pw2")
    pw1_cb = load_w_cols(k, P["conv_w_pw1"], pw1, "cv_pw1", D, 2 * D)
    load_w_bf16(k, P["conv_w_pw2"], pw2, pw2_b, D, D, chunk_rt=4)
    diag = k.sbuf("cv_diag", [128, KT, CW, 128], BF16); diag_b = Buf("cv_diag")
    for ct in range(KT):
        for j in range(CW):
            col = VCOL["conv_w_dw"] + j * 8 + ct
            k.emit("dve", lambda e, ct=ct, j=j, col=col: e.tensor_scalar(
                out=diag[:, ct, j, :], in0=C.ident[:], scalar1=vec[:, col:col + 1], scalar2=None, op0=ALU.mult),
                reads=[C.ident_b, vec_b], writes=[diag_b])
    xt = [k.sbuf(f"cv_x{i}", [128, KT, NT], F32) for i in range(2)]
    xt_b = [Buf(f"cv_x{i}") for i in range(2)]
    h = k.sbuf("cv_h", [128, KT, NT], BF16); h_b = Buf("cv_h")
    zg = [k.sbuf(f"cv_zg{i}", [128, KT, CW - 1 + NT], BF16) for i in range(2)]
    zg_b = [[Buf(f"cv_zg{i}_{c}") for c in range(KT)] for i in range(2)]
    y = k.sbuf("cv_y", [128, KT, NT], F32); y_b = [Buf(f"cv_y{c}") for c in range(KT)]
    ybf = k.sbuf("cv_ybf", [128, KT, NT], BF16); ybf_b = Buf("cv_ybf")
    ysq = k.sbuf("cv_ysq", [128, KT, NT], BF16); ysq_b = Buf("cv_ysq")
    sg = [k.sbuf(f"cv_sg{i}", [128, NT], F32) for i in range(2)]
    sg_b = [Buf(f"cv_sg{i}") for i in range(2)]
    mean = k.sbuf("cv_mean", [128, NT], F32); mean_b = Buf("cv_mean")
    var = k.sbuf("cv_var", [128, NT], F32); var_b = Buf("cv_var")
    ntiles = TOK // NT
    tps = S // NT
    gcol = VCOL["norm_mix"] + layer * 8
    so = k.sbuf("cv_so", [128, KT, NT], BF16); so_b = Buf("cv_so")
    sqp = [(k.sbuf(f"cv_sqp{i}", [128, NT], BF16), Buf(f"cv_sqp{i}")) for i in range(2)]

    def F(t):
        rmsnorm_T(k, C, xt[t % 2], xt_b[t % 2], vec, vec_b, gcol, h, h_b, NT, None, None, sq_parts=sqp)

    def G(t):
        z, zb = zg[t % 2], zg_b[t % 2]
        zp, zpb = zg[(t + 1) % 2], zg_b[(t + 1) % 2]
        if t % tps == 0:
            k.emit("pool", lambda e: e.memset(z[:, :, 0:CW - 1], 0.0), writes=zb)
        else:
            k.emit("pool", lambda e: e.tensor_copy(out=z[:, :, 0:CW - 1], in_=zp[:, :, NT:NT + CW - 1]), reads=zpb, writes=zb)
        for ct in range(KT):
            pa, pab = C.ps.next()
            pb_, pbb = C.ps.next()

            def mm(e, ct=ct, pa=pa, pb_=pb_):
                ins = None
                for half, pt in ((0, pa), (1, pb_)):
                    c0 = half * D + ct * 128
                    for kt in range(KT):
                        ins = e.matmul(pt[:, :NT], pw1[:, kt, c0:c0 + 128], h[:, kt, :NT], start=(kt == 0),
                                       stop=(kt == KT - 1))
                return ins
            k.emit("pe", mm, reads=[pw1_cb[ct // 4], pw1_cb[2 + ct // 4], h_b], writes=[pab, pbb])
            s_, s_b = sg[ct % 2], sg_b[ct % 2]
            c2 = VCOL["conv_b_pw1"] + 8 + ct
            c1 = VCOL["conv_b_pw1"] + ct
            k.emit("act", lambda e, pb_=pb_, s_=s_, c2=c2: e.activation(out=s_[:, :NT], in_=pb_[:, :NT], func=AF.Sigmoid,
                                                                       bias=vec[:, c2:c2 + 1]),
                   reads=[pbb, vec_b], writes=[s_b])
            k.emit("dve", lambda e, pa=pa, s_=s_, c1=c1, ct=ct: e.scalar_tensor_tensor(
                out=z[:, ct, CW - 1:CW - 1 + NT], in0=pa[:, :NT], scalar=vec[:, c1:c1 + 1], in1=s_[:, :NT],
                op0=ALU.add, op1=ALU.mult), reads=[pab, s_b, vec_b], writes=[zb[ct]])

    def Cv(t):
        z, zb = zg[t % 2], zg_b[t % 2]
        for ct in range(KT):
            pt, pb = C.ps.next()

            def mmc(e, ct=ct, pt=pt):
                ins = None
                for j in range(CW):
                    ins = e.matmul(pt[:, :NT], diag[:, ct, j, :], z[:, ct, j:j + NT], start=(j == 0), stop=(j == CW - 1))
                return ins
            k.emit("pe", mmc, reads=[diag_b, zb[ct]], writes=[pb])
            cb = VCOL["conv_b_dw"] + ct
            k.emit("act", lambda e, pt=pt, ct=ct, cb=cb: e.activation(out=y[:, ct, :NT], in_=pt[:, :NT], func=AF.Identity,
                                                                     bias=vec[:, cb:cb + 1]),
                   reads=[pb, vec_b], writes=[y_b[ct]])
            k.emit("act", lambda e, pt=pt, ct=ct, cb=cb: e.activation(out=ysq[:, ct, :NT], in_=pt[:, :NT], func=AF.Square,
                                                                     bias=vec[:, cb:cb + 1]),
                   reads=[pb, vec_b], writes=[ysq_b], nosame=True)
            k.emit("dve", lambda e, ct=ct: e.tensor_copy(out=ybf[:, ct, :NT], in_=y[:, ct, :NT]),
                   reads=[y_b[ct]], writes=[ybf_b], nosame=True)

    def L(t):
        pm, pmb = C.ps.next()
        pq, pqb = C.ps.next()

        def mms(e):
            ins = None
            for src, pt in ((ybf, pm), (ysq, pq)):
                for kt in range(KT):
                    ins = e.matmul(pt[:, :NT], C.ones_bf[:], src[:, kt, :NT], start=(kt == 0), stop=(kt == KT - 1))
            return ins
        k.emit("pe", mms, reads=[ybf_b, ysq_b, C.ones_b], writes=[pmb, pqb])
        k.emit("act", lambda e: e.activation(out=mean[:, :NT], in_=pm[:, :NT], func=AF.Identity, scale=1.0 / D),
               reads=[pmb], writes=[mean_b])
        k.emit("dve", lambda e: e.tensor_tensor(out=var[:, :NT], in0=mean[:, :NT], in1=mean[:, :NT], op=ALU.mult),
               reads=[mean_b], writes=[var_b])
        k.emit("dve", lambda e: e.scalar_tensor_tensor(out=var[:, :NT], in0=pq[:, :NT], scalar=1.0 / D,
                                                       in1=var[:, :NT], op0=ALU.mult, op1=ALU.subtract),
               reads=[pqb, var_b], writes=[var_b])
        k.emit("act", lambda e: e.activation(out=var[:, :NT], in_=var[:, :NT], func=AF.Sqrt, bias=EPS),
               reads=[var_b], writes=[var_b])
        k.emit("dve", lambda e: e.reciprocal(out=var[:, :NT], in_=var[:, :NT]), reads=[var_b], writes=[var_b])
        for ct in range(KT):
            k.emit("dve", lambda e, ct=ct: e.tensor_tensor(out=y[:, ct, :NT], in0=y[:, ct, :NT], in1=mean[:, :NT],
                                                           op=ALU.subtract), reads=[y_b[ct], mean_b], writes=[y_b[ct]])
            k.emit("pool", lambda e, ct=ct: e.tensor_tensor(out=y[:, ct, :NT], in0=y[:, ct, :NT], in1=var[:, :NT],
                                                            op=ALU.mult), reads=[y_b[ct], var_b], writes=[y_b[ct]])
            cg = VCOL["conv_ln_g"] + ct
            cbb = VCOL["conv_ln_b"] + ct
            k.emit("act", lambda e, ct=ct, cg=cg, cbb=cbb: e.activation(
                out=so[:, ct, :NT], in_=y[:, ct, :NT], func=AF.Silu, scale=vec[:, cg:cg + 1], bias=vec[:, cbb:cbb + 1]),
                reads=[y_b[ct], vec_b], writes=[so_b], nosame=True)

    def W(t):
        x, xb = xt[t % 2], xt_b[t % 2]
        for dt in range(KT):
            pt, pb = C.ps.next()

            def mm2(e, dt=dt, pt=pt):
                ins = None
                for kt in range(KT):
                    ins = e.matmul(pt[:, :NT], pw2[:, kt, dt * 128:(dt + 1) * 128], so[:, kt, :NT], start=(kt == 0),
                                   stop=(kt == KT - 1))
                return ins
            k.emit("pe", mm2, reads=[pw2_b, so_b], writes=[pb])
            c3 = VCOL["conv_b_pw2"] + dt
            k.emit("dve", lambda e, dt=dt, pt=pt, c3=c3: e.scalar_tensor_tensor(
                out=x[:, dt, :NT], in0=pt[:, :NT], scalar=vec[:, c3:c3 + 1], in1=x[:, dt, :NT], op0=ALU.add, op1=ALU.add),
                reads=[pb, xb, vec_b], writes=[xb], nosame=(dt > 0))
        k.dma("sp", P["Xo"][:, t * NT:(t + 1) * NT].rearrange("(kt p) n -> p kt n", p=128), x[:], reads=[xb],
              writes=P["Xou"].rng(t * NT, (t + 1) * NT), kind="store")

    _ldx(k, P, xt, xt_b, 0, NT)
    F(0)
    G(0)
    for t in range(ntiles):
        if t + 1 < ntiles:
            _ldx(k, P, xt, xt_b, t + 1, NT)
            F(t + 1)
        Cv(t)
        L(t)
        if t + 1 < ntiles:
            G(t + 1)
        W(t)
    k.end_phase()


def gmlp_phase(k, C, P, NT, layer):
    TOK = P["TOK"]
    E = D
    k.begin_phase()
    vec, vec_b = P["vec"], P["vec_b"]
    cst, cst_b = get_cst(k, P)
    win = k.sbuf("gm_win", [128, KT, 2 * E], BF16); win_b = Buf("gm_win")
    wout = k.sbuf("gm_wout", [128, KT, D], BF16); wout_b = Buf("gm_wout")
    win_cb = load_w_cols(k, P["gmlp_w_in"], win, "gm_win", D, 2 * E)
    load_w_bf16(k, P["gmlp_w_out"], wout, wout_b, E, D, chunk_rt=4)
    wsf = k.sbuf("gm_wsf", [128, 4, 128], F32); wsf_b = Buf("gm_wsf")
    k.dma("sp", wsf[:], P["gmlp_wsT"], writes=[wsf_b])
    wsT = k.sbuf("gm_wsT", [128, 4, 128], BF16); wsT_b = Buf("gm_wsT")
    tril = cst[:, CCOL["tril"]:CCOL["tril"] + 128]
    for hh in range(4):
        k.emit("dve", lambda e, hh=hh: e.tensor_tensor(out=wsT[:, hh, :], in0=wsf[:, hh, :], in1=tril, op=ALU.mult),
               reads=[wsf_b, cst_b], writes=[wsT_b])
    bsf = k.sbuf("gm_bsf", [1, 512], F32); bsf_b = Buf("gm_bsf")
    k.dma("sp", bsf[:], P["gmlp_b_s"], writes=[bsf_b])
    bsr = k.sbuf("gm_bsr", [1, 512], BF16); bsr_b = Buf("gm_bsr")
    k.emit("dve", lambda e: e.tensor_copy(out=bsr[:], in_=bsf[:]), reads=[bsf_b], writes=[bsr_b])
    lng = k.sbuf("gm_lng", [128, E], F32); lng_b = Buf("gm_lng")
    lnb = k.sbuf("gm_lnb", [128, E], F32); lnb_b = Buf("gm_lnb")
    k.dma("sp", lng[:], P["gmlp_ln_g"].partition_broadcast(128), writes=[lng_b])
    k.dma("sp", lnb[:], P["gmlp_ln_b"].partition_broadcast(128), writes=[lnb_b])
    xt = [k.sbuf(f"gm_x{i}", [128, KT, NT], F32) for i in range(2)]
    xt_b = [Buf(f"gm_x{i}") for i in range(2)]
    h = k.sbuf("gm_h", [128, KT, NT], BF16); h_b = Buf("gm_h")
    sqp = [(k.sbuf(f"gm_sq{i}", [128, NT], BF16), Buf(f"gm_sq{i}")) for i in range(2)]
    gate = k.sbuf("gm_gate", [128, KT, NT], BF16); gate_b = Buf("gm_gate")
    us = [k.sbuf(f"gm_u{i}", [128, KT, NT], BF16) for i in range(2)]
    us_b = [Buf(f"gm_u{i}") for i in range(2)]
    vt = [k.sbuf(f"gm_vt{i}", [128, E], F32) for i in range(2)]
    vt_b = [Buf(f"gm_vt{i}") for i in range(2)]
    vln = [k.sbuf(f"gm_vln{i}", [128, E], BF16) for i in range(2)]
    vln_b = [Buf(f"gm_vln{i}") for i in range(2)]
    st = [k.sbuf(f"gm_st{i}", [128, 2, 6], F32) for i in range(2)]
    st_b = [Buf(f"gm_st{i}") for i in range(2)]
    mv = [k.sbuf(f"gm_mv{i}", [128, 2], F32) for i in range(2)]
    mv_b = [Buf(f"gm_mv{i}") for i in range(2)]
    gcol = VCOL["norm_mix"] + layer * 8
    nch = NT // 128
    n_t = TOK // NT

    def F(t):
        rmsnorm_T(k, C, xt[t % 2], xt_b[t % 2], vec, vec_b, gcol, h, h_b, NT, None, None, sq_parts=sqp)

    def U(t):
        u, u_b = us[t % 2], us_b[t % 2]
        for ft in range(KT):
            pt, pb = C.ps.next()

            def mm(e, ft=ft, pt=pt):
                ins = None
                for kt in range(KT):
                    ins = e.matmul(pt[:, :NT], win[:, kt, ft * 128:(ft + 1) * 128], h[:, kt, :NT], start=(kt == 0),
                                   stop=(kt == KT - 1))
                return ins
            k.emit("pe", mm, reads=[win_cb[ft // 4], h_b], writes=[pb])
            k.emit("act", lambda e, ft=ft, pt=pt, u=u: e.activation(out=u[:, ft, :NT], in_=pt[:, :NT], func=AF.Gelu_apprx_tanh),
                   reads=[pb], writes=[u_b], nosame=(ft > 0))

    def V(t, c):
        ci = t * nch + c
        v_, v_b = vt[ci % 2], vt_b[ci % 2]
        vl, vl_b = vln[ci % 2], vln_b[ci % 2]
        s_, s_b = st[ci % 2], st_b[ci % 2]
        m_, m_b = mv[ci % 2], mv_b[ci % 2]
        for half in range(2):
            pt, pb = C.ps.next()

            def mmv(e, half=half, pt=pt):
                ins = None
                for kt in range(KT):
                    ins = e.matmul(pt[:, :512], h[:, kt, c * 128:(c + 1) * 128],
                                   win[:, kt, E + half * 512:E + (half + 1) * 512], start=(kt == 0), stop=(kt == KT - 1))
                return ins
            k.emit("pe", mmv, reads=[win_cb[2 + half], h_b], writes=[pb])
            k.emit("act", lambda e, half=half, pt=pt: e.activation(
                out=v_[:, half * 512:(half + 1) * 512], in_=pt[:, :512], func=AF.Gelu_apprx_tanh),
                reads=[pb], writes=[v_b], nosame=(half > 0))
            k.emit("dve", lambda e, half=half: e.bn_stats(out=s_[:, half, :], in_=v_[:, half * 512:(half + 1) * 512]),
                   reads=[v_b], writes=[s_b], nosame=(half > 0))
        k.emit("dve", lambda e: e.bn_aggr(out=m_[:], in_=s_[:].rearrange("p a b -> p (a b)")), reads=[s_b], writes=[m_b])
        k.emit("act", lambda e: e.activation(out=m_[:, 1:2], in_=m_[:, 1:2], func=AF.Sqrt, bias=EPS), reads=[m_b], writes=[m_b])
        k.emit("dve", lambda e: e.reciprocal(out=m_[:, 1:2], in_=m_[:, 1:2]), reads=[m_b], writes=[m_b])
        k.emit("dve", lambda e: e.tensor_scalar(out=v_[:], in0=v_[:], scalar1=m_[:, 0:1], scalar2=m_[:, 1:2],
                                                op0=ALU.subtract, op1=ALU.mult), reads=[v_b, m_b], writes=[v_b])
        k.emit("pool", lambda e: e.tensor_tensor(out=v_[:], in0=v_[:], in1=lng[:], op=ALU.mult), reads=[v_b, lng_b], writes=[v_b])
        k.emit("pool", lambda e: e.tensor_tensor(out=vl[:], in0=v_[:], in1=lnb[:], op=ALU.add), reads=[v_b, lnb_b], writes=[vl_b])

    def S(t, c):
        ci = t * nch + c
        vl, vl_b = vln[ci % 2], vln_b[ci % 2]
        u, u_b = us[t % 2], us_b[t % 2]
        for eh in range(2):
            pt, pb = C.ps.next()

            def mms(e, eh=eh, pt=pt):
                ins = None
                for q in range(4):
                    et = eh * 4 + q
                    hd = et // 2
                    e.matmul(pt[:, q * 128:(q + 1) * 128], vl[:, et * 128:(et + 1) * 128], wsT[:, hd, :],
                             start=True, stop=False)
                    ins = e.matmul(pt[:, q * 128:(q + 1) * 128], C.ones_bf[0:1, :], bsr[0:1, hd * 128:(hd + 1) * 128],
                                   start=False, stop=True)
                return ins
            k.emit("pe", mms, reads=[vl_b, wsT_b, bsr_b, C.ones_b], writes=[pb])
            k.emit("dve", lambda e, eh=eh, pt=pt: e.tensor_tensor(
                out=gate[:, eh * 4:(eh + 1) * 4, c * 128:(c + 1) * 128],
                in0=pt[:, :512].rearrange("p (q n) -> p q n", q=4),
                in1=u[:, eh * 4:(eh + 1) * 4, c * 128:(c + 1) * 128], op=ALU.mult),
                reads=[pb, u_b], writes=[gate_b], nosame=True)

    def W(t):
        x, xb = xt[t % 2], xt_b[t % 2]
        for dt in range(KT):
            pt, pb = C.ps.next()

            def mm2(e, dt=dt, pt=pt):
                ins = None
                for kt in range(KT):
                    ins = e.matmul(pt[:, :NT], wout[:, kt, dt * 128:(dt + 1) * 128], gate[:, kt, :NT], start=(kt == 0),
                                   stop=(kt == KT - 1))
                return ins
            k.emit("pe", mm2, reads=[wout_b, gate_b], writes=[pb])
            k.emit("dve", lambda e, dt=dt, pt=pt: e.tensor_tensor(out=x[:, dt, :NT], in0=pt[:, :NT], in1=x[:, dt, :NT],
                                                                  op=ALU.add), reads=[pb, xb], writes=[xb], nosame=(dt > 0))
        k.dma("sp", P["Xo"][:, t * NT:(t + 1) * NT].rearrange("(kt p) n -> p kt n", p=128), x[:], reads=[xb],
              writes=P["Xou"].rng(t * NT, (t + 1) * NT), kind="store")

    _ldx(k, P, xt, xt_b, 0, NT)
    F(0); U(0); V(0, 0)
    for t in range(n_t):
        if t + 1 < n_t:
            _ldx(k, P, xt, xt_b, t + 1, NT)
        for c in range(1, nch):
            V(t, c)
            if c == nch - 1 and t + 1 < n_t:
                F(t + 1)
            S(t, c - 1)
        if t + 1 < n_t:
            U(t + 1)
        S(t, nch - 1)
        if t + 1 < n_t:
            V(t + 1, 0)
        W(t)
    k.end_phase()


ATT_DIL = (1, 4, 16)
GW = 512
QW = 3 * GW


def attn_phase_a(k, C, P, NT, layer):
    TOK = P["TOK"]
    k.begin_phase()
    vec, vec_b = P["vec"], P["vec_b"]
    wq = k.sbuf("at_wqkv", [128, KT, 3 * QW], BF16); wq_b = Buf("at_wqkv")
    wq_cb = load_w_cols(k, P["attn_w_qkv"], wq, "at_wqkv", D, 3 * QW)
    xt = [k.sbuf(f"at_x{i}", [128, KT, NT], F32) for i in range(2)]
    xt_b = [Buf(f"at_x{i}") for i in range(2)]
    h = k.sbuf("at_h", [128, KT, NT], BF16); h_b = Buf("at_h")
    sq = k.sbuf("at_sq", [128, KT, NT], BF16); sq_b = Buf("at_sq")
    qk = [k.sbuf(f"at_qk{i}", [128, 24, NT], BF16) for i in range(2)]
    qk_b = [Buf(f"at_qk{i}") for i in range(2)]
    vtok = [k.sbuf(f"at_vtok{i}", [128, NT // 128, QW], BF16) for i in range(2)]
    vtok_b = [Buf(f"at_vtok{i}") for i in range(2)]
    gcol = VCOL["norm_mix"] + layer * 8
    ev = 0
    for t in range(TOK // NT):
        x, xb = xt[t % 2], xt_b[t % 2]
        q_, q_b = qk[t % 2], qk_b[t % 2]
        v_, v_b = vtok[t % 2], vtok_b[t % 2]
        if t == 0:
            _ldx(k, P, xt, xt_b, 0, NT)
        if (t + 1) * NT < P["TOK"]:
            _ldx(k, P, xt, xt_b, t + 1, NT)
        rmsnorm_T(k, C, x, xb, vec, vec_b, gcol, h, h_b, NT, sq, [sq_b])
        for ft in range(24):
            pt, pb = C.ps.next()

            def mm(e, ft=ft, pt=pt):
                ins = None
                for kt in range(KT):
                    ins = e.matmul(pt[:, :NT], wq[:, kt, ft * 128:(ft + 1) * 128], h[:, kt, :NT], start=(kt == 0),
                                   stop=(kt == KT - 1))
                return ins
            k.emit("pe", mm, reads=[wq_cb[ft // 4], h_b], writes=[pb])
            ev += 1
            if ev % 2:
                k.emit("act", lambda e, ft=ft, pt=pt, q_=q_: e.activation(out=q_[:, ft, :NT], in_=pt[:, :NT], func=AF.Copy),
                       reads=[pb], writes=[q_b])
            else:
                k.emit("dve", lambda e, ft=ft, pt=pt, q_=q_: e.tensor_copy(out=q_[:, ft, :NT], in_=pt[:, :NT]),
                       reads=[pb], writes=[q_b])
        k.dma("sp", P["QK"][:, t * NT:(t + 1) * NT].rearrange("(ft p) n -> p ft n", p=128), q_[:], reads=[q_b],
              writes=P["QKu"].rng(t * NT, (t + 1) * NT), kind="store")
        for c in range(NT // 128):
            for g in range(3):
                pt, pb = C.ps.next()

                def mmv(e, g=g, pt=pt, c=c):
                    ins = None
                    for kt in range(KT):
                        ins = e.matmul(pt[:, :GW], h[:, kt, c * 128:(c + 1) * 128],
                                       wq[:, kt, 2 * QW + g * GW:2 * QW + (g + 1) * GW], start=(kt == 0), stop=(kt == KT - 1))
                    return ins
                k.emit("pe", mmv, reads=[wq_cb[6 + g], h_b], writes=[pb])
                ev += 1
                if ev % 2:
                    k.emit("act", lambda e, g=g, pt=pt, c=c, v_=v_: e.activation(out=v_[:, c, g * GW:(g + 1) * GW],
                                                                               in_=pt[:, :GW], func=AF.Copy),
                           reads=[pb], writes=[v_b])
                else:
                    k.emit("dve", lambda e, g=g, pt=pt, c=c, v_=v_: e.tensor_copy(out=v_[:, c, g * GW:(g + 1) * GW],
                                                                                in_=pt[:, :GW]),
                           reads=[pb], writes=[v_b])
        k.dma("sp", P["V"][t * NT:(t + 1) * NT, :].rearrange("(c p) f -> p c f", p=128), v_[:], reads=[v_b],
              writes=P["Vu"].rng(t * NT, (t + 1) * NT), kind="store")
    k.end_phase()


def attn_phase_b(k, C, P, embed_c=False):
    TOK, S, NSEQ = P["TOK"], P["S"], P["NSEQ"]
    NKT = S // 128
    k.begin_phase()
    cst, cst_b = get_cst(k, P)
    mask = k.sbuf("ab_negm", [128, 2, 128], BF16); mask_b = Buf("ab_negm")
    for blk, nm in ((0, "maskA"), (1, "maskB")):
        k.emit("dve", lambda e, blk=blk, nm=nm: e.tensor_copy(out=mask[:, blk, :], in_=cst[:, CCOL[nm]:CCOL[nm] + 128]),
               reads=[cst_b], writes=[mask_b])
    identb = k.sbuf("ab_ident", [128, 128], BF16); identb_b = Buf("ab_ident")
    k.emit("dve", lambda e: e.tensor_copy(out=identb[:], in_=cst[:, CCOL["ident"]:CCOL["ident"] + 128]),
           reads=[cst_b], writes=[identb_b])
    sel = k.sbuf("ab_sel", [65, 64], F32); sel_b = Buf("ab_sel")
    k.emit("pool", lambda e: e.memset(sel[:], 0.0), writes=[sel_b])
    k.emit("pool", lambda e: e.memset(sel[64:65, :], 1.0), writes=[sel_b])
    qT = [k.sbuf(f"ab_q{i}", [128, S], BF16) for i in range(2)]
    kT = [k.sbuf(f"ab_k{i}", [128, S], BF16) for i in range(2)]
    vd = [k.sbuf(f"ab_v{i}", [128, NKT, 2, 65], BF16) for i in range(2)]
    qT_b = [Buf(f"ab_q{i}") for i in range(2)]
    kT_b = [Buf(f"ab_k{i}") for i in range(2)]
    vd_b = [Buf(f"ab_v{i}") for i in range(2)]
    for i in range(2):
        k.emit("pool", lambda e, i=i: e.memset(vd[i][:, :, :, 64:65], 1.0), writes=[vd_b[i]])
    oacc = k.sbuf("ab_oacc", [65, 2, S], F32); oacc_b = Buf("ab_oacc")
    NPE = 4
    pexp = [([k.sbuf(f"ab_pexp{i}_{hh}", [128, 2, 128], BF16) for hh in range(2)],
             [Buf(f"ab_pexp{i}_{hh}") for hh in range(2)]) for i in range(NPE)]
    mT = k.sbuf("ab_mT", [64, 2, S], BF16); mT_b = Buf("ab_mT")
    rec = [k.sbuf(f"ab_rec{i}", [64, 512], F32) for i in range(2)]
    rec_b = [Buf(f"ab_rec{i}") for i in range(2)]
    it = 0
    pi = 0
    ri = 0
    iters = []

    def make_block(sq_, hp, g):
        nonlocal it
        s0 = sq_ * S
        d = ATT_DIL[g]
        q_, k_, v_ = qT[it % 2], kT[it % 2], vd[it % 2]
        q_b, k_b, v_b = qT_b[it % 2], kT_b[it % 2], vd_b[it % 2]
        it += 1
        rq = g * GW + hp * 128
        nmt = S // (128 * d)

        def loads():
            k.dma("sp", q_[:], P["QK"][rq:rq + 128, s0:s0 + S], reads=P["QKu"].rng(s0, s0 + S), writes=[q_b])
            k.dma("sp", k_[:], P["QK"][QW + rq:QW + rq + 128, s0:s0 + S], reads=P["QKu"].rng(s0, s0 + S), writes=[k_b])
            for r in range(d):
                for hh in range(2):
                    src = P["V"][s0 + r:s0 + S:d, rq + hh * 64:rq + (hh + 1) * 64].rearrange("(mt p) dd -> p mt dd", p=128)
                    k.dma("sp", v_[:, r * nmt:(r + 1) * nmt, hh, 0:64], src, reads=P["Vu"].rng(s0, s0 + S),
                          writes=[v_b], nowaw=True)
        first = True
        for r in range(d):
            for mt in range(nmt):
                tb = r * nmt + mt
                qs = slice(r + d * 128 * mt, r + d * 128 * mt + d * 127 + 1, d)
                ks_prev = slice(r + d * 128 * (mt - 1), r + d * 128 * (mt - 1) + d * 127 + 1, d)
                blks = [1] if mt == 0 else [0, 1]
                st = {}

                def front(qs=qs, ks_prev=ks_prev, blks=blks, st=st, ld=(loads if first else None)):
                    nonlocal pi
                    if ld is not None:
                        ld()
                    scs = [C.ps.next() for _ in range(2)]
                    scv = [sc[:, :256].rearrange("p (b n) -> p b n", b=2) for sc, _ in scs]

                    def mmqk(e):
                        ins = None
                        for hh in range(2):
                            for blk in blks:
                                ksl = ks_prev if blk == 0 else qs
                                ins = e.matmul(scv[hh][:, blk, :], k_[hh * 64:(hh + 1) * 64, ksl],
                                               q_[hh * 64:(hh + 1) * 64, qs], start=True, stop=True)
                        return ins
                    k.emit("pe", mmqk, reads=[q_b, k_b], writes=[scs[0][1], scs[1][1]])
                    pe_, pe_b = pexp[pi % NPE]
                    pi += 1
                    b0 = blks[0]
                    for hh in range(2):
                        k.emit("act", lambda e, hh=hh: e.activation(
                            out=pe_[hh][:, b0:2, :], in_=scv[hh][:, b0:2, :], func=AF.Exp, scale=0.125),
                            reads=[scs[hh][1]], writes=[pe_b[hh]])
                        k.emit("pool" if hh == 0 else "dve", lambda e, hh=hh: e.tensor_tensor(
                            out=pe_[hh][:, b0:2, :], in0=pe_[hh][:, b0:2, :], in1=mask[:, b0:2, :], op=ALU.mult),
                            reads=[pe_b[hh], mask_b], writes=[pe_b[hh]])
                    st["pe"] = (pe_, pe_b)

                def back(qs=qs, blks=blks, st=st, tb=tb):
                    pe_, pe_b = st["pe"]
                    po, po_b = C.ps.next()
                    pov = po[0:65, 0:256].rearrange("p (a n) -> p a n", a=2)

                    def mmpv(e):
                        ins = None
                        for hh in range(2):
                            for bi, blk in enumerate(blks):
                                vt_ = tb - 1 if blk == 0 else tb
                                ins = e.matmul(pov[:, hh, :], v_[:, vt_, hh, :], pe_[hh][:, blk, :],
                                               start=(bi == 0), stop=(bi == len(blks) - 1))
                        return ins
                    k.emit("pe", mmpv, reads=[v_b] + pe_b, writes=[po_b])
                    if g == 0:
                        k.emit("dve", lambda e: e.tensor_copy(out=oacc[:, :, qs], in_=pov), reads=[po_b], writes=[oacc_b])
                    else:
                        k.emit("dve", lambda e: e.tensor_tensor(out=oacc[:, :, qs], in0=pov, in1=oacc[:, :, qs], op=ALU.add),
                               reads=[po_b, oacc_b], writes=[oacc_b])
                iters.append([front, back, None])
                first = False

    def make_post(sq_, hp):
        s0 = sq_ * S

        def post():
            nonlocal ri
            for hh in range(2):
                for c0 in range(0, S, 512):
                    db, db_b = C.ps.next()
                    k.emit("pe", lambda e, db=db, hh=hh, c0=c0: e.matmul(db[0:64, :512], sel[:, :], oacc[:, hh, c0:c0 + 512],
                                                                        start=True, stop=True),
                           reads=[sel_b, oacc_b], writes=[db_b])
                    r_, r_b = rec[ri % 2], rec_b[ri % 2]
                    ri += 1
                    k.emit("dve", lambda e, db=db, r_=r_: e.reciprocal(out=r_[:, :], in_=db[0:64, :512]),
                           reads=[db_b], writes=[r_b])
                    k.emit("pool", lambda e, r_=r_, hh=hh, c0=c0: e.tensor_tensor(
                        out=mT[:, hh, c0:c0 + 512], in0=oacc[0:64, hh, c0:c0 + 512], in1=r_[:, :], op=ALU.mult),
                        reads=[r_b, oacc_b], writes=[mT_b])
            for hh in range(2):
                rm = (hp * 2 + hh) * 64
                k.dma("sp", P["M"][rm:rm + 64, s0:s0 + S], mT[:, hh, :], reads=[mT_b], writes=P["Mu"].rng(s0, s0 + S),
                      kind="store")
        return post

    for sq_ in range(NSEQ):
        for hp in range(4):
            for g in range(3):
                make_block(sq_, hp, g)
            iters[-1][2] = make_post(sq_, hp)
    n_it = len(iters)
    LA = 2
    c_tile = attn_phase_c(k, C, P, 512, embedded=True) if embed_c else None
    per_seq = n_it // NSEQ
    tps_c = S // 512
    pending = []
    every = max(1, per_seq // (tps_c + 1))
    for i in range(min(LA, n_it)):
        iters[i][0]()
    for i in range(n_it):
        if i + LA < n_it:
            iters[i + LA][0]()
        iters[i][1]()
        if iters[i][2] is not None:
            iters[i][2]()
        if c_tile is not None:
            if (i + 1) % per_seq == 0:
                sq_done = (i + 1) // per_seq - 1
                pending.extend(range(sq_done * tps_c, (sq_done + 1) * tps_c))
            elif pending and (i % every == every - 1):
                c_tile(pending.pop(0))
    if c_tile is not None:
        for t in pending:
            c_tile(t)
    k.end_phase()


def attn_phase_c(k, C, P, NT, embedded=False):
    TOK = P["TOK"]
    if not embedded:
        k.begin_phase()
    wo = k.sbuf("ac_wo", [128, 4, D], BF16); wo_b = Buf("ac_wo")
    load_w_bf16(k, P["attn_w_o"], wo, wo_b, GW, D, chunk_rt=4)
    xt = [k.sbuf(f"ac_x{i}", [128, KT, NT], F32) for i in range(2)]
    xt_b = [Buf(f"ac_x{i}") for i in range(2)]
    mt_ = [k.sbuf(f"ac_m{i}", [128, 4, NT], BF16) for i in range(2)]
    mt_b = [Buf(f"ac_m{i}") for i in range(2)]
    tps = P["S"] // NT

    def c_tile(t):
        x, xb = xt[t % 2], xt_b[t % 2]
        m, mb = mt_[t % 2], mt_b[t % 2]

        def _ldm(tt):
            k.dma("sp", mt_[tt % 2][:], P["M"][:, tt * NT:(tt + 1) * NT].rearrange("(kt p) n -> p kt n", p=128),
                  reads=P["Mu"].rng(tt * NT, (tt + 1) * NT), writes=[mt_b[tt % 2]])
        if t % tps == 0:
            _ldx(k, P, xt, xt_b, t, NT)
            _ldm(t)
        if (t + 1) % tps != 0:
            _ldx(k, P, xt, xt_b, t + 1, NT)
            _ldm(t + 1)
        for dt in range(KT):
            pt, pb = C.ps.next()

            def mm(e, dt=dt, pt=pt, m=m):
                ins = None
                for kt in range(4):
                    ins = e.matmul(pt[:, :NT], wo[:, kt, dt * 128:(dt + 1) * 128], m[:, kt, :NT], start=(kt == 0),
                                   stop=(kt == 3))
                return ins
            k.emit("pe", mm, reads=[wo_b, mb], writes=[pb])
            k.emit("dve", lambda e, dt=dt, pt=pt, x=x: e.tensor_tensor(out=x[:, dt, :NT], in0=pt[:, :NT], in1=x[:, dt, :NT],
                                                                      op=ALU.add), reads=[pb, xb], writes=[xb])
        k.dma("sp", P["Xo"][:, t * NT:(t + 1) * NT].rearrange("(kt p) n -> p kt n", p=128), x[:], reads=[xb],
              writes=P["Xou"].rng(t * NT, (t + 1) * NT), kind="store")

    if embedded:
        return c_tile
    for t in range(TOK // NT):
        c_tile(t)
    k.end_phase()


SL = 8
SCB = 32
SNT = SL * SCB
TWO_PI = 2.0 * math.pi


def s5_prep(k, C, P, own_phase=True):
    if own_phase:
        k.begin_phase()
    else:
        k.deferred = []
    cst, cst_b = get_cst(k, P)
    GH = 16
    f32t = lambda name, shape: (k.sbuf(name, shape, F32), Buf(name))
    aTr, aTr_b = f32t("sp_aTr", [64, 64]); aTi, aTi_b = f32t("sp_aTi", [64, 64])
    ldt, ldt_b = f32t("sp_ldt", [64, 64])
    k.dma("sp", aTr[:], P["ssm_aT_re"], writes=[aTr_b])
    k.dma("sp", aTi[:], P["ssm_aT_im"], writes=[aTi_b])
    k.dma("sp", ldt[:], P["ssm_log_dt"].partition_broadcast(64), writes=[ldt_b])
    lr, lr_b = f32t("sp_lr", [64, 64]); li, li_b = f32t("sp_li", [64, 64])
    k.emit("act", lambda e: e.activation(out=ldt[:], in_=ldt[:], func=AF.Exp), reads=[ldt_b], writes=[ldt_b])
    k.emit("dve", lambda e: e.tensor_tensor(out=lr[:], in0=aTr[:], in1=ldt[:], op=ALU.mult), reads=[aTr_b, ldt_b], writes=[lr_b])
    k.emit("dve", lambda e: e.tensor_tensor(out=li[:], in0=aTi[:], in1=ldt[:], op=ALU.mult), reads=[aTi_b, ldt_b], writes=[li_b])
    PWr, PWr_b = f32t("sp_PWr", [64, 16, 64]); PWi, PWi_b = f32t("sp_PWi", [64, 16, 64])
    KV, KV_b = f32t("sp_KV", [64, 16, 64])
    mag, mag_b = f32t("sp_mag", [64, 16, 64])
    rr, rr_b = f32t("sp_rr", [64, 16, 64]); nf, nf_b = f32t("sp_nf", [64, 16, 64]); mk, mk_b = f32t("sp_mk", [64, 16, 64])
    ni = k.sbuf("sp_ni", [64, 16, 64], mybir.dt.int32); ni_b = Buf("sp_ni")
    powers = [-(s_ + 1) for s_ in range(8)] + [t_ + 1 for t_ in range(8)]
    for idx, kk in enumerate(powers):
        k.emit("pool", lambda e, idx=idx, kk=kk: e.memset(KV[:, idx, :], float(kk)), writes=[KV_b], nosame=True)
    bc16 = lambda ap: ap.unsqueeze(1).to_broadcast([64, 16, 64])
    k.emit("dve", lambda e: e.tensor_tensor(out=mag[:], in0=KV[:], in1=bc16(lr[:]), op=ALU.mult), reads=[KV_b, lr_b], writes=[mag_b])
    k.emit("act", lambda e: e.activation(out=mag[:], in_=mag[:], func=AF.Exp), reads=[mag_b], writes=[mag_b])
    for dst, dst_b, off in ((PWi, PWi_b, 0.0), (PWr, PWr_b, 0.25)):
        k.emit("dve", lambda e: e.tensor_tensor(out=rr[:], in0=KV[:], in1=bc16(li[:]), op=ALU.mult), reads=[KV_b, li_b], writes=[rr_b])
        k.emit("dve", lambda e, off=off: e.tensor_scalar(out=rr[:], in0=rr[:], scalar1=1.0 / TWO_PI, scalar2=64.5 + off,
                                                        op0=ALU.mult, op1=ALU.add), reads=[rr_b], writes=[rr_b])
        k.emit("dve", lambda e: e.tensor_copy(out=ni[:], in_=rr[:]), reads=[rr_b], writes=[ni_b])
        k.emit("dve", lambda e: e.tensor_copy(out=nf[:], in_=ni[:]), reads=[ni_b], writes=[nf_b])
        k.emit("dve", lambda e: e.tensor_tensor(out=rr[:], in0=rr[:], in1=nf[:], op=ALU.subtract), reads=[rr_b, nf_b], writes=[rr_b])
        k.emit("dve", lambda e: e.tensor_single_scalar(out=mk[:], in_=rr[:], scalar=0.0, op=ALU.is_lt), reads=[rr_b], writes=[mk_b])
        k.emit("dve", lambda e: e.tensor_tensor(out=rr[:], in0=rr[:], in1=mk[:], op=ALU.add), reads=[rr_b, mk_b], writes=[rr_b])
        k.emit("dve", lambda e: e.tensor_single_scalar(out=mk[:], in_=rr[:], scalar=1.0, op=ALU.is_ge), reads=[rr_b], writes=[mk_b])
        k.emit("dve", lambda e: e.tensor_tensor(out=rr[:], in0=rr[:], in1=mk[:], op=ALU.subtract), reads=[rr_b, mk_b], writes=[rr_b])
        k.emit("act", lambda e: e.activation(out=nf[:], in_=rr[:], func=AF.Sin, scale=TWO_PI, bias=-math.pi),
               reads=[rr_b], writes=[nf_b])
        k.emit("dve", lambda e, dst=dst: e.tensor_tensor(out=dst[:], in0=nf[:], in1=mag[:], op=ALU.mult),
               reads=[nf_b, mag_b], writes=[dst_b])
    gr, gr_b = f32t("sp_gr", [64, 64]); gi, gi_b = f32t("sp_gi", [64, 64])
    den, den_b = f32t("sp_den", [64, 64]); t1, t1_b = f32t("sp_t1", [64, 64]); t2, t2_b = f32t("sp_t2", [64, 64])
    am1, am1_b = f32t("sp_am1", [64, 64])
    K1 = 8
    dv = lambda fn, reads, writes: k.emit("dve", fn, reads=reads, writes=writes)
    dv(lambda e: e.tensor_scalar_add(out=am1[:], in0=PWr[:, K1, :], scalar1=-1.0), [PWr_b], [am1_b])
    dv(lambda e: e.tensor_tensor(out=den[:], in0=aTr[:], in1=aTr[:], op=ALU.mult), [aTr_b], [den_b])
    dv(lambda e: e.tensor_tensor(out=t1[:], in0=aTi[:], in1=aTi[:], op=ALU.mult), [aTi_b], [t1_b])
    dv(lambda e: e.tensor_tensor(out=den[:], in0=den[:], in1=t1[:], op=ALU.add), [den_b, t1_b], [den_b])
    dv(lambda e: e.reciprocal(out=den[:], in_=den[:]), [den_b], [den_b])
    dv(lambda e: e.tensor_tensor(out=t1[:], in0=am1[:], in1=aTr[:], op=ALU.mult), [am1_b, aTr_b], [t1_b])
    dv(lambda e: e.tensor_tensor(out=t2[:], in0=PWi[:, K1, :], in1=aTi[:], op=ALU.mult), [PWi_b, aTi_b], [t2_b])
    dv(lambda e: e.tensor_tensor(out=t1[:], in0=t1[:], in1=t2[:], op=ALU.add), [t1_b, t2_b], [t1_b])
    dv(lambda e: e.tensor_tensor(out=gr[:], in0=t1[:], in1=den[:], op=ALU.mult), [t1_b, den_b], [gr_b])
    dv(lambda e: e.tensor_tensor(out=t1[:], in0=PWi[:, K1, :], in1=aTr[:], op=ALU.mult), [PWi_b, aTr_b], [t1_b])
    dv(lambda e: e.tensor_tensor(out=t2[:], in0=am1[:], in1=aTi[:], op=ALU.mult), [am1_b, aTi_b], [t2_b])
    dv(lambda e: e.tensor_tensor(out=t1[:], in0=t1[:], in1=t2[:], op=ALU.subtract), [t1_b, t2_b], [t1_b])
    dv(lambda e: e.tensor_tensor(out=gi[:], in0=t1[:], in1=den[:], op=ALU.mult), [t1_b, den_b], [gi_b])
    ab8, ab8_b = f32t("sp_ab8", [64, 2, 64])
    dv(lambda e: e.tensor_copy(out=ab8[:, 0, :], in_=PWr[:, 15, :]), [PWr_b], [ab8_b])
    dv(lambda e: e.tensor_copy(out=ab8[:, 1, :], in_=PWi[:, 15, :]), [PWi_b], [ab8_b])
    k.dma("sp", P["Wd_ab"], ab8[:], reads=[ab8_b], writes=[P["Wd_b"]], kind="store")
    if "s5dump" in P["dbg"]:
        k.dma("sp", P["dump"][0:64, 0:1024], PWr[:].rearrange("p a b -> p (a b)"), reads=[PWr_b], writes=[P["Wd_b"]], kind="store")
        k.dma("sp", P["dump"][0:64, 1024:2048], PWi[:].rearrange("p a b -> p (a b)"), reads=[PWi_b], writes=[P["Wd_b"]], kind="store")
        k.dma("sp", P["dump"][0:64, 2048:2112], gr[:], reads=[gr_b], writes=[P["Wd_b"]], kind="store")
        k.dma("sp", P["dump"][0:64, 2112:2176], gi[:], reads=[gi_b], writes=[P["Wd_b"]], kind="store")
    Bre, Bre_b = f32t("sp_Bre", [64, GH, 16]); Bim, Bim_b = f32t("sp_Bim", [64, GH, 16])
    Cre, Cre_b = f32t("sp_Cre", [64, GH, 16]); Cim, Cim_b = f32t("sp_Cim", [64, GH, 16])
    Gre, Gre_b = f32t("sp_Gre", [64, GH, 16]); Gim, Gim_b = f32t("sp_Gim", [64, GH, 16])
    tA, tA_b = f32t("sp_tA", [64, GH, 16]); tB, tB_b = f32t("sp_tB", [64, GH, 16])
    Ere, Ere_b = f32t("sp_Ere", [64, GH, 16, 8]); Eim, Eim_b = f32t("sp_Eim", [64, GH, 16, 8])
    Fre, Fre_b = f32t("sp_Fre", [64, GH, 16, 8]); Fmi, Fmi_b = f32t("sp_Fmi", [64, GH, 16, 8])
    Qre, Qre_b = f32t("sp_Qre", [64, GH, 128]); Qim, Qim_b = f32t("sp_Qim", [64, GH, 128])
    tQ, tQ_b = f32t("sp_tQ", [64, GH, 128])
    T_sb = k.sbuf("sp_Tsb", [128, GH, 128], BF16); T_sbb = Buf("sp_Tsb")
    Bt_sb = k.sbuf("sp_Btsb", [128, GH, 2, 64], BF16); Bt_sbb = Buf("sp_Btsb")
    F_sb = k.sbuf("sp_Fsb", [64, GH, 2, 128], BF16); F_sbb = Buf("sp_Fsb")
    blkmask = cst[:, CCOL["blkmask"]:CCOL["blkmask"] + 128]
    ident64 = cst[0:64, CCOL["ident"]:CCOL["ident"] + 64]

    def bc(ap2d, n):
        return ap2d.unsqueeze(2).to_broadcast([64, GH, n])

    def cmul(out_re, out_re_b, out_im, out_im_b, x_re, x_im, x_bs, w_re, w_im, w_bs, n, neg_im=False):
        dv(lambda e: e.tensor_tensor(out=tA[:, :, :n] if n <= 16 else tQ[:], in0=x_re, in1=bc(w_re, n), op=ALU.mult),
           x_bs + w_bs, [tA_b if n <= 16 else tQ_b])
        sc1 = tA[:, :, :n] if n <= 16 else tQ[:]
        sc1_b = tA_b if n <= 16 else tQ_b
        dv(lambda e: e.tensor_tensor(out=out_re, in0=x_im, in1=bc(w_im, n), op=ALU.mult), x_bs + w_bs, [out_re_b])
        dv(lambda e: e.tensor_tensor(out=out_re, in0=sc1, in1=out_re, op=ALU.subtract), [sc1_b, out_re_b], [out_re_b])
        dv(lambda e: e.tensor_tensor(out=sc1, in0=x_re, in1=bc(w_im, n), op=ALU.mult), x_bs + w_bs, [sc1_b])
        dv(lambda e: e.tensor_tensor(out=out_im, in0=x_im, in1=bc(w_re, n), op=ALU.mult), x_bs + w_bs, [out_im_b])
        if neg_im:
            dv(lambda e: e.scalar_tensor_tensor(out=out_im, in0=sc1, scalar=-1.0, in1=out_im, op0=ALU.mult, op1=ALU.subtract),
               [sc1_b, out_im_b], [out_im_b])
        else:
            dv(lambda e: e.tensor_tensor(out=out_im, in0=sc1, in1=out_im, op=ALU.add), [sc1_b, out_im_b], [out_im_b])

    t4, t4_b = f32t("sp_t4", [64, GH, 16, 8])

    def cmul4(o_re, o_re_b, o_im, o_im_b, x_re, x_im, x_bs, k0, g0, neg_im):
        xr = x_re[:].unsqueeze(3).to_broadcast([64, GH, 16, 8])
        xi = x_im[:].unsqueeze(3).to_broadcast([64, GH, 16, 8])
        wr = PWr[:, k0:k0 + 8, g0:g0 + GH].rearrange("p s g -> p g s").unsqueeze(2).to_broadcast([64, GH, 16, 8])
        wi = PWi[:, k0:k0 + 8, g0:g0 + GH].rearrange("p s g -> p g s").unsqueeze(2).to_broadcast([64, GH, 16, 8])
        wb = [PWr_b, PWi_b]
        dv(lambda e: e.tensor_tensor(out=t4[:], in0=xr, in1=wr, op=ALU.mult), x_bs + wb, [t4_b])
        dv(lambda e: e.tensor_tensor(out=o_re[:], in0=xi, in1=wi, op=ALU.mult), x_bs + wb, [o_re_b])
        dv(lambda e: e.tensor_tensor(out=o_re[:], in0=t4[:], in1=o_re[:], op=ALU.subtract), [t4_b, o_re_b], [o_re_b])
        dv(lambda e: e.tensor_tensor(out=t4[:], in0=xr, in1=wi, op=ALU.mult), x_bs + wb, [t4_b])
        dv(lambda e: e.tensor_tensor(out=o_im[:], in0=xi, in1=wr, op=ALU.mult), x_bs + wb, [o_im_b])
        if neg_im:
            dv(lambda e: e.scalar_tensor_tensor(out=o_im[:], in0=t4[:], scalar=-1.0, in1=o_im[:], op0=ALU.mult, op1=ALU.subtract),
               [t4_b, o_im_b], [o_im_b])
        else:
            dv(lambda e: e.tensor_tensor(out=o_im[:], in0=t4[:], in1=o_im[:], op=ALU.add), [t4_b, o_im_b], [o_im_b])

    for gh in range(64 // GH):
        g0 = gh * GH
        k.dma("sp", Bre[:], P["ssm_BT_re"][:, g0:g0 + GH, :], writes=[Bre_b])
        k.dma("sp", Bim[:], P["ssm_BT_im"][:, g0:g0 + GH, :], writes=[Bim_b])
        k.dma("sp", Cre[:], P["ssm_CT_re"][:, g0:g0 + GH, :], writes=[Cre_b])
        k.dma("sp", Cim[:], P["ssm_CT_im"][:, g0:g0 + GH, :], writes=[Cim_b])
        cmul(Gre[:], Gre_b, Gim[:], Gim_b, Bre[:], Bim[:], [Bre_b, Bim_b], gr[:, g0:g0 + GH], gi[:, g0:g0 + GH], [gr_b, gi_b], 16)
        cmul4(Ere, Ere_b, Eim, Eim_b, Gre, Gim, [Gre_b, Gim_b], 0, g0, False)
        cmul4(Fre, Fre_b, Fmi, Fmi_b, Cre, Cim, [Cre_b, Cim_b], 8, g0, True)
        Ere2 = Ere[:].rearrange("p g a b -> p g (a b)")
        Eim2 = Eim[:].rearrange("p g a b -> p g (a b)")
        Fre2 = Fre[:].rearrange("p g a b -> p g (a b)")
        Fmi2 = Fmi[:].rearrange("p g a b -> p g (a b)")
        cmul(Qre[:], Qre_b, Qim[:], Qim_b, Ere2, Eim2, [Ere_b, Eim_b], PWr[:, 15, g0:g0 + GH], PWi[:, 15, g0:g0 + GH],
             [PWr_b, PWi_b], 128)
        for g4 in range(0, GH, 4):
            pt, pb = C.ps.next()

            def mmT(e, g4=g4, pt=pt):
                ins = None
                for q in range(4):
                    e.matmul(pt[:, q * 128:(q + 1) * 128], Ere2[:, g4 + q, :], Fre2[:, g4 + q, :], start=True, stop=False)
                    ins = e.matmul(pt[:, q * 128:(q + 1) * 128], Eim2[:, g4 + q, :], Fmi2[:, g4 + q, :], start=False, stop=True)
                return ins
            k.emit("pe", mmT, reads=[Ere_b, Eim_b, Fre_b, Fmi_b], writes=[pb])
            dv(lambda e, g4=g4, pt=pt: e.tensor_tensor(out=T_sb[:, g4:g4 + 4, :], in0=pt[:, :512].rearrange("p (q n) -> p q n", q=4),
                                                        in1=blkmask.unsqueeze(1).to_broadcast([128, 4, 128]), op=ALU.mult),
               [pb, cst_b], [T_sbb])
        for ri, Q in ((0, Qre), (1, Qim)):
            for g8 in range(0, GH, 8):
                pt, pb = C.ps.next()

                def mmB(e, g8=g8, pt=pt, Q=Q):
                    ins = None
                    for q in range(8):
                        ins = e.matmul(pt[:, q * 64:(q + 1) * 64], Q[:, g8 + q, :], ident64, start=True, stop=True)
                    return ins
                k.emit("pe", mmB, reads=[Qre_b, Qim_b, cst_b], writes=[pb])
                k.emit("act", lambda e, g8=g8, pt=pt, ri=ri: e.activation(
                    out=Bt_sb[:, g8:g8 + 8, ri, :], in_=pt[:, :512].rearrange("p (q n) -> p q n", q=8), func=AF.Copy),
                    reads=[pb], writes=[Bt_sbb])
        k.emit("act", lambda e: e.activation(out=F_sb[:, :, 0, :], in_=Fre2, func=AF.Copy), reads=[Fre_b], writes=[F_sbb])
        k.emit("act", lambda e: e.activation(out=F_sb[:, :, 1, :], in_=Fmi2, func=AF.Copy), reads=[Fmi_b], writes=[F_sbb])
        k.dma("sp", P["Wd_T"][:, g0:g0 + GH, :], T_sb[:], reads=[T_sbb], writes=[P["Wd_b"]], kind="store")
        k.dma("sp", P["Wd_Bt"][:, g0:g0 + GH, :, :], Bt_sb[:], reads=[Bt_sbb], writes=[P["Wd_b"]], kind="store")
        k.dma("sp", P["Wd_F"][:, g0:g0 + GH, :, :], F_sb[:], reads=[F_sbb], writes=[P["Wd_b"]], kind="store")
    if own_phase:
        k.end_phase()
        return None
    thunks = k.deferred
    k.deferred = None
    return thunks


def s5_phase_a(k, C, P, layer, with_prep=False):
    TOK = P["TOK"]
    NT = SNT
    k.begin_phase()
    thunks = s5_prep(k, C, P, own_phase=False) if with_prep else []
    n_tiles = TOK // NT
    per_tile = (len(thunks) + n_tiles - 1) // n_tiles if thunks else 0
    vec, vec_b = P["vec"], P["vec_b"]
    xt = [k.sbuf(f"sa_x{i}", [128, KT, NT], F32) for i in range(2)]
    xt_b = [Buf(f"sa_x{i}") for i in range(2)]
    hp = [k.sbuf(f"sa_hp{i}", [128, KT, SL, SCB], BF16) for i in range(2)]
    hp_b = [Buf(f"sa_hp{i}") for i in range(2)]
    sq = k.sbuf("sa_sq", [128, KT, NT], BF16); sq_b = Buf("sa_sq")
    gcol = VCOL["norm_mix"] + layer * 8
    for t in range(TOK // NT):
        x, xb = xt[t % 2], xt_b[t % 2]
        h, hb = hp[t % 2], hp_b[t % 2]
        if t == 0:
            _ldx(k, P, xt, xt_b, 0, NT)
        if (t + 1) * NT < P["TOK"]:
            _ldx(k, P, xt, xt_b, t + 1, NT)
        hview = h[:].rearrange("p k s c -> p k (s c)")
        rmsnorm_T(k, C, x, xb, vec, vec_b, gcol, None, hb, NT, sq, [sq_b],
                  out_fn=lambda kt, h=h: h[:, kt, :, :].rearrange("p s c -> p c s"),
                  in_fn=lambda ap: ap.rearrange("p (c s) -> p c s", s=SL))
        dst = P["Ud"][t].rearrange("(gh gl) p s c -> (gl p) gh (s c)", gl=8)
        k.dma("sp", dst, h[:].rearrange("p k s c -> p k (s c)"), reads=[hb], writes=[P["Udu"][t]], kind="store")
        for th in thunks[t * per_tile:(t + 1) * per_tile]:
            th()
    for th in thunks[n_tiles * per_tile:]:
        th()
    k.end_phase()


def s5_phase_b(k, C, P):
    TOK, S, NSEQ = P["TOK"], P["S"], P["NSEQ"]
    NB = S // SNT
    NG = 64
    k.begin_phase()
    Tw = k.sbuf("sb_T", [128, NG, 128], BF16); Tw_b = Buf("sb_T")
    Btw = k.sbuf("sb_Bt", [128, NG, 2, 64], BF16); Btw_b = Buf("sb_Bt")
    Fw = k.sbuf("sb_F", [64, NG, 2, 128], BF16); Fw_b = Buf("sb_F")
    ab = k.sbuf("sb_ab", [64, 2, NG], F32); ab_b = Buf("sb_ab")
    k.dma("sp", Tw[:], P["Wd_T"], reads=[P["Wd_b"]], writes=[Tw_b])
    k.dma("sp", Btw[:], P["Wd_Bt"], reads=[P["Wd_b"]], writes=[Btw_b])
    k.dma("sp", Fw[:], P["Wd_F"], reads=[P["Wd_b"]], writes=[Fw_b])
    k.dma("sp", ab[:], P["Wd_ab"], reads=[P["Wd_b"]], writes=[ab_b])
    W = NSEQ * NG
    AR2 = k.sbuf("sb_AR2", [64, 2, NSEQ, NG], F32); AR2_b = Buf("sb_AR2")
    ASI = k.sbuf("sb_ASI", [64, 2, NSEQ, NG], F32); ASI_b = Buf("sb_ASI")
    for sq_ in range(NSEQ):
        for ri in range(2):
            k.emit("dve", lambda e, sq_=sq_, ri=ri: e.tensor_copy(out=AR2[:, ri, sq_, :], in_=ab[:, 0, :]), reads=[ab_b], writes=[AR2_b])
        k.emit("dve", lambda e, sq_=sq_: e.tensor_scalar(out=ASI[:, 0, sq_, :], in0=ab[:, 1, :], scalar1=-1.0, scalar2=None,
                                                        op0=ALU.mult), reads=[ab_b], writes=[ASI_b])
        k.emit("dve", lambda e, sq_=sq_: e.tensor_copy(out=ASI[:, 1, sq_, :], in_=ab[:, 1, :]), reads=[ab_b], writes=[ASI_b])
    U = [k.sbuf(f"sb_U{i}", [128, NSEQ, NG, SCB], BF16) for i in range(2)]
    U_b = [Buf(f"sb_U{i}") for i in range(2)]
    Vs = [k.sbuf(f"sb_V{i}", [64, 2, NSEQ, NG, SCB + 1], F32) for i in range(2)]
    Vs_b = [Buf(f"sb_V{i}") for i in range(2)]
    Hb = k.sbuf("sb_Hb", [64, 2, NSEQ, NG, SCB], BF16); Hb_b = Buf("sb_Hb")
    Ys = [k.sbuf(f"sb_Ys{i}", [128, NSEQ, NG, SCB], BF16) for i in range(2)]
    Ys_b = [Buf(f"sb_Ys{i}") for i in range(2)]
    HG = NG // 2
    t1 = [k.sbuf(f"sb_t1{i}", [64, 2, NSEQ, HG], F32) for i in range(2)]
    t2 = [k.sbuf(f"sb_t2{i}", [64, 2, NSEQ, HG], F32) for i in range(2)]
    t1_b = [Buf(f"sb_t1{i}") for i in range(2)]
    t2_b = [Buf(f"sb_t2{i}") for i in range(2)]
    k.emit("pool", lambda e: e.memset(Vs[0][:, :, :, :, 0:1], 0.0), writes=[Vs_b[0]])

    def load_u(cb):
        u, ub = U[cb % 2], U_b[cb % 2]
        for sq_ in range(NSEQ):
            tile = sq_ * NB + cb
            k.dma("sp", u[:, sq_, :, :], P["Ud"][tile].rearrange("g p s c -> (p s) g c"),
                  reads=[P["Udu"][tile]], writes=[ub], nowaw=True)

    def v_stage(cb):
        u, ub = U[cb % 2], U_b[cb % 2]
        V, V_b = Vs[cb % 2], Vs_b[cb % 2]
        for sq_ in range(NSEQ):
            for ri in range(2):
                for g8 in range(0, NG, 16):
                    pt, pb = C.ps.next()

                    def mmV(e, sq_=sq_, ri=ri, g8=g8, pt=pt):
                        ins = None
                        for q in range(16):
                            ins = e.matmul(pt[0:64, q * SCB:(q + 1) * SCB], Btw[:, g8 + q, ri, :], u[:, sq_, g8 + q, :],
                                           start=True, stop=True)
                        return ins
                    k.emit("pe", mmV, reads=[Btw_b, ub], writes=[pb])
                    src = pt[0:64, :16 * SCB].rearrange("p (q c) -> p q c", q=16)
                    dst = V[:, ri, sq_, g8:g8 + 16, 1:SCB + 1]
                    k.emit("act", lambda e, src=src, dst=dst: e.activation(out=dst, in_=src, func=AF.Copy), reads=[pb],
                           writes=[V_b])

    load_u(0)
    v_stage(0)
    for cb in range(NB):
        u, ub = U[cb % 2], U_b[cb % 2]
        ys, ysb = Ys[cb % 2], Ys_b[cb % 2]
        V, V_b = Vs[cb % 2], Vs_b[cb % 2]
        if cb + 1 < NB:
            load_u(cb + 1)
            v_stage(cb + 1)
        first = True
        for c in range(SCB):
            for step in range(5):
                for hf_ in range(2):
                    gs = slice(hf_ * HG, (hf_ + 1) * HG)
                    a, a_b, b_, b_b = t1[hf_], t1_b[hf_], t2[hf_], t2_b[hf_]
                    ns = False
                    if step == 0:
                        k.emit("dve", lambda e, a=a, gs=gs, c=c, V=V: e.tensor_tensor(out=a[:], in0=AR2[:, :, :, gs], in1=V[:, :, :, gs, c],
                                                                             op=ALU.mult), reads=[AR2_b, V_b], writes=[a_b], nosame=ns)
                    elif step == 1:
                        k.emit("dve", lambda e, b_=b_, gs=gs, c=c, V=V: e.tensor_tensor(out=b_[:, 0], in0=ASI[:, 0, :, gs], in1=V[:, 1, :, gs, c],
                                                                               op=ALU.mult), reads=[ASI_b, V_b], writes=[b_b], nosame=ns)
                    elif step == 2:
                        k.emit("dve", lambda e, b_=b_, gs=gs, c=c, V=V: e.tensor_tensor(out=b_[:, 1], in0=ASI[:, 1, :, gs], in1=V[:, 0, :, gs, c],
                                                                               op=ALU.mult), reads=[ASI_b, V_b], writes=[b_b], nosame=ns)
                    elif step == 3:
                        k.emit("dve", lambda e, a=a, b_=b_: e.tensor_tensor(out=a[:], in0=a[:], in1=b_[:], op=ALU.add),
                               reads=[a_b, b_b], writes=[a_b], nosame=ns)
                    else:
                        k.emit("dve", lambda e, a=a, gs=gs, c=c, V=V: e.tensor_tensor(out=V[:, :, :, gs, c + 1], in0=V[:, :, :, gs, c + 1], in1=a[:],
                                                                             op=ALU.add), reads=[a_b, V_b], writes=[V_b], nosame=ns)
                    if step == 0 and hf_ == 1:
                        first = False
        if cb + 1 < NB:
            Vn, Vn_b = Vs[(cb + 1) % 2], Vs_b[(cb + 1) % 2]
            k.emit("dve", lambda e, V=V, Vn=Vn: e.tensor_copy(out=Vn[:, :, :, :, 0:1], in_=V[:, :, :, :, SCB:SCB + 1]),
                   reads=[V_b, Vn_b], writes=[Vn_b])
        for ri in range(2):
            k.emit("act", lambda e, ri=ri, V=V: e.activation(out=Hb[:, ri], in_=V[:, ri, :, :, 0:SCB], func=AF.Copy), reads=[V_b],
                   writes=[Hb_b])
        for sq_ in range(NSEQ):
            for g8 in range(0, NG, 16):
                pt, pb = C.ps.next()

                def mmY(e, sq_=sq_, g8=g8, pt=pt, u=u):
                    ins = None
                    for q in range(16):
                        g = g8 + q
                        o = pt[:, q * SCB:(q + 1) * SCB]
                        e.matmul(o, Tw[:, g, :], u[:, sq_, g, :], start=True, stop=False)
                        e.matmul(o, Fw[:, g, 0, :], Hb[:, 0, sq_, g, :], start=False, stop=False)
                        ins = e.matmul(o, Fw[:, g, 1, :], Hb[:, 1, sq_, g, :], start=False, stop=True)
                    return ins
                k.emit("pe", mmY, reads=[Tw_b, Fw_b, ub, Hb_b], writes=[pb])
                src = pt[:, :16 * SCB].rearrange("p (q c) -> p q c", q=16)
                dst = ys[:, sq_, g8:g8 + 16, :]
                k.emit("act", lambda e, src=src, dst=dst: e.activation(out=dst, in_=src, func=AF.Copy), reads=[pb], writes=[ysb])
        for sq_ in range(NSEQ):
            tile = sq_ * NB + cb
            k.dma("sp", P["Yd"][tile].rearrange("g q t c -> (q t) g c"), ys[:, sq_, :, :],
                  reads=[ysb], writes=[P["Ydu"][tile]], kind="store")
    k.end_phase()


def s5_phase_c(k, C, P, layer):
    TOK = P["TOK"]
    NT = SNT
    k.begin_phase()
    vec, vec_b = P["vec"], P["vec_b"]
    wg = k.sbuf("sc_wglu", [128, KT, 2 * D], BF16); wg_b = Buf("sc_wglu")
    wg_cb = load_w_cols(k, P["ssm_w_glu"], wg, "sc_wglu", D, 2 * D)
    xt = [k.sbuf(f"sc_x{i}", [128, KT, NT], F32) for i in range(2)]
    xt_b = [Buf(f"sc_x{i}") for i in range(2)]
    yp = [k.sbuf(f"sc_yp{i}", [128, KT, SL, SCB], BF16) for i in range(2)]
    yp_b = [Buf(f"sc_yp{i}") for i in range(2)]
    hf = k.sbuf("sc_hf", [128, KT, NT], F32); hf_b = Buf("sc_hf")
    sqp = [(k.sbuf(f"sc_sq{i}", [128, NT], BF16), Buf(f"sc_sq{i}")) for i in range(2)]
    gls = [k.sbuf(f"sc_gl{i}", [128, KT, NT], BF16) for i in range(2)]
    gls_b = [Buf(f"sc_gl{i}") for i in range(2)]
    sg = [k.sbuf(f"sc_sg{i}", [128, NT], F32) for i in range(2)]
    sg_b = [Buf(f"sc_sg{i}") for i in range(2)]
    gcol = VCOL["norm_mix"] + layer * 8
    n_t = TOK // NT

    def ldy(tt):
        src = P["Yd"][tt].rearrange("(gh gl) q t c -> (gl q) gh (t c)", gl=8)
        k.dma("sp", yp[tt % 2][:].rearrange("p k s c -> p k (s c)"), src, reads=[P["Ydu"][tt]], writes=[yp_b[tt % 2]])

    def F(t):
        x, xb = xt[t % 2], xt_b[t % 2]
        y_, y_b = yp[t % 2], yp_b[t % 2]
        gl, gl_b = gls[t % 2], gls_b[t % 2]
        rmsnorm_T(k, C, x, xb, vec, vec_b, gcol, hf, hf_b, NT, None, None, sq_parts=sqp)
        for kt in range(KT):
            dc = VCOL["ssm_d"] + kt
            k.emit("dve", lambda e, kt=kt, dc=dc: e.scalar_tensor_tensor(
                out=hf[:, kt, :].rearrange("p (c s) -> p c s", s=SL), in0=hf[:, kt, :].rearrange("p (c s) -> p c s", s=SL),
                scalar=vec[:, dc:dc + 1], in1=y_[:, kt, :, :].rearrange("p s c -> p c s"), op0=ALU.mult, op1=ALU.add),
                reads=[hf_b, y_b, vec_b], writes=[hf_b], nosame=(kt > 0))

    def F2(t):
        gl, gl_b = gls[t % 2], gls_b[t % 2]
        k.emit("act", lambda e: e.activation(out=gl[:], in_=hf[:], func=AF.Gelu_apprx_tanh), reads=[hf_b], writes=[gl_b])

    def M(t, cts, store):
        x, xb = xt[t % 2], xt_b[t % 2]
        gl, gl_b = gls[t % 2], gls_b[t % 2]
        for ct in cts:
            pa, pab = C.ps.next()
            pb_, pbb = C.ps.next()

            def mm(e, ct=ct, pa=pa, pb_=pb_):
                ins = None
                for half, pt in ((0, pa), (1, pb_)):
                    c0 = half * D + ct * 128
                    for kt in range(KT):
                        ins = e.matmul(pt[:, :NT], wg[:, kt, c0:c0 + 128], gl[:, kt, :NT], start=(kt == 0), stop=(kt == KT - 1))
                return ins
            k.emit("pe", mm, reads=[wg_cb[ct // 4], wg_cb[2 + ct // 4], gl_b], writes=[pab, pbb])
            s_, s_b = sg[ct % 2], sg_b[ct % 2]
            k.emit("act", lambda e, pb_=pb_, s_=s_: e.activation(out=s_[:, :NT], in_=pb_[:, :NT], func=AF.Sigmoid),
                   reads=[pbb], writes=[s_b])
            k.emit("dve", lambda e, pa=pa, s_=s_: e.tensor_tensor(out=s_[:, :NT], in0=pa[:, :NT], in1=s_[:, :NT], op=ALU.mult),
                   reads=[pab, s_b], writes=[s_b])
            k.emit("pool", lambda e, ct=ct, s_=s_: e.tensor_tensor(out=x[:, ct, :NT], in0=x[:, ct, :NT], in1=s_[:, :NT], op=ALU.add),
                   reads=[s_b, xb], writes=[xb])
        if store:
            k.dma("sp", P["Xo"][:, t * NT:(t + 1) * NT].rearrange("(kt p) n -> p kt n", p=128), x[:], reads=[xb],
                  writes=P["Xou"].rng(t * NT, (t + 1) * NT), kind="store")

    _ldx(k, P, xt, xt_b, 0, NT)
    ldy(0)
    F(0)
    F2(0)
    for t in range(n_t):
        if t + 1 < n_t:
            _ldx(k, P, xt, xt_b, t + 1, NT)
            ldy(t + 1)
        M(t, range(0, 4), False)
        if t + 1 < n_t:
            F(t + 1)
        M(t, range(4, KT), True)
        if t + 1 < n_t:
            F2(t + 1)
    k.end_phase()


def mlp_phase(k, C, P, NT, layer, final=False):
    TOK = P["TOK"]
    k.begin_phase()
    vec, vec_b = P["vec"], P["vec_b"]
    mlp = MLPPhase(k, C, NT)
    mlp.load_weights(P["mlp_w_in"], P["mlp_w_out"], layer)
    xt = [k.sbuf(f"ml_x{i}", [128, KT, NT], F32) for i in range(2)]
    xt_b = [Buf(f"ml_x{i}") for i in range(2)]
    n_t = TOK // NT
    gcol = VCOL["norm_mlp"] + layer * 8

    def load(t):
        k.dma("sp", xt[t % 2][:], P["X"][:, t * NT:(t + 1) * NT].rearrange("(kt p) n -> p kt n", p=128),
              reads=P["Xu"].rng(t * NT, (t + 1) * NT), writes=[xt_b[t % 2]])

    load(0)
    mlp.front(xt[0], xt_b[0], vec, vec_b, gcol, NT)
    for t in range(n_t):
        x, xb = xt[t % 2], xt_b[t % 2]
        mlp.mlp1(NT)
        if t + 1 < n_t:
            load(t + 1)
            mlp.front(xt[(t + 1) % 2], xt_b[(t + 1) % 2], vec, vec_b, gcol, NT)
        mlp.mlp2(x, xb, NT)
        if final:
            rmsnorm_T(k, C, x, xb, vec, vec_b, VCOL["norm_final"], x, xb, NT, mlp.a[:, 0:KT, :], mlp.a_b[0:KT])
        k.dma("sp", P["Xo"][:, t * NT:(t + 1) * NT].rearrange("(kt p) n -> p kt n", p=128), x[:], reads=[xb],
              writes=P["Xou"].rng(t * NT, (t + 1) * NT), kind="store")
    k.end_phase()


WEIGHT_INPUTS = [("mlp_w_in", [4, D, DFF]), ("mlp_w_out", [4, DFF, D]), ("ssm_w_glu", [D, 2 * D]),
                 ("conv_w_pw1", [D, 2 * D]), ("conv_w_pw2", [D, D]), ("gmlp_w_in", [D, 2 * D]),
                 ("gmlp_w_out", [D, D]), ("attn_w_qkv", [D, 4608]), ("attn_w_o", [512, D])]


SMALL_INPUTS = [("gmlp_wsT", [128, 4, 128]), ("gmlp_b_s", [1, 512]), ("gmlp_ln_g", [D]), ("gmlp_ln_b", [D]),
                ("ssm_aT_re", [64, 64]), ("ssm_aT_im", [64, 64]), ("ssm_log_dt", [64]),
                ("ssm_BT_re", [64, 64, 16]), ("ssm_BT_im", [64, 64, 16]), ("ssm_CT_re", [64, 64, 16]), ("ssm_CT_im", [64, 64, 16])]


def pack_small(inp):
    o = {}
    ws = np.asarray(inp["gmlp_w_s"], np.float32).reshape(4, 128, 128)
    o["gmlp_wsT"] = np.ascontiguousarray(ws.transpose(2, 0, 1))
    o["gmlp_b_s"] = np.ascontiguousarray(np.asarray(inp["gmlp_b_s"], np.float32).reshape(1, 512))
    o["gmlp_ln_g"] = np.ascontiguousarray(np.asarray(inp["gmlp_ln_g"], np.float32).reshape(D))
    o["gmlp_ln_b"] = np.ascontiguousarray(np.asarray(inp["gmlp_ln_b"], np.float32).reshape(D))
    f = lambda n, shape: np.asarray(inp[n], np.float32).reshape(shape)
    o["ssm_aT_re"] = np.ascontiguousarray(f("ssm_a_re", (64, 64)).T)
    o["ssm_aT_im"] = np.ascontiguousarray(f("ssm_a_im", (64, 64)).T)
    o["ssm_log_dt"] = np.ascontiguousarray(f("ssm_log_dt", (64,)))
    o["ssm_BT_re"] = np.ascontiguousarray(f("ssm_b_re", (64, 64, 16)).transpose(1, 0, 2))
    o["ssm_BT_im"] = np.ascontiguousarray(f("ssm_b_im", (64, 64, 16)).transpose(1, 0, 2))
    o["ssm_CT_re"] = np.ascontiguousarray(f("ssm_c_re", (64, 16, 64)).transpose(2, 0, 1))
    o["ssm_CT_im"] = np.ascontiguousarray(f("ssm_c_im", (64, 16, 64)).transpose(2, 0, 1))
    return o


def build_program(cfg):
    NSEQ, S = cfg["NSEQ"], cfg["S"]
    TOK = NSEQ * S
    phases = cfg["phases"]
    nc = bass.Bass("TRN2", target_bir_lowering=False)
    P = {"TOK": TOK, "S": S, "NSEQ": NSEQ, "dbg": cfg.get("dbg", ())}

    def din(name, shape):
        P[name] = nc.dram_tensor(name, list(shape), F32, kind="ExternalInput").ap()
        return P[name]

    xT = din("xT", [D, TOK])
    vecs = din("vecs", [128, NV])
    consts = din("consts", [128, NCONST])
    for name, shape in WEIGHT_INPUTS:
        din(name, shape)
    for name, shape in SMALL_INPUTS:
        din(name, shape)
    out = nc.dram_tensor("out", [D, TOK], F32, kind="ExternalOutput").ap()
    xs = nc.dram_tensor("xs", [D, TOK], F32, kind="Internal").ap()
    Xu_in, Xu_s, Xu_out = DU("xin", TOK), DU("xs", TOK), DU("xout", TOK)
    P["QK"] = nc.dram_tensor("qk_s", [2 * QW, TOK], BF16, kind="Internal").ap()
    P["V"] = nc.dram_tensor("v_s", [TOK, QW], BF16, kind="Internal").ap()
    P["M"] = nc.dram_tensor("m_s", [GW, TOK], BF16, kind="Internal").ap()
    P["QKu"], P["Vu"], P["Mu"] = DU("qk", TOK), DU("v", TOK), DU("m", TOK)
    ntl = TOK // SNT
    P["Ud"] = nc.dram_tensor("ud_s", [ntl, 64, 16, SL, SCB], BF16, kind="Internal").ap()
    P["Yd"] = nc.dram_tensor("yd_s", [ntl, 64, 16, SL, SCB], BF16, kind="Internal").ap()
    P["Udu"] = [Buf(f"ud{i}") for i in range(ntl)]
    P["Ydu"] = [Buf(f"yd{i}") for i in range(ntl)]
    P["Wd_T"] = nc.dram_tensor("wd_T", [128, 64, 128], BF16, kind="Internal").ap()
    P["Wd_Bt"] = nc.dram_tensor("wd_Bt", [128, 64, 2, 64], BF16, kind="Internal").ap()
    P["Wd_F"] = nc.dram_tensor("wd_F", [64, 64, 2, 128], BF16, kind="Internal").ap()
    P["Wd_ab"] = nc.dram_tensor("wd_ab", [64, 2, 64], F32, kind="Internal").ap()
    P["Wd_b"] = Buf("wd")
    if "s5dump" in P["dbg"]:
        P["dump"] = nc.dram_tensor("dump", [128, 4096], F32, kind="ExternalOutput").ap()

    with contextlib.ExitStack() as stack:
        k = K(nc, stack)
        C = Common(k, 512)
        vec = k.sbuf("vecs_sb", [128, NV], F32); vec_b = Buf("vecs")
        k.dma("sp", vec[:], vecs, writes=[vec_b])
        P["vec"], P["vec_b"] = vec, vec_b
        for i, ph in enumerate(phases):
            first, last = (i == 0), (i == len(phases) - 1)
            P["X"], P["Xu"] = (xT, Xu_in) if first else (xs, Xu_s)
            P["Xo"], P["Xou"] = (out, Xu_out) if last else (xs, Xu_s)
            name, layer = ph
            if name == "mlp":
                mlp_phase(k, C, P, 512, layer, final=(last and cfg.get("final", False)))
            elif name == "conv":
                conv_phase(k, C, P, 256, layer)
            elif name == "gmlp":
                gmlp_phase(k, C, P, 512, layer)
            elif name == "s5":
                if cfg.get("s5_stop") == "prep":
                    s5_prep(k, C, P)
                else:
                    s5_phase_a(k, C, P, layer, with_prep=True)
                    s5_phase_b(k, C, P)
                    s5_phase_c(k, C, P, layer)
            elif name == "attn":
                attn_phase_a(k, C, P, 512, layer)
                if cfg.get("attn_stop") != "a":
                    attn_phase_b(k, C, P, embed_c=(cfg.get("attn_stop") != "b"))
            else:
                raise ValueError(name)
        k.wait_all("sp", [b.w for b in Xu_out.b])
        k.run()
        if cfg.get("verbose"):
            print("semaphores used:", k.nsem)
    return nc


FULL_PHASES = [("s5", 0), ("mlp", 0), ("conv", 1), ("mlp", 1), ("gmlp", 2), ("mlp", 2), ("attn", 3), ("mlp", 3)]
_NC_CACHE = {}


def kernel(**inputs):
    x = np.asarray(inputs["x"], np.float32)
    B, S, Dm = x.shape
    nseq = B // NCORES
    key = (nseq, S)
    if key not in _NC_CACHE:
        _NC_CACHE[key] = build_program(dict(NSEQ=nseq, S=S, phases=FULL_PHASES, final=True))
    nc = _NC_CACHE[key]
    shared = pack_small(inputs)
    shared["vecs"] = pack_vecs(inputs)
    shared["consts"] = make_consts()
    for name, shape in WEIGHT_INPUTS:
        shared[name] = np.ascontiguousarray(np.asarray(inputs[name], np.float32).reshape(shape))
    in_maps = []
    for c in range(NCORES):
        m = dict(shared)
        xc = x[c * nseq:(c + 1) * nseq].reshape(nseq * S, Dm)
        m["xT"] = np.ascontiguousarray(xc.T)
        in_maps.append(m)
    res = run_bass_kernel_spmd(nc, in_maps, core_ids=list(range(NCORES)))
    out = np.empty((B, S, Dm), np.float32)
    for c in range(NCORES):
        out[c * nseq:(c + 1) * nseq] = np.ascontiguousarray(res.results[c]["out"].T).reshape(nseq, S, Dm)
    return out
```
